# Optimizing a Trainium2 kernel written in Bass

```python
import math
import jax
import jax.numpy as jnp
from jax import lax
import numpy as np

D_MODEL = 2048
BATCH = 16
SEQ = 256
DEPTH = 4
DEC_BATCH = 8
DEC_SEQ = 1024
PAST_LEN = 512

GRID_W = 64
N_MIXERS = 3
N_ATTN = (DEPTH + 2) // 3
N_HYENA = (DEPTH + 1) // 3
N_RWKV = DEPTH // 3

ATTN_HEADS = 16
ATTN_KV_HEADS = 4
ATTN_HEAD_DIM = 128
ATTN_GROUP = ATTN_HEADS // ATTN_KV_HEADS
WINDOW = 128
ATTN_BLOCK = 128
ROPE_BASE = 10000.0
NEG_INF = -1e30

HYENA_EMB = 33
HYENA_FILTER_WIDTH = 64
HYENA_TARGET = 1e-2
HYENA_FAST_DECAY = 0.3
HYENA_SLOW_DECAY = 1.5

RWKV_HEAD_DIM = 64
RWKV_HEADS = D_MODEL // RWKV_HEAD_DIM
RWKV_DECAY_LORA = 96
RWKV_A_LORA = 96
RWKV_GATE_LORA = 256
RWKV_GN_EPS = 64e-5

D_FF = 4 * D_MODEL
NORM_EPS = 1e-6

kernel_name = "hybrid_diffusion_attn_hyena_rwkv7_step"


def _rmsnorm(x, g):
    xf = x.astype(jnp.float32)
    y = xf * lax.rsqrt(jnp.mean(xf * xf, axis=-1, keepdims=True) + NORM_EPS)
    return (y * g.astype(jnp.float32)).astype(x.dtype)


def _ada(cond, w, b):
    m = jax.nn.silu(cond) @ w + b
    return jnp.split(m[:, None, :], 6, axis=-1)


def _modulate(h, shift, scale):
    return h * (1 + scale) + shift


def _mlp(h, w1, w2):
    return jnp.square(jax.nn.relu(h @ w1)) @ w2


def _axial_angles(L):
    rows = L // GRID_W
    t = jnp.arange(rows * GRID_W)
    row = (t // GRID_W).astype(jnp.float32)
    col = (t % GRID_W).astype(jnp.float32)
    half = ATTN_HEAD_DIM // 2
    inv = ROPE_BASE ** (-jnp.arange(0, half, 2, dtype=jnp.float32) / half)
    return row[:, None] * inv, col[:, None] * inv


def _rotate(x, ang):
    x1, x2 = jnp.split(x, 2, axis=-1)
    c, s = jnp.cos(ang), jnp.sin(ang)
    return jnp.concatenate([x1 * c - x2 * s, x1 * s + x2 * c], axis=-1)


def _apply_axial_rope(x, ang_row, ang_col):
    L = x.shape[1]
    shp = (L,) + (1,) * (x.ndim - 3) + (ang_row.shape[-1],)
    xr, xc = jnp.split(x.astype(jnp.float32), 2, axis=-1)
    out = jnp.concatenate([_rotate(xr, ang_row.reshape(shp)), _rotate(xc, ang_col.reshape(shp))], axis=-1)
    return out.astype(x.dtype)


def _attn_qkv(h, wqkv):
    B, L, _ = h.shape
    qkv = h @ wqkv
    qd = ATTN_HEADS * ATTN_HEAD_DIM
    kd = ATTN_KV_HEADS * ATTN_HEAD_DIM
    q = qkv[..., :qd].reshape(B, L, ATTN_KV_HEADS, ATTN_GROUP, ATTN_HEAD_DIM)
    k = qkv[..., qd:qd + kd].reshape(B, L, ATTN_KV_HEADS, ATTN_HEAD_DIM)
    v = qkv[..., qd + kd:].reshape(B, L, ATTN_KV_HEADS, ATTN_HEAD_DIM)
    return q, k, v


def _softmax_with_sink(s, sink):
    sk = jnp.broadcast_to(sink.astype(jnp.float32).reshape(ATTN_KV_HEADS, ATTN_GROUP, 1, 1), s.shape[:-1] + (1,))
    p = jax.nn.softmax(jnp.concatenate([s, sk], axis=-1), axis=-1)
    return p[..., :-1]


def _attn_context(h, wqkv, wo, sink):
    B, L, _ = h.shape
    q, k, v = _attn_qkv(h, wqkv)
    nq = L // ATTN_BLOCK
    qb = jnp.moveaxis(q.reshape(B, nq, ATTN_BLOCK, ATTN_KV_HEADS, ATTN_GROUP, ATTN_HEAD_DIM), 1, 0)
    scale = ATTN_HEAD_DIM ** -0.5

    def block(qblk):
        s = jnp.einsum("bqhgd,bshd->bhgqs", qblk, k).astype(jnp.float32) * scale
        p = _softmax_with_sink(s, sink)
        return jnp.einsum("bhgqs,bshd->bqhgd", p.astype(v.dtype), v)

    o = lax.map(block, qb)
    o = jnp.moveaxis(o, 0, 1).reshape(B, L, ATTN_HEADS * ATTN_HEAD_DIM)
    return o @ wo, k, v


def _attn_latent(h, k_ctx, v_ctx, wqkv, wo, sink):
    B, L, _ = h.shape
    q, k, v = _attn_qkv(h, wqkv)
    ang_r, ang_c = _axial_angles(L)
    q = _apply_axial_rope(q, ang_r, ang_c)
    k = _apply_axial_rope(k, ang_r, ang_c)
    nb = L // ATTN_BLOCK
    nl = 3 * ATTN_BLOCK

    def band(t):
        tp = jnp.pad(t, ((0, 0), (ATTN_BLOCK, ATTN_BLOCK), (0, 0), (0, 0)))
        tp = tp.reshape(B, nb + 2, ATTN_BLOCK, ATTN_KV_HEADS, ATTN_HEAD_DIM)
        tb = jnp.concatenate([tp[:, :-2], tp[:, 1:-1], tp[:, 2:]], axis=2)
        return jnp.moveaxis(tb, 1, 0)

    k_band, v_band = band(k), band(v)
    qb = jnp.moveaxis(q.reshape(B, nb, ATTN_BLOCK, ATTN_KV_HEADS, ATTN_GROUP, ATTN_HEAD_DIM), 1, 0)
    qi = jnp.arange(ATTN_BLOCK)
    ki = jnp.arange(nl)
    rel = ki[None, :] - ATTN_BLOCK - qi[:, None]
    scale = ATTN_HEAD_DIM ** -0.5

    def block(args):
        j, qblk, kblk, vblk = args
        kpos = (j - 1) * ATTN_BLOCK + ki
        valid = (jnp.abs(rel) <= WINDOW) & ((kpos >= 0) & (kpos < L))[None, :]
        s_loc = jnp.einsum("bqhgd,bshd->bhgqs", qblk, kblk).astype(jnp.float32) * scale
        s_loc = jnp.where(valid, s_loc, NEG_INF)
        s_ctx = jnp.einsum("bqhgd,bshd->bhgqs", qblk, k_ctx).astype(jnp.float32) * scale
        p = _softmax_with_sink(jnp.concatenate([s_loc, s_ctx], axis=-1), sink).astype(v.dtype)
        return (jnp.einsum("bhgqs,bshd->bqhgd", p[..., :nl], vblk)
                + jnp.einsum("bhgqs,bshd->bqhgd", p[..., nl:], v_ctx))

    o = lax.map(block, (jnp.arange(nb), qb, k_band, v_band))
    o = jnp.moveaxis(o, 0, 1).reshape(B, L, ATTN_HEADS * ATTN_HEAD_DIM)
    return o @ wo


def _centred_conv3(u, w, b):
    up = jnp.pad(u, ((0, 0), (1, 1), (0, 0)))
    return up[:, :-2] * w[0] + up[:, 1:-1] * w[1] + up[:, 2:] * w[2] + b


def _hyena_filter(L, f_w1, f_b1, f_w2, f_b2, f_w3, f_b3, freq, f_wout):
    f32 = jnp.float32
    t = jnp.linspace(0.0, 1.0, L, dtype=f32)[:, None]
    bands = (HYENA_EMB - 1) // 2
    w = 2 * math.pi * jnp.arange(L, dtype=f32)[:, None] / L
    fr = jnp.linspace(1e-4, bands - 1, bands, dtype=f32)[None, :]
    z = jnp.concatenate([t, jnp.cos(fr * w), -jnp.sin(fr * w)], axis=-1)
    fq = freq.astype(f32)
    hdn = jnp.sin(fq * (z @ f_w1.astype(f32) + f_b1.astype(f32)))
    hdn = jnp.sin(fq * (hdn @ f_w2.astype(f32) + f_b2.astype(f32)))
    hdn = jnp.sin(fq * (hdn @ f_w3.astype(f32) + f_b3.astype(f32)))
    filt = (hdn @ f_wout.astype(f32)).reshape(L, 2, D_MODEL)
    deltas = jnp.linspace(math.log(HYENA_TARGET) / HYENA_SLOW_DECAY,
                          math.log(HYENA_TARGET) / HYENA_FAST_DECAY, D_MODEL, dtype=f32)
    filt = filt * jnp.exp(-t * jnp.abs(deltas))[:, None, :]
    h_fwd, h_bwd = filt[:, 0], filt[:, 1]
    return jnp.concatenate([h_fwd, jnp.zeros((1, D_MODEL), f32), h_bwd[:0:-1]], axis=0)


def _hyena(h, w_in, conv_w, conv_b, f_w1, f_b1, f_w2, f_b2, f_w3, f_b3, freq, f_wout, fbias, w_out):
    B, L, _ = h.shape
    u = _centred_conv3(h @ w_in, conv_w, conv_b)
    x0, x1, v = jnp.split(u, 3, axis=-1)
    z = (v * x1).astype(jnp.float32)
    h_circ = _hyena_filter(L, f_w1, f_b1, f_w2, f_b2, f_w3, f_b3, freq, f_wout)
    zf = jnp.fft.rfft(z, n=2 * L, axis=1)
    hf = jnp.fft.rfft(h_circ, axis=0)
    y = jnp.fft.irfft(zf * hf[None], n=2 * L, axis=1)[:, :L] + z * fbias.astype(jnp.float32)
    return (y.astype(h.dtype) * x0) @ w_out


def _rwkv_step(S, inp):
    r, w, kk, b, k, v = inp
    sa = jnp.einsum("bhvk,bhk->bhv", S, -kk)
    S = S * w[:, :, None, :] + sa[..., None] * b[:, :, None, :] + v[..., None] * k[:, :, None, :]
    return S, jnp.einsum("bhvk,bhk->bhv", S, r)


def _rwkv_scan(S0, r, w, kk, b, k, v, reverse):
    xs = tuple(jnp.moveaxis(t, 1, 0) for t in (r, w, kk, b, k, v))
    S, ys = lax.scan(_rwkv_step, S0.astype(jnp.float32), xs, reverse=reverse)
    return S, jnp.moveaxis(ys, 0, 1)


def _rwkv(h, s0_fwd, s0_bwd, mu, wr, wk, wv, wo, w0, w1, w2, a0, a1, a2, g1, g2, k_k, k_a, r_k, ln_w, ln_b):
    f32 = jnp.float32
    B, L, D = h.shape
    hp = jnp.pad(h, ((0, 0), (1, 1), (0, 0)))
    xx = 0.5 * (hp[:, :-2] + hp[:, 2:]) - h
    xr, xw, xk, xv, xa, xg = [h + xx * mu[i] for i in range(6)]

    def heads(t):
        return t.astype(f32).reshape(B, L, RWKV_HEADS, RWKV_HEAD_DIM)

    r = heads(xr @ wr)
    k = heads(xk @ wk)
    v = heads(xv @ wv)
    g = jax.nn.sigmoid(xg @ g1) @ g2
    kk = k * k_k.astype(f32).reshape(RWKV_HEADS, RWKV_HEAD_DIM)
    kk = kk * lax.rsqrt(jnp.sum(kk * kk, axis=-1, keepdims=True) + 1e-12)
    k_a_h = k_a.astype(f32).reshape(RWKV_HEADS, RWKV_HEAD_DIM)
    r_k_h = r_k.astype(f32)
    ys, bonuses, finals = [], [], []
    for d, (s0, rev) in enumerate(((s0_fwd, False), (s0_bwd, True))):
        wlog = -jax.nn.softplus(-(w0[d] + jnp.tanh(xw @ w1[d]) @ w2[d])) - 0.5
        decay = heads(jnp.exp(-jnp.exp(wlog.astype(f32))))
        a = heads(jax.nn.sigmoid(a0[d] + (xa @ a1[d]) @ a2[d]))
        kd = k * (1 + (a - 1) * k_a_h)
        s_fin, y_d = _rwkv_scan(s0, r, decay, kk, kk * a, kd, v, rev)
        ys.append(y_d)
        bonuses.append(jnp.sum(r * kd * r_k_h, axis=-1, keepdims=True) * v)
        finals.append(s_fin)
    y = ys[0] + ys[1]
    mean = jnp.mean(y, axis=-1, keepdims=True)
    var = jnp.mean(jnp.square(y - mean), axis=-1, keepdims=True)
    yn = ((y - mean) * lax.rsqrt(var + RWKV_GN_EPS)).reshape(B, L, D) * ln_w.astype(f32) + ln_b.astype(f32)
    out = (yn + (bonuses[0] + bonuses[1]).reshape(B, L, D)).astype(h.dtype) * g
    return out @ wo, jnp.stack(finals, axis=1).astype(h.dtype)


def setup_inputs(seed: int = 0) -> dict:
    key = jax.random.key(seed)
    keys = jax.random.split(key, 64)
    counter = iter(range(64))
    f32 = jnp.float32

    def nrm(shape, scale=1.0):
        return scale * jax.random.normal(keys[next(counter)], shape, f32)

    D = D_MODEL
    QD = ATTN_HEADS * ATTN_HEAD_DIM
    QKV = (ATTN_HEADS + 2 * ATTN_KV_HEADS) * ATTN_HEAD_DIM
    HF = HYENA_FILTER_WIDTH
    w0_base = jnp.linspace(-6.0, -0.5, D, dtype=f32)
    return {
        "x_prompt": nrm((BATCH, SEQ, D)),
        "x_sample": nrm((DEC_BATCH, DEC_SEQ, D)),
        "cache_attn_k": nrm((DEC_BATCH, N_ATTN, PAST_LEN, ATTN_KV_HEADS, ATTN_HEAD_DIM)),
        "cache_attn_v": nrm((DEC_BATCH, N_ATTN, PAST_LEN, ATTN_KV_HEADS, ATTN_HEAD_DIM)),
        "state_rwkv": nrm((DEC_BATCH, N_RWKV, 2, RWKV_HEADS, RWKV_HEAD_DIM, RWKV_HEAD_DIM), 0.3),
        "c": nrm((DEC_BATCH, D)),
        "c_ctx": nrm((D,)),
        "ada_w": nrm((DEPTH, D, 6 * D), 0.5 * D ** -0.5),
        "ada_b": nrm((DEPTH, 6 * D), 0.01),
        "norm1_g": 1.0 + nrm((DEPTH, D), 0.05),
        "norm2_g": 1.0 + nrm((DEPTH, D), 0.05),
        "mlp_w1": nrm((DEPTH, D, D_FF), D ** -0.5),
        "mlp_w2": nrm((DEPTH, D_FF, D), D_FF ** -0.5),
        "attn_wqkv": nrm((N_ATTN, D, QKV), D ** -0.5),
        "attn_wo": nrm((N_ATTN, QD, D), QD ** -0.5),
        "attn_sink": nrm((N_ATTN, ATTN_HEADS), 0.5),
        "hy_w_in": nrm((N_HYENA, D, 3 * D), D ** -0.5),
        "hy_conv_w": nrm((N_HYENA, 3, 3 * D), 3 ** -0.5),
        "hy_conv_b": nrm((N_HYENA, 3 * D), 0.01),
        "hy_f_w1": nrm((N_HYENA, HYENA_EMB, HF), 1.0),
        "hy_f_b1": nrm((N_HYENA, HF), 0.1),
        "hy_f_w2": nrm((N_HYENA, HF, HF), HF ** -0.5),
        "hy_f_b2": nrm((N_HYENA, HF), 0.1),
        "hy_f_w3": nrm((N_HYENA, HF, HF), HF ** -0.5),
        "hy_f_b3": nrm((N_HYENA, HF), 0.1),
        "hy_f_freq": 1.0 + nrm((N_HYENA, HF), 0.05),
        "hy_f_wout": nrm((N_HYENA, HF, 2 * D), 0.02 * HF ** -0.5),
        "hy_bias": nrm((N_HYENA, D), 0.1),
        "hy_w_out": nrm((N_HYENA, D, D), D ** -0.5),
        "rw_mu": 0.5 + nrm((N_RWKV, 6, D), 0.1),
        "rw_wr": nrm((N_RWKV, D, D), D ** -0.5),
        "rw_wk": nrm((N_RWKV, D, D), D ** -0.5),
        "rw_wv": nrm((N_RWKV, D, D), D ** -0.5),
        "rw_wo": nrm((N_RWKV, D, D), D ** -0.5),
        "rw_w0": w0_base + nrm((N_RWKV, 2, D), 0.1),
        "rw_w1": nrm((N_RWKV, 2, D, RWKV_DECAY_LORA), D ** -0.5),
        "rw_w2": nrm((N_RWKV, 2, RWKV_DECAY_LORA, D), 0.1 * RWKV_DECAY_LORA ** -0.5),
        "rw_a0": nrm((N_RWKV, 2, D), 0.1),
        "rw_a1": nrm((N_RWKV, 2, D, RWKV_A_LORA), D ** -0.5),
        "rw_a2": nrm((N_RWKV, 2, RWKV_A_LORA, D), 0.5 * RWKV_A_LORA ** -0.5),
        "rw_g1": nrm((N_RWKV, D, RWKV_GATE_LORA), D ** -0.5),
        "rw_g2": nrm((N_RWKV, RWKV_GATE_LORA, D), RWKV_GATE_LORA ** -0.5),
        "rw_k_k": 0.85 + nrm((N_RWKV, D), 0.02),
        "rw_k_a": 1.0 + nrm((N_RWKV, D), 0.02),
        "rw_r_k": nrm((N_RWKV, RWKV_HEADS, RWKV_HEAD_DIM), 0.1),
        "rw_ln_w": 1.0 + nrm((N_RWKV, D), 0.05),
        "rw_ln_b": nrm((N_RWKV, D), 0.01),
        "final_norm_g": 1.0 + nrm((D,), 0.05),
    }


def reference(x_prompt, x_sample, cache_attn_k, cache_attn_v, state_rwkv, c, c_ctx,
              ada_w, ada_b, norm1_g, norm2_g, mlp_w1, mlp_w2,
              attn_wqkv, attn_wo, attn_sink,
              hy_w_in, hy_conv_w, hy_conv_b, hy_f_w1, hy_f_b1, hy_f_w2, hy_f_b2, hy_f_w3, hy_f_b3,
              hy_f_freq, hy_f_wout, hy_bias, hy_w_out,
              rw_mu, rw_wr, rw_wk, rw_wv, rw_wo, rw_w0, rw_w1, rw_w2, rw_a0, rw_a1, rw_a2,
              rw_g1, rw_g2, rw_k_k, rw_k_a, rw_r_k, rw_ln_w, rw_ln_b,
              final_norm_g):
    xp = x_prompt
    xs = x_sample
    new_k, new_v, new_s = [], [], []
    cond_ctx = c_ctx[None, :]
    for i in range(DEPTH):
        kind = i % N_MIXERS
        j = i // N_MIXERS
        sh1p, sc1p, gt1p, sh2p, sc2p, gt2p = _ada(cond_ctx, ada_w[i], ada_b[i])
        sh1s, sc1s, gt1s, sh2s, sc2s, gt2s = _ada(c, ada_w[i], ada_b[i])
        hp = _modulate(_rmsnorm(xp, norm1_g[i]), sh1p, sc1p)
        hs = _modulate(_rmsnorm(xs, norm1_g[i]), sh1s, sc1s)
        if kind == 0:
            op, kc, vc = _attn_context(hp, attn_wqkv[j], attn_wo[j], attn_sink[j])
            os_ = _attn_latent(hs, cache_attn_k[:, j], cache_attn_v[:, j], attn_wqkv[j], attn_wo[j], attn_sink[j])
            new_k.append(kc)
            new_v.append(vc)
        elif kind == 1:
            hy = (hy_w_in[j], hy_conv_w[j], hy_conv_b[j], hy_f_w1[j], hy_f_b1[j], hy_f_w2[j], hy_f_b2[j],
                  hy_f_w3[j], hy_f_b3[j], hy_f_freq[j], hy_f_wout[j], hy_bias[j], hy_w_out[j])
            op = _hyena(hp, *hy)
            os_ = _hyena(hs, *hy)
        else:
            rw = (rw_mu[j], rw_wr[j], rw_wk[j], rw_wv[j], rw_wo[j], rw_w0[j], rw_w1[j], rw_w2[j],
                  rw_a0[j], rw_a1[j], rw_a2[j], rw_g1[j], rw_g2[j], rw_k_k[j], rw_k_a[j], rw_r_k[j],
                  rw_ln_w[j], rw_ln_b[j])
            zeros = jnp.zeros((xp.shape[0], RWKV_HEADS, RWKV_HEAD_DIM, RWKV_HEAD_DIM), jnp.float32)
            op, st = _rwkv(hp, zeros, zeros, *rw)
            os_, _ = _rwkv(hs, state_rwkv[:, j, 0], state_rwkv[:, j, 1], *rw)
            new_s.append(st)
        xp = xp + gt1p * op
        xs = xs + gt1s * os_
        hp = _modulate(_rmsnorm(xp, norm2_g[i]), sh2p, sc2p)
        hs = _modulate(_rmsnorm(xs, norm2_g[i]), sh2s, sc2s)
        xp = xp + gt2p * _mlp(hp, mlp_w1[i], mlp_w2[i])
        xs = xs + gt2s * _mlp(hs, mlp_w1[i], mlp_w2[i])
    y_prompt = _rmsnorm(xp, final_norm_g)
    y_sample = _rmsnorm(xs, final_norm_g)
    new_attn_k = jnp.stack(new_k, axis=1)
    new_attn_v = jnp.stack(new_v, axis=1)
    new_rwkv_state = jnp.stack(new_s, axis=1)
    return (y_prompt, y_sample, new_attn_k, new_attn_v, new_rwkv_state)
```

```python
import math
from contextlib import ExitStack
import numpy as np
import concourse.bass as bass
import concourse.mybir as mybir
from concourse.bass_utils import run_bass_kernel_spmd

F32 = mybir.dt.float32
BF16 = mybir.dt.bfloat16
ALU = mybir.AluOpType
AF = mybir.ActivationFunctionType
AX = mybir.AxisListType

D = 2048
KC = 16
DFF = 8192
DEPTH = 4
NCORES = 8
LS = 1024
LP = 256
PAST = 512
EPS = 1e-6
WSLOT = 8192
NSLOTS = 3

CFG = {"depth": 4, "mixers": (True, True, True), "passes": ("A", "B"), "ncores": 8}


class Buf:
    __slots__ = ("name", "w", "r", "dsem", "dcnt")

    def __init__(self, name=""):
        self.name = name
        self.w = {}
        self.r = {}
        self.dsem = None
        self.dcnt = 0


class Eng:
    EPOCH = 30000

    def __init__(self, sch, name, self_sync):
        self.sch = sch
        self.name = name
        self.ops = []
        self.cnt = 0
        self.sems = []
        self.waited = {}
        self.self_sync = self_sync
        self.pending = False

    def _ticket(self, n):
        e = (n - 1) // self.EPOCH
        while len(self.sems) <= e:
            self.sems.append(self.sch.new_sem(f"{self.name}{len(self.sems)}"))
        return (self.sems[e], (n - 1) % self.EPOCH + 1)

    def _wait_deps(self, r, w):
        deps = {}

        def merge(d):
            for k, (sem, val) in d.items():
                if k not in deps or deps[k][1] < val:
                    deps[k] = (sem, val)

        for b in r:
            merge(b.w)
        for b in w:
            merge(b.w)
            merge(b.r)
        own = {id(x) for x in self.sems}
        for k, (sem, val) in deps.items():
            if k in own and not self.self_sync:
                continue
            if self.waited.get(k, 0) >= val:
                continue
            self.waited[k] = val
            if not self.sch.dry:
                self.ops.append(("w", sem, val))

    def wait_ticket(self, sem, val):
        k = id(sem)
        if self.waited.get(k, 0) >= val:
            return
        self.waited[k] = val
        if not self.sch.dry:
            self.ops.append(("w", sem, val))

    def add(self, fn, r=(), w=(), inc=True):
        self._wait_deps(r, w)
        n = self.cnt + 1
        tk = self._ticket(n)
        key = id(tk[0])
        if inc:
            self.cnt = n
            self.pending = False
        else:
            self.pending = True
        if not self.sch.dry:
            self.ops.append(("i", fn, tk[0] if inc else None, 1))
        for b in w:
            b.w = {key: tk}
            b.r = {}
        for b in r:
            if b in w:
                continue
            if key not in b.r or b.r[key][1] < tk[1]:
                b.r[key] = tk

    def dma(self, fn, dbuf, r=(), w=()):
        self._wait_deps(r, w)
        if dbuf.dsem is None:
            dbuf.dsem = self.sch.new_sem("d" + dbuf.name)
            self.sch.dma_bufs.append(dbuf)
        dbuf.dcnt += 16
        tk = (dbuf.dsem, dbuf.dcnt)
        key = id(dbuf.dsem)
        if not self.sch.dry:
            self.ops.append(("i", fn, dbuf.dsem, 16))
        for b in w:
            b.w = {key: tk}
            b.r = {}
        for b in r:
            if b in w:
                continue
            if key not in b.r or b.r[key][1] < tk[1]:
                b.r[key] = tk


class Sched:
    def __init__(self, nc, es, dry):
        self.nc = nc
        self.es = es
        self.dry = dry
        self.nsem = 0
        self.dma_bufs = []
        self.pe = Eng(self, "pe", False)
        self.act = Eng(self, "act", True)
        self.dve = Eng(self, "dve", True)
        self.pool = Eng(self, "pool", True)
        self.sp = Eng(self, "sp", True)
        self.engs = [self.pe, self.act, self.dve, self.pool, self.sp]
        self.uid = 0

    def new_sem(self, name):
        self.nsem += 1
        if self.dry:
            return object()
        self.uid += 1
        return self.es.enter_context(self.nc.semaphore(f"s{self.uid}_{name}"))

    def barrier(self, include_dma=True):
        assert not self.pe.pending
        for e in self.engs:
            for f in self.engs:
                if f is e or f.cnt == 0:
                    continue
                sem, val = f._ticket(f.cnt)
                e.wait_ticket(sem, val)
            if include_dma:
                for b in self.dma_bufs:
                    if b.name.startswith("wslot"):
                        continue
                    e.wait_ticket(b.dsem, b.dcnt)


def replay(e, ops):
    for op in ops:
        if op[0] == "w":
            e.wait_ge(op[1], op[2])
        else:
            ins = op[1](e)
            if op[2] is not None:
                ins.then_inc(op[2], op[3])


class Prog:
    def __init__(self, nc, es, dry, reqs):
        self.nc = nc
        self.es = es
        self.S = Sched(nc, es, dry)
        self.dry = dry
        self.reqs = reqs if reqs is not None else []
        self.have_reqs = reqs is not None
        self.next_use = 0
        self.next_issue = 0
        self.tn = 0
        self.ps_i = 0
        self.done = set()

    def sb(self, shape, dt, name="t"):
        self.tn += 1
        return self.nc.sbuf_tensor(f"{name}_{self.tn}_{int(self.dry)}", list(shape), dt)

    def sbp(self, shape, dt, name="t"):
        return self.es.enter_context(self.sb(shape, dt, name))

    def psum(self):
        i = self.ps_i
        self.ps_i = (self.ps_i + 1) % 5
        return self.ps_t[i], self.ps_b[i]

    def psf(self, i):
        return self.ps_t[i], self.ps_b[i]

    def wget(self, src_fn, shape):
        a, b = shape
        assert a * b <= WSLOT
        if not self.have_reqs:
            self.reqs.append((src_fn, shape))
            idx = len(self.reqs) - 1
        else:
            idx = self.next_use
            self.next_use += 1
            self._try_issue()
            assert self.next_issue > idx, "weight slot deadlock: too many live slots"
        slot = idx % NSLOTS
        view = self.wslots[slot][:, 0:a * b].rearrange("p (a b) -> p a b", a=a)
        return view, self.wbufs[slot], idx

    def wrel(self, idx):
        if not self.have_reqs:
            return
        self.done.add(idx)
        self._try_issue()

    def _try_issue(self):
        while self.next_issue < len(self.reqs) and self.next_issue < self.next_use + NSLOTS and \
                (self.next_issue < NSLOTS or (self.next_issue - NSLOTS) in self.done):
            self._wissue(self.next_issue)
            self.next_issue += 1

    def _wissue(self, i):
        src_fn, (a, b) = self.reqs[i]
        slot = i % NSLOTS
        view = self.wslots[slot][:, 0:a * b].rearrange("p (a b) -> p a b", a=a)
        src = src_fn(self.dt)
        self.S.pool.dma(lambda e, o=view, s=src: e.dma_start(out=o, in_=s),
                        dbuf=self.wbufs[slot], w=[self.wbufs[slot]])


def bcast(ap, shape):
    return ap.to_broadcast(list(shape))


def build_program(dry, reqs):
    nc = bass.Bass("TRN2", target_bir_lowering=False)
    es = ExitStack()
    P = Prog(nc, es, dry, reqs)
    S = P.S
    pe, act, dve, pool, sp = S.pe, S.act, S.dve, S.pool, S.sp
    depth = CFG["depth"]

    P.dt = {}

    def din(name, shape, dt=F32):
        P.dt[name] = nc.dram_tensor(name, list(shape), dt, kind="ExternalInput").ap()
        return P.dt[name]

    def dout(name, shape, dt=F32):
        return nc.dram_tensor(name, list(shape), dt, kind="ExternalOutput").ap()

    x_p = din("x_p", [2 * LP, D])
    x_s = din("x_s", [LS, D])
    cvec = din("cvec", [128, KC, 2])
    ada_w = din("ada_w", [DEPTH, D, 6 * D])
    ada_b = din("ada_b", [128, DEPTH, 96])
    n1g = din("n1g", [128, DEPTH, KC])
    n2g = din("n2g", [128, DEPTH, KC])
    fng = din("fng", [128, KC])
    mlp_w1 = din("mlp_w1", [DEPTH, D, DFF])
    mlp_w2 = din("mlp_w2", [DEPTH, DFF, D])
    ident_d = din("ident", [128, 128])
    din("attn_wqkv", [2, D, 3072])
    din("attn_wo", [2, D, D])
    sink_d = din("sinkb", [128, 2, 16])
    ck_d = din("ck", [2, PAST, 512])
    cvv_d = din("cvv", [2, PAST, 512])
    ropeC_d = din("ropeC", [128, LS])
    ropeS_d = din("ropeS", [128, LS])
    perm_d = din("permS", [128, 128])
    mask_d = din("masks", [128, 2, 512])
    din("hy_w_in", [1, D, 3 * D])
    din("hy_w_out", [1, D, D])
    hycw_d = din("hy_cw", [128, 3, 48])
    hycb_d = din("hy_cb", [128, 48])
    hyfb_d = din("hy_fb", [128, KC])
    hyw1_d = din("hy_f_w1", [33, 64])
    hyw2_d = din("hy_f_w2", [64, 64])
    hyw3_d = din("hy_f_w3", [64, 64])
    hybf_d = din("hy_bf", [64, 4])
    hywo_d = din("hy_f_wout", [64, 2, D])
    absd_d = din("absd", [128, D])
    for nm in ("rw_wr", "rw_wk", "rw_wv", "rw_wo_perm"):
        din(nm, [D, D])
    din("rw_w1", [2, D, 96])
    din("rw_a1", [2, D, 96])
    din("rw_g1", [D, 256])
    rw_w2_d = din("rw_w2", [2, 96, D])
    rw_a2_d = din("rw_a2", [2, 96, D])
    rw_g2_d = din("rw_g2", [256, D])
    rwmu_d = din("rw_mu", [128, 6, KC])
    rww0_d = din("rw_w0", [128, 2, KC])
    rwa0_d = din("rw_a0", [128, 2, KC])
    rwvec_d = din("rw_vec", [128, 3, KC])
    rwln_d = din("rw_lnp", [128, 2, KC])
    rwln2_d = din("rw_ln2", [128, 2, KC])
    din("rw_wo", [D, D])
    st_d = din("st", [2, 32, 64, 64])
    bones_d = din("bones", [128, 128])
    ohm_d = din("ohm", [128, 64 + 128 + 254])

    def scr(name, shape, dt=F32):
        return nc.dram_tensor("scr_" + name, list(shape), dt).ap()
    SC = {k: scr(k, [KC, 128, LS]) for k in ("r", "k", "v", "a", "w0", "w1", "b0", "b1", "kd0", "kd1", "bon", "g")}
    SC["vtm"] = scr("vtm", [LS, D], BF16)
    for k_ in ("lw0", "lw1", "yf0", "yf1"):
        SC[k_] = scr(k_, [KC, 128, LS])
    rwmask_d = din("rwmask", [128, 2, 128])
    hmask_d = din("hmask", [128, 3, 128])
    SC["y0"] = scr("y0", [LS // 64, 128, 1024])
    SC["y1"] = scr("y1", [LS // 64, 128, 1024])
    SCB = {k: Buf("scr_" + k) for k in SC}
    HC = {}
    for LL in (LS, LP):
        HC[LL] = dict(zemb=din(f"zemb{LL}", [33, LL]), tneg=din(f"tneg{LL}", [128, LL // 128]),
                      m0=din(f"m0_{LL}", [128, LL // 128, 2]), phi=din(f"phi{LL}", [128, LL // 128, 2]))
        din(f"dftC{LL}", [LL, LL])
        din(f"dftS{LL}", [LL, LL])

    y_p = dout("y_p", [2 * LP, D])
    y_s = dout("y_s", [LS, D])
    nk = dout("nk", [2, 2, LP, 512])
    nv = dout("nv", [2, 2, LP, 512])
    nst = dout("nst", [2, 2, 32, 64, 64])

    es.__enter__()
    dumpB = Buf("dump")

    def dump(name, ap, shape, dtype=F32):
        if name not in CFG.get("dump", ()):
            return
        S.barrier()
        o = dout("dbg_" + name, shape, dtype)
        sp.dma(lambda e: e.dma_start(out=o, in_=ap), dbuf=dumpB, w=[dumpB])
        S.barrier()
    xB = Buf("x")
    P.wslots = [P.sbp([128, WSLOT], BF16, f"wslot{i}") for i in range(NSLOTS)]
    P.wbufs = [Buf(f"wslot{i}") for i in range(NSLOTS)]
    ident = P.sbp([128, 128], F32, "ident")
    identb = P.sbp([128, 128], BF16, "identb")
    onesb = P.sbp([128, 128], BF16, "onesb")
    mods = P.sbp([128, DEPTH, 96, 2], F32, "mods")
    n1gt = P.sbp([128, DEPTH, KC], F32, "n1g")
    n2gt = P.sbp([128, DEPTH, KC], F32, "n2g")
    fngt = P.sbp([128, KC], F32, "fng")
    cv_t = P.sbp([128, KC, 2], F32, "cv")
    scond = P.sbp([128, KC, 2], BF16, "scond")
    normc = P.sbp([128, 6, KC], F32, "normc")
    permb = P.sbp([128, 128], BF16, "permb")
    maskb = P.sbp([128, 2, 512], BF16, "maskb")
    sinke = P.sbp([128, 2, 16], F32, "sinke")
    constB = Buf("const")
    modsB = Buf("mods")
    normcB = Buf("normc")
    NTmax = LS // 512
    xBt = [Buf(f"x{t}") for t in range(NTmax)]
    hBt = [Buf(f"h{t}") for t in range(NTmax)]

    P.ps_pair = [P.es.enter_context(nc.psum_tensor(f"pp{i}_{int(dry)}", [128, 1024], F32)) for i in range(4)]
    P.ps_t = [P.ps_pair[i // 2][:, (i % 2) * 512:(i % 2 + 1) * 512] for i in range(8)]
    P.ps_b = [Buf(f"ps{i}") for i in range(8)]

    sp.dma(lambda e: e.dma_start(out=ident[:, :], in_=ident_d[:, :]), dbuf=constB, w=[constB])
    sp.dma(lambda e: e.dma_start(out=n1gt[:, :, :], in_=n1g[:, :, :]), dbuf=constB, w=[constB])
    sp.dma(lambda e: e.dma_start(out=n2gt[:, :, :], in_=n2g[:, :, :]), dbuf=constB, w=[constB])
    sp.dma(lambda e: e.dma_start(out=fngt[:, :], in_=fng[:, :]), dbuf=constB, w=[constB])
    sp.dma(lambda e: e.dma_start(out=cv_t[:, :, :], in_=cvec[:, :, :]), dbuf=constB, w=[constB])
    c2 = Buf("c2")
    with P.sb([128, 128], F32, "permf") as permf, P.sb([128, 2, 512], F32, "maskf") as maskf:
        tmpB = Buf("tmpc")
        sp.dma(lambda e: e.dma_start(out=permf[:, :], in_=perm_d[:, :]), dbuf=tmpB, w=[tmpB])
        sp.dma(lambda e: e.dma_start(out=maskf[:, :, :], in_=mask_d[:, :, :]), dbuf=tmpB, w=[tmpB])
        sp.dma(lambda e: e.dma_start(out=sinke[:, :, :], in_=sink_d[:, :, :]), dbuf=tmpB, w=[tmpB])
        dve.add(lambda e: e.tensor_copy(out=permb[:, :], in_=permf[:, :]), r=[tmpB], w=[c2])
        dve.add(lambda e: e.tensor_copy(out=maskb[:, :, :], in_=maskf[:, :, :]), r=[tmpB], w=[c2])
        act.add(lambda e: e.activation(out=sinke[:, :, :], in_=sinke[:, :, :], func=AF.Exp), r=[tmpB], w=[tmpB])
        S.barrier()
    dve.add(lambda e: e.tensor_copy(out=identb[:, :], in_=ident[:, :]), r=[constB], w=[c2])
    dve.add(lambda e: e.memset(onesb[:, :], 1.0), w=[c2])
    act.add(lambda e: e.activation(out=scond[:, :, :], in_=cv_t[:, :, :], func=AF.Silu), r=[constB], w=[c2])

    adab_cm = P.sb([128, DEPTH, 96], F32, "adab")
    adab = adab_cm.__enter__()
    sp.dma(lambda e: e.dma_start(out=adab[:, :, :], in_=ada_b[:, :, :]), dbuf=constB, w=[constB])
    for l in range(depth):
        for g in range(24):
            wv, wb, wi = P.wget(lambda dt, l=l, g=g: dt["ada_w"][l, :, g * 512:(g + 1) * 512].rearrange("(kc p) n -> p kc n", p=128),
                            (KC, 512))
            ps, psb = P.psum()
            for nn in range(4):
                for kc in range(KC):
                    pe.add(lambda e, ps=ps, wv=wv, nn=nn, kc=kc: e.matmul(
                        ps[:, nn * 2:(nn + 1) * 2], wv[:, kc, nn * 128:(nn + 1) * 128], scond[:, kc, :],
                        start=(kc == 0), stop=(kc == KC - 1)),
                        r=[wb, c2], w=[psb], inc=(kc == KC - 1 and nn == 3))
            dve.add(lambda e, ps=ps, l=l, g=g: e.tensor_tensor(
                out=mods[:, l, g * 4:(g + 1) * 4, :],
                in0=ps[:, 0:8].rearrange("p (a b) -> p a b", a=4),
                in1=bcast(adab[:, l, g * 4:(g + 1) * 4].unsqueeze(2), [128, 4, 2]), op=ALU.add),
                r=[psb, constB], w=[modsB])
            P.wrel(wi)

    S.barrier()
    adab_cm.__exit__(None, None, None)
    dump("mods", mods[:, 0:depth, :, :], [128, depth, 96, 2])
    def run_pass(which, pes):
        isA = which == "A"
        T = LS if isA else 2 * LP
        NT = T // 512
        wsel = 1 if isA else 0
        xin = x_s if isA else x_p
        yout = y_s if isA else y_p
        xt = pes.enter_context(P.sb([128, KC, T], F32, "x"))
        ht = pes.enter_context(P.sb([128, KC, T], BF16, "h"))

        def xs(c0, c1, t0, t1):
            return xt[:, c0:c1, t0:t1]

        with P.sb([128, 2, D], F32, "stage") as stage:
            stB = [Buf(f"stage{which}0"), Buf(f"stage{which}1")]
            for tb in range(T // 128):
                sl = tb % 2
                sp.dma(lambda e, sl=sl, tb=tb: e.dma_start(out=stage[:, sl, :], in_=xin[tb * 128:(tb + 1) * 128, :]),
                       dbuf=stB[sl], w=[stB[sl]])
                for cg in range(4):
                    ps, psb = P.psum()
                    for j in range(4):
                        c = cg * 4 + j
                        pe.add(lambda e, ps=ps, j=j, c=c, sl=sl: e.transpose(
                            ps[:, j * 128:(j + 1) * 128], stage[:, sl, c * 128:(c + 1) * 128], ident[:, :]),
                            r=[stB[sl], constB], w=[psb], inc=(j == 3))
                    eng = act if cg % 2 == 0 else dve
                    if eng is act:
                        act.add(lambda e, ps=ps, cg=cg, tb=tb: e.copy(
                            out=xt[:, cg * 4:(cg + 1) * 4, tb * 128:(tb + 1) * 128],
                            in_=ps[:, :].rearrange("p (a b) -> p a b", a=4)),
                            r=[psb], w=[xBt[tb // 4]])
                    else:
                        dve.add(lambda e, ps=ps, cg=cg, tb=tb: e.tensor_copy(
                            out=xt[:, cg * 4:(cg + 1) * 4, tb * 128:(tb + 1) * 128],
                            in_=ps[:, :].rearrange("p (a b) -> p a b", a=4)),
                            r=[psb], w=[xBt[tb // 4]])
            S.barrier()

        def prep_norm(l):
            def md(j):
                return mods[:, l, j * 16:(j + 1) * 16, wsel]
            for half, gt in ((0, n1gt), (1, n2gt)):
                dve.add(lambda e, half=half, gt=gt: e.scalar_tensor_tensor(
                    out=normc[:, 3 * half + 0, :], in0=md(3 * half + 1), scalar=1.0, in1=gt[:, l, :],
                    op0=ALU.add, op1=ALU.mult), r=[modsB, constB], w=[normcB])
                dve.add(lambda e, half=half: e.tensor_copy(out=normc[:, 3 * half + 1, :], in_=md(3 * half + 0)),
                        r=[modsB], w=[normcB])
                dve.add(lambda e, half=half: e.tensor_copy(out=normc[:, 3 * half + 2, :], in_=md(3 * half + 2)),
                        r=[modsB], w=[normcB])

        def rms_rstd(tt, scr, scrB):
            sq, rstd = scr
            ps, psb = P.psum()
            for c in range(KC):
                k = c % 4
                act.add(lambda e, c=c, k=k, tt=tt: e.activation(
                    out=sq[:, k, :], in_=xt[:, c, tt * 512:(tt + 1) * 512], func=AF.Square),
                    r=[xBt[tt]], w=[scrB[k]])
                pe.add(lambda e, ps=ps, c=c, k=k: e.matmul(ps[:, :], onesb[:, :], sq[:, k, :],
                                                           start=(c == 0), stop=(c == KC - 1)),
                       r=[scrB[k], c2], w=[psb], inc=True)
            dve.add(lambda e, ps=ps: e.tensor_scalar(out=rstd[:, :], in0=ps[:, :], scalar1=1.0 / D, scalar2=EPS,
                                                     op0=ALU.mult, op1=ALU.add), r=[psb], w=[scrB[4]])
            act.add(lambda e: e.activation(out=rstd[:, :], in_=rstd[:, :], func=AF.Sqrt), r=[scrB[4]], w=[scrB[4]])
            dve.add(lambda e: e.reciprocal(out=rstd[:, :], in_=rstd[:, :]), r=[scrB[4]], w=[scrB[4]])
            return rstd

        def norm_mod(half, scr, scrB):
            sq, rstd, tmp = scr
            for tt in range(NT):
                rms_rstd(tt, (sq, rstd), scrB)
                for c in range(KC):
                    k = c % 2
                    dve.add(lambda e, c=c, k=k, tt=tt: e.scalar_tensor_tensor(
                        out=tmp[:, k, :], in0=xt[:, c, tt * 512:(tt + 1) * 512], scalar=normc[:, 3 * half, c:c + 1],
                        in1=rstd[:, :], op0=ALU.mult, op1=ALU.mult),
                        r=[xBt[tt], normcB, scrB[4]], w=[scrB[5 + k]])
                    act.add(lambda e, c=c, k=k, tt=tt: e.activation(
                        out=ht[:, c, tt * 512:(tt + 1) * 512], in_=tmp[:, k, :], func=AF.Identity,
                        bias=normc[:, 3 * half + 1, c:c + 1], scale=1.0),
                        r=[scrB[5 + k], normcB], w=[hBt[tt]])

        def mlp(l):
            with P.sb([128, 2, 4, T], BF16, "hid") as hid, P.sb([128, 2, 512], F32, "sqt") as sqt:
                hidB = [[Buf(f"hid{a}{t}") for t in range(NT)] for a in range(2)]
                sqB = [Buf("sqt0"), Buf("sqt1")]
                si = 0
                for g in range(DFF // 512):
                    hb = g % 2
                    w1v, w1b, w1i = P.wget(lambda dt, g=g: dt["mlp_w1"][l, :, g * 512:(g + 1) * 512].rearrange(
                        "(kc p) n -> p kc n", p=128), (KC, 512))
                    w2v, w2b, w2i = P.wget(lambda dt, g=g: dt["mlp_w2"][l, g * 512:(g + 1) * 512, :].rearrange(
                        "(kc p) n -> p kc n", p=128), (4, D))
                    for nn in range(4):
                        for tt in range(NT):
                            ps, psb = P.psum()
                            for kc in range(KC):
                                pe.add(lambda e, ps=ps, nn=nn, kc=kc, tt=tt, w1v=w1v: e.matmul(
                                    ps[:, :], w1v[:, kc, nn * 128:(nn + 1) * 128], ht[:, kc, tt * 512:(tt + 1) * 512],
                                    start=(kc == 0), stop=(kc == KC - 1)),
                                    r=[w1b, hBt[tt]], w=[psb], inc=(kc == KC - 1))
                            k = si % 2
                            si += 1
                            act.add(lambda e, ps=ps, k=k: e.activation(out=sqt[:, k, :], in_=ps[:, :], func=AF.Square),
                                    r=[psb], w=[sqB[k]])
                            dve.add(lambda e, ps=ps, k=k, hb=hb, nn=nn, tt=tt: e.scalar_tensor_tensor(
                                out=hid[:, hb, nn, tt * 512:(tt + 1) * 512], in0=ps[:, :], scalar=0.0, in1=sqt[:, k, :],
                                op0=ALU.is_gt, op1=ALU.mult), r=[psb, sqB[k]], w=[hidB[hb][tt]])
                    P.wrel(w1i)
                    for oc in range(KC):
                        for tt in range(NT):
                            ps, psb = P.psum()
                            for kc in range(4):
                                pe.add(lambda e, ps=ps, oc=oc, kc=kc, tt=tt, w2v=w2v, hb=hb: e.matmul(
                                    ps[:, :], w2v[:, kc, oc * 128:(oc + 1) * 128], hid[:, hb, kc, tt * 512:(tt + 1) * 512],
                                    start=(kc == 0), stop=(kc == 3)),
                                    r=[w2b, hidB[hb][tt]], w=[psb], inc=(kc == 3))
                            dve.add(lambda e, ps=ps, oc=oc, tt=tt: e.scalar_tensor_tensor(
                                out=xt[:, oc, tt * 512:(tt + 1) * 512], in0=ps[:, :], scalar=normc[:, 5, oc:oc + 1],
                                in1=xt[:, oc, tt * 512:(tt + 1) * 512], op0=ALU.mult, op1=ALU.add),
                                r=[psb, normcB, xBt[tt]], w=[xBt[tt]])
                    P.wrel(w2i)
                S.barrier()


        def attn(l):
            ja = l // 3
            NB = T // 128
            SCALE = 128.0 ** -0.5
            ctxs = []
            ctxs.append(P.sb([128, T], BF16, "kT")); ctxs.append(P.sb([128, NB, 128], BF16, "vtm"))
            ctxs.append(P.sb([128, 4, T], BF16, "qg")); ctxs.append(P.sb([128, 4, T], BF16, "og"))
            ctxs.append(P.sb([128, 2, 512], BF16, "ebuf")); ctxs.append(P.sb([128, 4, 128], F32, "den"))
            ctxs.append(P.sb([128, 512], BF16, "rawb")); ctxs.append(P.sb([128, 2, 512], F32, "rtmp"))
            ctxs.append(P.sb([128, 4, 512] if not isA else [128, 1, 4], F32, "kst"))
            ctxs.append(P.sb([128, 4, 512] if not isA else [128, 1, 4], F32, "vst"))
            if isA:
                ctxs.append(P.sb([128, LS], F32, "ropeC")); ctxs.append(P.sb([128, LS], F32, "ropeS"))
                ctxs.append(P.sb([128, 4, 128], F32, "ckst")); ctxs.append(P.sb([128, PAST], BF16, "ctxk"))
                ctxs.append(P.sb([128, 4, 128], BF16, "ctxv"))
            with ExitStack() as les:
                tl = [les.enter_context(c) for c in ctxs]
                kT, vtm, qg, og, ebuf, den, rawb, rtmp, kst, vst = tl[:10]
                kTB, vtmB, qgB, ogB, denB, rawB = Buf("kT"), Buf("vtm"), Buf("qg"), Buf("og"), Buf("den"), Buf("rawb")
                eB = [Buf("e0"), Buf("e1")]
                rtB = [Buf("rt0"), Buf("rt1")]
                kstB, vstB = Buf(f"kst{which}{l}"), Buf(f"vst{which}{l}")
                if isA:
                    ropeC, ropeS, ckst, ctxk, ctxv = tl[10:]
                    ropeB, ckstB, ctxkB, ctxvB = Buf(f"rope{l}"), Buf(f"ckst{l}"), Buf("ctxk"), Buf(f"ctxv{l}")
                    sp.dma(lambda e: e.dma_start(out=ropeC[:, :], in_=ropeC_d[:, :]), dbuf=ropeB, w=[ropeB])
                    sp.dma(lambda e: e.dma_start(out=ropeS[:, :], in_=ropeS_d[:, :]), dbuf=ropeB, w=[ropeB])

                def evac_rope(ps, psb, out_ap, outB, tt):
                    if not isA:
                        act.add(lambda e: e.copy(out=out_ap, in_=ps[:, :]), r=[psb], w=[outB])
                        return
                    act.add(lambda e: e.copy(out=rawb[:, :], in_=ps[:, :]), r=[psb], w=[rawB])
                    pp, ppb = P.psum()
                    pe.add(lambda e: e.matmul(pp[:, :], permb[:, :], rawb[:, :], start=True, stop=True),
                           r=[c2, rawB], w=[ppb])
                    dve.add(lambda e: e.tensor_tensor(out=rtmp[:, 0, :], in0=pp[:, :], in1=ropeS[:, tt * 512:(tt + 1) * 512],
                                                      op=ALU.mult), r=[ppb, ropeB], w=[rtB[0]])
                    dve.add(lambda e: e.tensor_tensor(out=rtmp[:, 1, :], in0=ps[:, :], in1=ropeC[:, tt * 512:(tt + 1) * 512],
                                                      op=ALU.mult), r=[psb, ropeB, rawB], w=[rtB[1]])
                    dve.add(lambda e: e.tensor_tensor(out=out_ap, in0=rtmp[:, 0, :], in1=rtmp[:, 1, :], op=ALU.add),
                            r=[rtB[0], rtB[1]], w=[outB])

                for g in range(4):
                    wk, wkb, wki = P.wget(lambda dt, g=g: dt["attn_wqkv"][ja, :, 2048 + g * 128:2048 + (g + 1) * 128].rearrange(
                        "(kc p) n -> p kc n", p=128), (KC, 128))
                    wv, wvb, wvi = P.wget(lambda dt, g=g: dt["attn_wqkv"][ja, :, 2560 + g * 128:2560 + (g + 1) * 128].rearrange(
                        "(kc p) n -> p kc n", p=128), (KC, 128))
                    for tt in range(NT):
                        ps, psb = P.psum()
                        for kc in range(KC):
                            pe.add(lambda e, ps=ps, kc=kc, tt=tt, wk=wk: e.matmul(
                                ps[:, :], wk[:, kc, :], ht[:, kc, tt * 512:(tt + 1) * 512], start=(kc == 0), stop=(kc == KC - 1)),
                                r=[wkb, hBt[tt]], w=[psb], inc=(kc == KC - 1))
                        evac_rope(ps, psb, kT[:, tt * 512:(tt + 1) * 512], kTB, tt)
                    for tq in range(NB // 4):
                        for src_w, src_b, dst, dstB, stg, stgB, dram in ((wk, wkb, None, None, kst, kstB, nk),
                                                                       (wv, wvb, vtm, vtmB, vst, vstB, nv)):
                            if dst is None and isA:
                                continue
                            ps, psb = P.psum()
                            for j in range(4):
                                tb = tq * 4 + j
                                for kc in range(KC):
                                    pe.add(lambda e, ps=ps, kc=kc, tb=tb, j=j, src_w=src_w: e.matmul(
                                        ps[:, j * 128:(j + 1) * 128], ht[:, kc, tb * 128:(tb + 1) * 128], src_w[:, kc, :],
                                        start=(kc == 0), stop=(kc == KC - 1)),
                                        r=[src_b, hBt[tb // 4]], w=[psb], inc=(kc == KC - 1 and j == 3))
                            if not isA:
                                dve.add(lambda e, ps=ps, stg=stg, g=g: e.tensor_copy(
                                    out=stg[:, :, g * 128:(g + 1) * 128], in_=ps[:, :].rearrange("p (a b) -> p a b", a=4)),
                                    r=[psb], w=[stgB])
                                if dst is not None:
                                    act.add(lambda e, stg=stg, tq=tq, dst=dst, g=g: e.copy(
                                        out=dst[:, tq * 4:(tq + 1) * 4, :], in_=stg[:, :, g * 128:(g + 1) * 128]),
                                        r=[stgB], w=[dstB])
                            elif dst is not None:
                                act.add(lambda e, ps=ps, tq=tq, dst=dst: e.copy(
                                    out=dst[:, tq * 4:(tq + 1) * 4, :], in_=ps[:, :].rearrange("p (a b) -> p a b", a=4)),
                                    r=[psb], w=[dstB])
                    P.wrel(wki)
                    P.wrel(wvi)
                    if CFG.get("attn_stop", 9) <= 1:
                        continue
                    wq, wqb, wqi = P.wget(lambda dt, g=g: dt["attn_wqkv"][ja, :, g * 512:(g + 1) * 512].rearrange(
                        "(kc p) n -> p kc n", p=128), (KC, 512))
                    for nn in range(4):
                        for tt in range(NT):
                            ps, psb = P.psum()
                            for kc in range(KC):
                                pe.add(lambda e, ps=ps, kc=kc, tt=tt, nn=nn, wq=wq: e.matmul(
                                    ps[:, :], wq[:, kc, nn * 128:(nn + 1) * 128], ht[:, kc, tt * 512:(tt + 1) * 512],
                                    start=(kc == 0), stop=(kc == KC - 1)),
                                    r=[wqb, hBt[tt]], w=[psb], inc=(kc == KC - 1))
                            evac_rope(ps, psb, qg[:, nn, tt * 512:(tt + 1) * 512], qgB, tt)
                    P.wrel(wqi)
                    if CFG.get("attn_stop", 9) <= 2:
                        continue
                    if isA:
                        sp.dma(lambda e, g=g: e.dma_start(out=ckst[:, :, :], in_=ck_d[ja, :, g * 128:(g + 1) * 128].rearrange(
                            "(sb p) d -> p sb d", p=128)), dbuf=ckstB, w=[ckstB])
                        ps, psb = P.psum()
                        for sbk in range(4):
                            pe.add(lambda e, ps=ps, sbk=sbk: e.transpose(ps[:, sbk * 128:(sbk + 1) * 128], ckst[:, sbk, :], ident[:, :]),
                                   r=[ckstB, constB], w=[psb], inc=(sbk == 3))
                        act.add(lambda e, ps=ps: e.copy(out=ctxk[:, :], in_=ps[:, :]), r=[psb], w=[ctxkB])
                        pool.dma(lambda e, g=g: e.dma_start(out=ctxv[:, :, :], in_=cvv_d[ja, :, g * 128:(g + 1) * 128].rearrange(
                            "(sb p) d -> p sb d", p=128)), dbuf=ctxvB, w=[ctxvB])
                    qblocks = []
                    if isA:
                        for jq in range(NB):
                            kb = []
                            for jj in (jq - 1, jq, jq + 1):
                                if 0 <= jj < NB:
                                    m = None if jj == jq else (0 if jj < jq else 1)
                                    kb.append((kT[:, jj * 128:(jj + 1) * 128], kTB, vtm[:, jj, :], vtmB, m))
                            for sbk in range(4):
                                kb.append((ctxk[:, sbk * 128:(sbk + 1) * 128], ctxkB, ctxv[:, sbk, :], ctxvB, None))
                            qblocks.append((jq, kb))
                    else:
                        for sq_ in range(2):
                            for qb in range(2):
                                kb = [(kT[:, jj * 128:(jj + 1) * 128], kTB, vtm[:, jj, :], vtmB, None)
                                      for jj in (sq_ * 2, sq_ * 2 + 1)]
                                qblocks.append((sq_ * 2 + qb, kb))
                    ei = 0
                    for tqb, kb in qblocks:
                        q0 = tqb * 128
                        ops_, opb = P.psf(5)
                        dps, dpb = P.psf(6)
                        for i, (kap, kB_, vap, vB_, m) in enumerate(kb):
                            first, last = (i == 0), (i == len(kb) - 1)
                            sps, spb = P.psum()
                            pe.add(lambda e, sps=sps, kap=kap, q0=q0: e.matmul(sps[:, :], kap, qg[:, :, q0:q0 + 128],
                                                                             start=True, stop=True),
                                   r=[kB_, qgB], w=[spb])
                            k = ei % 2
                            ei += 1
                            act.add(lambda e, sps=sps, k=k: e.activation(out=ebuf[:, k, :], in_=sps[:, :], func=AF.Exp, scale=SCALE),
                                    r=[spb], w=[eB[k]])
                            if m is not None:
                                dve.add(lambda e, k=k, m=m: e.tensor_tensor(out=ebuf[:, k, :], in0=ebuf[:, k, :], in1=maskb[:, m, :],
                                                                          op=ALU.mult), r=[eB[k], c2], w=[eB[k]])
                            pe.add(lambda e, ops_=ops_, vap=vap, k=k, first=first, last=last: e.matmul(
                                ops_[:, :], vap, ebuf[:, k, :], start=first, stop=last), r=[vB_, eB[k]], w=[opb], inc=False)
                            pe.add(lambda e, dps=dps, k=k, first=first, last=last: e.matmul(
                                dps[:, :], onesb[:, :], ebuf[:, k, :], start=first, stop=last), r=[c2, eB[k]], w=[dpb], inc=True)
                        dve.add(lambda e, dps=dps, g=g: e.tensor_tensor(
                            out=den[:, :, :], in0=dps[:, :].rearrange("p (a b) -> p a b", a=4),
                            in1=bcast(sinke[:, ja, g * 4:(g + 1) * 4].unsqueeze(2), [128, 4, 128]), op=ALU.add),
                            r=[dpb, tmpB], w=[denB])
                        dve.add(lambda e: e.reciprocal(out=den[:, :, :], in_=den[:, :, :]), r=[denB], w=[denB])
                        dve.add(lambda e, ops_=ops_, q0=q0: e.tensor_tensor(
                            out=og[:, :, q0:q0 + 128], in0=ops_[:, :].rearrange("p (a b) -> p a b", a=4), in1=den[:, :, :],
                            op=ALU.mult), r=[opb, denB], w=[ogB])
                    if CFG.get("attn_stop", 9) <= 3:
                        continue
                    wo, wob, woi = P.wget(lambda dt, g=g: dt["attn_wo"][ja, g * 512:(g + 1) * 512, :].rearrange(
                        "(kc p) n -> p kc n", p=128), (4, D))
                    for oc in range(KC):
                        for tt in range(NT):
                            ps, psb = P.psum()
                            for kc in range(4):
                                pe.add(lambda e, ps=ps, oc=oc, kc=kc, tt=tt, wo=wo: e.matmul(
                                    ps[:, :], wo[:, kc, oc * 128:(oc + 1) * 128], og[:, kc, tt * 512:(tt + 1) * 512],
                                    start=(kc == 0), stop=(kc == 3)), r=[wob, ogB], w=[psb], inc=(kc == 3))
                            dve.add(lambda e, ps=ps, oc=oc, tt=tt: e.scalar_tensor_tensor(
                                out=xt[:, oc, tt * 512:(tt + 1) * 512], in0=ps[:, :], scalar=normc[:, 2, oc:oc + 1],
                                in1=xt[:, oc, tt * 512:(tt + 1) * 512], op0=ALU.mult, op1=ALU.add),
                                r=[psb, normcB, xBt[tt]], w=[xBt[tt]])
                    P.wrel(woi)
                if not isA:
                    for stg, stgB, dram in ((kst, kstB, nk), (vst, vstB, nv)):
                        for tb in range(4):
                            sq_, t0 = tb // 2, (tb % 2) * 128
                            sp.dma(lambda e, stg=stg, tb=tb, sq_=sq_, t0=t0, dram=dram: e.dma_start(
                                out=dram[sq_, ja, t0:t0 + 128, :], in_=stg[:, tb, :]), dbuf=stgB, r=[stgB])
                S.barrier()

        MIXERS[0] = attn


        def hyena(l):
            L = LS if isA else LP
            nseq = T // L
            NBL = L // 128
            NBT = T // 128
            hc = HC[L]
            PI = math.pi
            with ExitStack() as les:
                def A(shape, dt, name):
                    return les.enter_context(P.sb(shape, dt, name))
                hd = A([128, 2, L], F32, "hd")
                usb = A([128, nseq, L + 2], F32, "usb")
                x1c = A([128, nseq, L], F32, "x1c")
                z32 = A([128, nseq, L], F32, "z32")
                zb = A([128, T], BF16, "zb")
                x0c = A([128, T], BF16, "x0c")
                ztm = A([128, NBT, 128], BF16, "ztm")
                he = A([128, NBL, 128], BF16, "he")
                ho = A([128, NBL, 128], BF16, "ho")
                Hr = A([128, NBL, 128], BF16, "Hr")
                Hi = A([128, NBL, 128], BF16, "Hi")
                Yr = A([128, NBT, 128], BF16, "Yr")
                Wi = A([128, NBT, 128], BF16, "Wi")
                gout = A([128, T], BF16, "gout")
                woutc = A([64, 2, 128], F32, "woutc")
                absdc = A([128, 128], F32, "absdc")
                dec = A([128, 128], F32, "dec")
                tA = A([128, 512], F32, "tA")
                tAB = Buf("tA")
                tB = A([128, 512], F32, "tB")
                kflt, kfB = tA, tAB
                fw1 = A([33, 64], F32, "fw1")
                fw2 = A([64, 64], F32, "fw2")
                fw3 = A([64, 64], F32, "fw3")
                fbf = A([64, 4], F32, "fbf")
                cw = A([128, 3, 48], F32, "cw")
                cb = A([128, 48], F32, "cb")
                fbv = A([128, KC], F32, "fbv")
                tneg = A([128, NBL], F32, "tneg")
                m0 = A([128, NBL, 2], F32, "m0")
                phi = A([128, NBL, 2], F32, "phi")
                negpi = A([128, 1], F32, "negpi")
                kint = A([64, 512], mybir.dt.int32, "kint")
                kiB = Buf("kint")
                hcB = Buf(f"hyc{which}")
                hdB = [Buf("hd0"), Buf("hd1")]
                usbB, x1B, z32B, zbB, x0B, ztmB = Buf("usb"), Buf("x1c"), Buf("z32"), Buf("zb"), Buf("x0c"), Buf("ztm")
                heB, hoB, HB, YB, goutB = Buf("he"), Buf("ho"), Buf("H"), Buf("Y"), Buf("gout")
                wocB, adcB, decB, tBB = Buf(f"woc{which}"), Buf(f"adc{which}"), Buf("dec"), Buf("tB")
                for dst, src in ((fw1[:, :], hyw1_d), (fw2[:, :], hyw2_d), (fw3[:, :], hyw3_d), (fbf[:, :], hybf_d),
                                 (cw[:, :, :], hycw_d), (cb[:, :], hycb_d), (fbv[:, :], hyfb_d), (tneg[:, :], hc["tneg"]),
                                 (m0[:, :, :], hc["m0"]), (phi[:, :, :], hc["phi"])):
                    sp.dma(lambda e, dst=dst, src=src: e.dma_start(out=dst, in_=src), dbuf=hcB, w=[hcB])
                sp.dma(lambda e: e.dma_start(out=hd[0:33, 1, :], in_=hc["zemb"]), dbuf=hcB, w=[hcB, hdB[1]])
                dve.add(lambda e: e.memset(negpi[:, :], -PI), w=[hcB])
                dve.add(lambda e: e.memset(usb[:, :, :], 0.0), w=[usbB])
                CW = min(512, L)
                for li, (wt, kin, src_i, dst_i) in enumerate(((fw1, 33, 1, 0), (fw2, 64, 0, 1), (fw3, 64, 1, 0))):
                    for ct in range(L // CW):
                        ps, psb = P.psum()
                        pe.add(lambda e, ps=ps, wt=wt, kin=kin, src_i=src_i, ct=ct: e.matmul(
                            ps[0:64, 0:CW], wt[0:kin, :], hd[0:kin, src_i, ct * CW:(ct + 1) * CW], start=True, stop=True),
                            r=[hcB, hdB[src_i]], w=[psb])
                        dsl = hd[0:64, dst_i, ct * CW:(ct + 1) * CW]
                        dve.add(lambda e, ps=ps, dsl=dsl, li=li: e.tensor_scalar(
                            out=dsl, in0=ps[0:64, 0:CW], scalar1=fbf[:, li:li + 1], scalar2=fbf[:, 3:4],
                            op0=ALU.add, op1=ALU.mult), r=[psb, hcB], w=[hdB[dst_i]])
                        dve.add(lambda e, dsl=dsl: e.tensor_scalar(out=dsl, in0=dsl, scalar1=1.0 / (2.0 * PI), scalar2=8.5,
                                                                  op0=ALU.mult, op1=ALU.add), r=[hdB[dst_i]], w=[hdB[dst_i]])
                        dve.add(lambda e, dsl=dsl: e.tensor_copy(out=kint[0:64, 0:CW], in_=dsl), r=[hdB[dst_i]], w=[kiB])
                        dve.add(lambda e: e.tensor_copy(out=kflt[0:64, 0:CW], in_=kint[0:64, 0:CW]), r=[kiB], w=[kfB])
                        dve.add(lambda e, dsl=dsl: e.tensor_tensor(out=dsl, in0=dsl, in1=kflt[0:64, 0:CW], op=ALU.subtract),
                                r=[hdB[dst_i], kfB], w=[hdB[dst_i]])
                        dve.add(lambda e, dsl=dsl: e.tensor_scalar(out=kflt[0:64, 0:CW], in0=dsl, scalar1=0.0, scalar2=None, op0=ALU.is_lt),
                                r=[hdB[dst_i]], w=[kfB])
                        dve.add(lambda e, dsl=dsl: e.tensor_tensor(out=dsl, in0=dsl, in1=kflt[0:64, 0:CW], op=ALU.add),
                                r=[hdB[dst_i], kfB], w=[hdB[dst_i]])
                        act.add(lambda e, dsl=dsl: e.activation(out=dsl, in_=dsl, func=AF.Sin, bias=negpi[0:64, :], scale=2.0 * PI),
                                r=[hdB[dst_i], hcB], w=[hdB[dst_i]])
                hd3B = hdB[0]

                def proj_conv(c, col0, dst_fn):
                    wv, wb, wi = P.wget(lambda dt, c=c, col0=col0: dt["hy_w_in"][0, :, col0 + c * 128:col0 + (c + 1) * 128].rearrange(
                        "(kc p) n -> p kc n", p=128), (KC, 128))
                    for tt in range(NT):
                        ps, psb = P.psum()
                        for kc in range(KC):
                            pe.add(lambda e, ps=ps, kc=kc, tt=tt, wv=wv: e.matmul(
                                ps[:, :], wv[:, kc, :], ht[:, kc, tt * 512:(tt + 1) * 512], start=(kc == 0), stop=(kc == KC - 1)),
                                r=[wb, hBt[tt]], w=[psb], inc=(kc == KC - 1))
                        if isA:
                            act.add(lambda e, ps=ps, tt=tt: e.copy(out=usb[:, 0, 1 + tt * 512:1 + (tt + 1) * 512], in_=ps[:, :]),
                                    r=[psb], w=[usbB])
                        else:
                            act.add(lambda e, ps=ps: e.copy(out=usb[:, :, 1:L + 1], in_=ps[:, :].rearrange("p (a b) -> p a b", a=2)),
                                    r=[psb], w=[usbB])
                    P.wrel(wi)
                    ch = col0 // 128 + c
                    dve.add(lambda e: e.tensor_scalar(out=tmpc[:, :, :], in0=usb[:, :, 0:L], scalar1=cw[:, 0, ch:ch + 1],
                                                      scalar2=cb[:, ch:ch + 1], op0=ALU.mult, op1=ALU.add),
                            r=[usbB, hcB], w=[tmpcB])
                    dve.add(lambda e: e.scalar_tensor_tensor(out=tmpc[:, :, :], in0=usb[:, :, 1:L + 1], scalar=cw[:, 1, ch:ch + 1],
                                                             in1=tmpc[:, :, :], op0=ALU.mult, op1=ALU.add),
                            r=[usbB, hcB, tmpcB], w=[tmpcB])
                    dst_ap, dstB = dst_fn()
                    dve.add(lambda e: e.scalar_tensor_tensor(out=dst_ap, in0=usb[:, :, 2:L + 2], scalar=cw[:, 2, ch:ch + 1],
                                                             in1=tmpc[:, :, :], op0=ALU.mult, op1=ALU.add),
                            r=[usbB, hcB, tmpcB], w=[dstB])

                tmpc = les.enter_context(P.sb([128, nseq, L], F32, "tmpc"))
                tmpcB = Buf("tmpc")

                for c in range(KC):
                    proj_conv(c, 2048, lambda: (x1c[:, :, :], x1B))
                    proj_conv(c, 4096, lambda: (z32[:, :, :], z32B))
                    dve.add(lambda e: e.tensor_tensor(out=z32[:, :, :], in0=z32[:, :, :], in1=x1c[:, :, :], op=ALU.mult),
                            r=[z32B, x1B], w=[z32B])
                    act.add(lambda e: e.copy(out=zb[:, :].rearrange("p (a b) -> p a b", a=nseq), in_=z32[:, :, :]), r=[z32B], w=[zbB])
                    proj_conv(c, 0, lambda: (x0c[:, :].rearrange("p (a b) -> p a b", a=nseq), x0B))
                    z32f = z32[:, :, :].rearrange("p a b -> p (a b)")
                    for tq in range(NBT // 4):
                        ps, psb = P.psum()
                        for j in range(4):
                            tb = tq * 4 + j
                            pe.add(lambda e, ps=ps, j=j, tb=tb: e.transpose(ps[:, j * 128:(j + 1) * 128], z32f[:, tb * 128:(tb + 1) * 128],
                                                                            ident[:, :]), r=[z32B, constB], w=[psb], inc=(j == 3))
                        act.add(lambda e, ps=ps, tq=tq: e.copy(out=ztm[:, tq * 4:(tq + 1) * 4, :],
                                                              in_=ps[:, :].rearrange("p (a b) -> p a b", a=4)), r=[psb], w=[ztmB])
                    sp.dma(lambda e, c=c: e.dma_start(out=woutc[:, :, :], in_=hywo_d[:, :, c * 128:(c + 1) * 128]), dbuf=wocB, w=[wocB])
                    sp.dma(lambda e, c=c: e.dma_start(out=absdc[:, :], in_=absd_d[:, c * 128:(c + 1) * 128]), dbuf=adcB, w=[adcB])
                    for nb in range(NBL):
                        ps, psb = P.psum()
                        for dd in range(2):
                            pe.add(lambda e, ps=ps, dd=dd, nb=nb: e.matmul(
                                ps[:, dd * 128:(dd + 1) * 128], hd[0:64, 0, nb * 128:(nb + 1) * 128], woutc[:, dd, :],
                                start=True, stop=True), r=[hd3B, wocB], w=[psb], inc=(dd == 1))
                        act.add(lambda e, nb=nb: e.activation(out=dec[:, :], in_=absdc[:, :], func=AF.Exp, scale=tneg[:, nb:nb + 1]),
                                r=[adcB, hcB], w=[decB])
                        dve.add(lambda e, ps=ps: e.tensor_tensor(out=tA[:, 0:128], in0=ps[:, 0:128], in1=dec[:, :], op=ALU.mult),
                                r=[psb, decB], w=[tAB])
                        dve.add(lambda e, ps=ps: e.tensor_tensor(out=tB[:, 0:128], in0=ps[:, 128:256], in1=dec[:, :], op=ALU.mult),
                                r=[psb, decB], w=[tBB])
                        dve.add(lambda e, nb=nb: e.scalar_tensor_tensor(out=he[:, nb, :], in0=tB[:, 0:128], scalar=m0[:, nb, 0:1],
                                                                        in1=tA[:, 0:128], op0=ALU.mult, op1=ALU.add),
                                r=[tAB, tBB, hcB], w=[heB])
                        dve.add(lambda e, nb=nb: e.scalar_tensor_tensor(out=ho[:, nb, :], in0=tB[:, 0:128], scalar=m0[:, nb, 1:2],
                                                                        in1=tA[:, 0:128], op0=ALU.mult, op1=ALU.add),
                                r=[tAB, tBB, hcB], w=[hoB])
                    Ct, CtB, Cti = P.wget(lambda dt: dt[f"dftC{L}"][:, :].rearrange("(tb p) f -> p tb f", p=128), (NBL, L))
                    St, StB, Sti = P.wget(lambda dt: dt[f"dftS{L}"][:, :].rearrange("(tb p) f -> p tb f", p=128), (NBL, L))
                    for fo in range(NBL):
                        ps, psb = P.psum()
                        for gi, (tab, tabB, src, srcB) in enumerate(((Ct, CtB, he, heB), (St, StB, he, heB),
                                                                    (Ct, CtB, ho, hoB), (St, StB, ho, hoB))):
                            for nb in range(NBL):
                                pe.add(lambda e, ps=ps, gi=gi, tab=tab, src=src, nb=nb, fo=fo: e.matmul(
                                    ps[:, gi * 128:(gi + 1) * 128], tab[:, nb, fo * 128:(fo + 1) * 128], src[:, nb, :],
                                    start=(nb == 0), stop=(nb == NBL - 1)), r=[tabB, srcB], w=[psb],
                                    inc=(nb == NBL - 1 and gi == 3))
                        dve.add(lambda e, ps=ps, fo=fo: e.tensor_scalar(out=tA[:, 0:128], in0=ps[:, 128:256], scalar1=phi[:, fo, 1:2],
                                                                       scalar2=None, op0=ALU.mult), r=[psb, hcB], w=[tAB])
                        dve.add(lambda e, ps=ps, fo=fo: e.scalar_tensor_tensor(out=Hr[:, fo, :], in0=ps[:, 0:128], scalar=phi[:, fo, 0:1],
                                                                               in1=tA[:, 0:128], op0=ALU.mult, op1=ALU.add),
                                r=[psb, hcB, tAB], w=[HB])
                        dve.add(lambda e, ps=ps, fo=fo: e.tensor_scalar(out=tB[:, 0:128], in0=ps[:, 384:512], scalar1=phi[:, fo, 0:1],
                                                                       scalar2=None, op0=ALU.mult), r=[psb, hcB], w=[tBB])
                        dve.add(lambda e, ps=ps, fo=fo: e.scalar_tensor_tensor(out=Hi[:, fo, :], in0=ps[:, 256:384], scalar=phi[:, fo, 1:2],
                                                                               in1=tB[:, 0:128], op0=ALU.mult, op1=ALU.subtract),
                                r=[psb, hcB, tBB], w=[HB])
                    for sq_ in range(nseq):
                        for fo in range(NBL):
                            ps, psb = P.psum()
                            for gi, (tab, tabB) in enumerate(((Ct, CtB), (St, StB))):
                                for tb in range(NBL):
                                    pe.add(lambda e, ps=ps, gi=gi, tab=tab, tb=tb, fo=fo, sq_=sq_: e.matmul(
                                        ps[:, gi * 128:(gi + 1) * 128], tab[:, tb, fo * 128:(fo + 1) * 128], ztm[:, sq_ * NBL + tb, :],
                                        start=(tb == 0), stop=(tb == NBL - 1)), r=[tabB, ztmB], w=[psb],
                                        inc=(tb == NBL - 1 and gi == 1))
                            yi = sq_ * NBL + fo
                            dve.add(lambda e, ps=ps, fo=fo: e.tensor_tensor(out=tA[:, 0:128], in0=ps[:, 0:128], in1=Hr[:, fo, :], op=ALU.mult),
                                    r=[psb, HB], w=[tAB])
                            dve.add(lambda e, ps=ps, fo=fo: e.tensor_tensor(out=tA[:, 128:256], in0=ps[:, 128:256], in1=Hi[:, fo, :], op=ALU.mult),
                                    r=[psb, HB], w=[tAB])
                            dve.add(lambda e, yi=yi: e.tensor_tensor(out=Yr[:, yi, :], in0=tA[:, 0:128], in1=tA[:, 128:256], op=ALU.add),
                                    r=[tAB], w=[YB])
                            dve.add(lambda e, ps=ps, fo=fo: e.tensor_tensor(out=tB[:, 0:128], in0=ps[:, 128:256], in1=Hr[:, fo, :], op=ALU.mult),
                                    r=[psb, HB], w=[tBB])
                            dve.add(lambda e, ps=ps, fo=fo: e.tensor_tensor(out=tB[:, 128:256], in0=ps[:, 0:128], in1=Hi[:, fo, :], op=ALU.mult),
                                    r=[psb, HB], w=[tBB])
                            dve.add(lambda e, yi=yi: e.tensor_tensor(out=Wi[:, yi, :], in0=tB[:, 0:128], in1=tB[:, 128:256], op=ALU.subtract),
                                    r=[tBB], w=[YB])
                    NW = min(512, L)
                    for sq_ in range(nseq):
                        for tr in range(L // NW):
                            ps, psb = P.psum()
                            for fo in range(NBL):
                                pe.add(lambda e, ps=ps, fo=fo, sq_=sq_, tr=tr: e.matmul(
                                    ps[:, 0:NW], Yr[:, sq_ * NBL + fo, :], Ct[:, fo, tr * NW:(tr + 1) * NW], start=(fo == 0), stop=False),
                                    r=[YB, CtB], w=[psb], inc=False)
                                pe.add(lambda e, ps=ps, fo=fo, sq_=sq_, tr=tr: e.matmul(
                                    ps[:, 0:NW], Wi[:, sq_ * NBL + fo, :], St[:, fo, tr * NW:(tr + 1) * NW], start=False, stop=(fo == NBL - 1)),
                                    r=[YB, StB], w=[psb], inc=(fo == NBL - 1))
                            c0 = sq_ * L + tr * NW
                            dve.add(lambda e, c0=c0, c=c: e.tensor_scalar(out=tA[:, 0:NW], in0=zb[:, c0:c0 + NW], scalar1=fbv[:, c:c + 1],
                                                                         scalar2=None, op0=ALU.mult), r=[zbB, hcB], w=[tAB])
                            dve.add(lambda e, ps=ps: e.scalar_tensor_tensor(out=tB[:, 0:NW], in0=ps[:, 0:NW], scalar=1.0 / L, in1=tA[:, 0:NW],
                                                                            op0=ALU.mult, op1=ALU.add), r=[psb, tAB], w=[tBB])
                            dve.add(lambda e, c0=c0: e.tensor_tensor(out=gout[:, c0:c0 + NW], in0=tB[:, 0:NW], in1=x0c[:, c0:c0 + NW], op=ALU.mult),
                                    r=[tBB, x0B], w=[goutB])
                    P.wrel(Cti)
                    P.wrel(Sti)
                    wo, wob, woi = P.wget(lambda dt, c=c: dt["hy_w_out"][0, c * 128:(c + 1) * 128, :].rearrange(
                        "(kc p) n -> p kc n", p=128), (1, D))
                    for oc in range(KC):
                        for tt in range(NT):
                            ps, psb = P.psum()
                            pe.add(lambda e, ps=ps, oc=oc, tt=tt, wo=wo: e.matmul(
                                ps[:, :], wo[:, 0, oc * 128:(oc + 1) * 128], gout[:, tt * 512:(tt + 1) * 512], start=True, stop=True),
                                r=[wob, goutB], w=[psb])
                            dve.add(lambda e, ps=ps, oc=oc, tt=tt: e.scalar_tensor_tensor(
                                out=xt[:, oc, tt * 512:(tt + 1) * 512], in0=ps[:, :], scalar=normc[:, 2, oc:oc + 1],
                                in1=xt[:, oc, tt * 512:(tt + 1) * 512], op0=ALU.mult, op1=ALU.add),
                                r=[psb, normcB, xBt[tt]], w=[xBt[tt]])
                    P.wrel(woi)
                S.barrier()

        MIXERS[1] = hyena


        def rwkv(l):
            L = LS if isA else LP
            nseq = T // L
            NBT = T // 128
            E05 = math.exp(-0.5)
            allh = [hBt[t] for t in range(NT)]
            with ExitStack() as les:
                def A(shape, dt, name, st=les):
                    return st.enter_context(P.sb(shape, dt, name))
                mu = A([128, 6, KC], F32, "mu"); om = A([128, 6, KC], F32, "om"); hm = A([128, 6, KC], F32, "hm")
                w0t = A([128, 2, KC], F32, "w0t"); a0t = A([128, 2, KC], F32, "a0t")
                vec = A([128, 3, KC], F32, "vec"); omka = A([128, KC], F32, "omka")
                lnp = A([128, 2, KC], F32, "lnp")
                bones = A([128, 128], F32, "bones")
                ohm = A([128, 64 + 128 + 254], F32, "ohm")
                zsel = A([128, 254], BF16, "zsel")
                rcB = Buf(f"rwc{which}")
                for dst, src in ((mu[:, :, :], rwmu_d), (w0t[:, :, :], rww0_d), (a0t[:, :, :], rwa0_d), (vec[:, :, :], rwvec_d),
                                 (lnp[:, :, :], rwln_d), (bones[:, :], bones_d), (ohm[:, :], ohm_d)):
                    sp.dma(lambda e, dst=dst, src=src: e.dma_start(out=dst, in_=src), dbuf=rcB, w=[rcB])
                dve.add(lambda e: e.tensor_scalar(out=om[:, :, :], in0=mu[:, :, :], scalar1=-1.0, scalar2=1.0, op0=ALU.mult, op1=ALU.add),
                        r=[rcB], w=[rcB])
                dve.add(lambda e: e.tensor_scalar(out=hm[:, :, :], in0=mu[:, :, :], scalar1=0.5, scalar2=None, op0=ALU.mult), r=[rcB], w=[rcB])
                dve.add(lambda e: e.tensor_scalar(out=omka[:, :], in0=vec[:, 1, :], scalar1=-1.0, scalar2=1.0, op0=ALU.mult, op1=ALU.add),
                        r=[rcB], w=[rcB])
                dve.add(lambda e: e.tensor_copy(out=zsel[:, :], in_=ohm[:, 192:446]), r=[rcB], w=[rcB])
                S.barrier()

                with ExitStack() as s1:
                    tw = A([128, 2, T], BF16, "tw", s1); ta = A([128, 2, T], BF16, "ta", s1); tg = A([128, 2, T], BF16, "tg", s1)
                    twB, taB, tgB = Buf("tw"), Buf("ta"), Buf("tg")
                    with ExitStack() as s1a:
                        mixb = A([128, KC, T], BF16, "mixb", s1a)
                        t1 = A([128, T], F32, "t1", s1a)
                        stg = A([128, 1, 512], F32, "stg", s1a)
                        stgb = A([128, 1, 512], BF16, "stgb", s1a)
                        mixB, t1B = Buf("mixb"), Buf("t1")
                        stgB = [Buf(f"stg{which}0"), Buf(f"stg{which}1")]
                        stgbB = [Buf(f"stgb{which}0"), Buf(f"stgb{which}1")]
                        cnt = {"s": 0, "b": 0}

                        def build_mix(m):
                            for c in range(KC):
                                dve.add(lambda e, c=c: e.tensor_tensor(out=t1[:, 1:T - 1], in0=ht[:, c, 0:T - 2], in1=ht[:, c, 2:T], op=ALU.add),
                                        r=allh, w=[t1B])
                                for sq_ in range(nseq):
                                    s0 = sq_ * L
                                    dve.add(lambda e, c=c, s0=s0: e.tensor_copy(out=t1[:, s0:s0 + 1], in_=ht[:, c, s0 + 1:s0 + 2]), r=allh, w=[t1B])
                                    dve.add(lambda e, c=c, s0=s0: e.tensor_copy(out=t1[:, s0 + L - 1:s0 + L], in_=ht[:, c, s0 + L - 2:s0 + L - 1]),
                                            r=allh, w=[t1B])
                                dve.add(lambda e, c=c, m=m: e.tensor_scalar(out=t1[:, :], in0=t1[:, :], scalar1=hm[:, m, c:c + 1], scalar2=None,
                                                                           op0=ALU.mult), r=[t1B, rcB], w=[t1B])
                                dve.add(lambda e, c=c, m=m: e.scalar_tensor_tensor(out=mixb[:, c, :], in0=ht[:, c, 0:T], scalar=om[:, m, c:c + 1],
                                                                                  in1=t1[:, :], op0=ALU.mult, op1=ALU.add),
                                        r=allh + [t1B, rcB], w=[mixB])

                        def proj_to_scratch(wname, key, also_tm=False):
                            for g in range(4):
                                wv, wb, wi = P.wget(lambda dt, g=g: dt[wname][:, g * 512:(g + 1) * 512].rearrange("(kc p) n -> p kc n", p=128),
                                                    (KC, 512))
                                for nn in range(4):
                                    c = g * 4 + nn
                                    for tt in range(NT):
                                        ps, psb = P.psum()
                                        for kc in range(KC):
                                            pe.add(lambda e, ps=ps, kc=kc, tt=tt, nn=nn, wv=wv: e.matmul(
                                                ps[:, :], wv[:, kc, nn * 128:(nn + 1) * 128], mixb[:, kc, tt * 512:(tt + 1) * 512],
                                                start=(kc == 0), stop=(kc == KC - 1)), r=[wb, mixB], w=[psb], inc=(kc == KC - 1))
                                        k = 0
                                        act.add(lambda e, ps=ps, k=k: e.copy(out=stg[:, k, :], in_=ps[:, :]), r=[psb], w=[stgB[k]])
                                        sp.dma(lambda e, k=k, c=c, tt=tt: e.dma_start(out=SC[key][c, :, tt * 512:(tt + 1) * 512], in_=stg[:, k, :]),
                                               dbuf=stgB[k], r=[stgB[k]], w=[SCB[key]])
                                if also_tm:
                                    for tb in range(NBT):
                                        ps, psb = P.psum()
                                        for kc in range(KC):
                                            pe.add(lambda e, ps=ps, kc=kc, tb=tb, wv=wv: e.matmul(
                                                ps[:, :], mixb[:, kc, tb * 128:(tb + 1) * 128], wv[:, kc, :],
                                                start=(kc == 0), stop=(kc == KC - 1)), r=[wb, mixB], w=[psb], inc=(kc == KC - 1))
                                        k = 0
                                        act.add(lambda e, ps=ps, k=k: e.copy(out=stgb[:, k, :], in_=ps[:, :]), r=[psb], w=[stgbB[k]])
                                        sp.dma(lambda e, k=k, tb=tb, g=g: e.dma_start(
                                            out=SC["vtm"][tb * 128:(tb + 1) * 128, g * 512:(g + 1) * 512], in_=stgb[:, k, :]),
                                            dbuf=stgbB[k], r=[stgbB[k]], w=[SCB["vtm"]])
                                P.wrel(wi)

                        def lora1(wname, d, ncols, dst, dstB, func):
                            wv, wb, wi = P.wget(lambda dt: (dt[wname][d] if d is not None else dt[wname]).rearrange(
                                "(kc p) n -> p kc n", p=128), (KC, ncols))
                            for oc in range((ncols + 127) // 128):
                                mm_ = min(128, ncols - oc * 128)
                                for tt in range(NT):
                                    ps, psb = P.psum()
                                    for kc in range(KC):
                                        pe.add(lambda e, ps=ps, kc=kc, tt=tt, oc=oc, mm_=mm_, wv=wv: e.matmul(
                                            ps[0:mm_, :], wv[:, kc, oc * 128:oc * 128 + mm_], mixb[:, kc, tt * 512:(tt + 1) * 512],
                                            start=(kc == 0), stop=(kc == KC - 1)), r=[wb, mixB], w=[psb], inc=(kc == KC - 1))
                                    sl = dst(oc, mm_, tt)
                                    act.add(lambda e, ps=ps, sl=sl, mm_=mm_: e.activation(out=sl, in_=ps[0:mm_, :], func=func), r=[psb], w=[dstB])
                            P.wrel(wi)

                        build_mix(0)
                        proj_to_scratch("rw_wr", "r")
                        build_mix(2)
                        proj_to_scratch("rw_wk", "k")
                        build_mix(3)
                        proj_to_scratch("rw_wv", "v", also_tm=True)
                        build_mix(1)
                        for d in range(2):
                            lora1("rw_w1", d, 96, lambda oc, m_, tt, d=d: tw[0:96, d, tt * 512:(tt + 1) * 512], twB, AF.Tanh)
                        build_mix(4)
                        for d in range(2):
                            lora1("rw_a1", d, 96, lambda oc, m_, tt, d=d: ta[0:96, d, tt * 512:(tt + 1) * 512], taB, AF.Copy)
                        build_mix(5)
                        lora1("rw_g1", None, 256, lambda oc, m_, tt: tg[:, oc, tt * 512:(tt + 1) * 512], tgB, AF.Sigmoid)
                        S.barrier()

                    with ExitStack() as s1b:
                        rc = A([128, T], F32, "rc", s1b); kcx = A([128, T], F32, "kcx", s1b); vc = A([128, T], F32, "vc", s1b)
                        ad = A([128, 2, T], F32, "ad", s1b)
                        X1 = A([128, T], F32, "X1", s1b); X2 = A([128, T], F32, "X2", s1b); X3 = A([128, T], F32, "X3", s1b)
                        lw2 = A([128, 2, 128], BF16, "lw2", s1b); la2 = A([128, 2, 128], BF16, "la2", s1b); lg2 = A([128, 2, 128], BF16, "lg2", s1b)
                        rcB_, kcB, vcB, adB = Buf(f"rc{which}"), Buf(f"kc{which}"), Buf(f"vc{which}"), Buf("ad")
                        X1B, X2B, X3B = Buf(f"X1{which}"), Buf(f"X2{which}"), Buf(f"X3{which}")
                        l2B = Buf(f"l2{which}")

                        def store(X, XB, key, c):
                            sp.dma(lambda e: e.dma_start(out=SC[key][c, :, 0:T], in_=X[:, :]), dbuf=XB, r=[XB], w=[SCB[key]])

                        for c in range(KC):
                            for dstt, dB, key in ((rc, rcB_, "r"), (kcx, kcB, "k"), (vc, vcB, "v")):
                                sp.dma(lambda e, dstt=dstt, key=key, c=c: e.dma_start(out=dstt[:, :], in_=SC[key][c, :, 0:T]),
                                       dbuf=dB, r=[SCB[key]], w=[dB])
                            for d in range(2):
                                pool.dma(lambda e, d=d, c=c: e.dma_start(out=lw2[0:96, d, :], in_=rw_w2_d[d, :, c * 128:(c + 1) * 128]),
                                         dbuf=l2B, w=[l2B])
                                pool.dma(lambda e, d=d, c=c: e.dma_start(out=la2[0:96, d, :], in_=rw_a2_d[d, :, c * 128:(c + 1) * 128]),
                                         dbuf=l2B, w=[l2B])
                                pool.dma(lambda e, d=d, c=c: e.dma_start(out=lg2[:, d, :], in_=rw_g2_d[d * 128:(d + 1) * 128, c * 128:(c + 1) * 128]),
                                         dbuf=l2B, w=[l2B])
                            for d in range(2):
                                for tt in range(NT):
                                    ps, psb = P.psum()
                                    pe.add(lambda e, ps=ps, d=d, tt=tt: e.matmul(ps[:, :], lw2[0:96, d, :], tw[0:96, d, tt * 512:(tt + 1) * 512],
                                                                                start=True, stop=True), r=[l2B, twB], w=[psb])
                                    act.add(lambda e, ps=ps, d=d, tt=tt, c=c: e.activation(out=X1[:, tt * 512:(tt + 1) * 512], in_=ps[:, :],
                                                                                          func=AF.Sigmoid, bias=w0t[:, d, c:c + 1], scale=1.0),
                                            r=[psb, rcB], w=[X1B])
                                if CFG.get("rw_scan", "chunk") == "chunk":
                                    act.add(lambda e: e.activation(out=X1[:, :], in_=X1[:, :], func=AF.Copy, scale=-E05), r=[X1B], w=[X1B])
                                    store(X1, X1B, f"lw{d}", c)
                                else:
                                    act.add(lambda e: e.activation(out=X1[:, :], in_=X1[:, :], func=AF.Exp, scale=-E05), r=[X1B], w=[X1B])
                                    store(X1, X1B, f"w{d}", c)
                            for tt in range(NT):
                                for d in range(2):
                                    ps, psb = P.psum()
                                    pe.add(lambda e, ps=ps, d=d, tt=tt: e.matmul(ps[:, :], la2[0:96, d, :], ta[0:96, d, tt * 512:(tt + 1) * 512],
                                                                                start=True, stop=True), r=[l2B, taB], w=[psb])
                                    act.add(lambda e, ps=ps, d=d, tt=tt, c=c: e.activation(out=ad[:, d, tt * 512:(tt + 1) * 512], in_=ps[:, :],
                                                                                          func=AF.Sigmoid, bias=a0t[:, d, c:c + 1], scale=1.0),
                                            r=[psb, rcB], w=[adB])
                                ps, psb = P.psum()
                                for kc in range(2):
                                    pe.add(lambda e, ps=ps, kc=kc, tt=tt: e.matmul(ps[:, :], lg2[:, kc, :], tg[:, kc, tt * 512:(tt + 1) * 512],
                                                                                  start=(kc == 0), stop=(kc == 1)), r=[l2B, tgB], w=[psb], inc=(kc == 1))
                                act.add(lambda e, ps=ps, tt=tt: e.copy(out=X2[:, tt * 512:(tt + 1) * 512], in_=ps[:, :]), r=[psb], w=[X2B])
                            store(X2, X2B, "g", c)
                            dve.add(lambda e, c=c: e.tensor_scalar(out=X1[:, :], in0=kcx[:, :], scalar1=vec[:, 0, c:c + 1], scalar2=None, op0=ALU.mult),
                                    r=[kcB, rcB], w=[X1B])
                            act.add(lambda e: e.activation(out=X2[:, :], in_=X1[:, :], func=AF.Square), r=[X1B], w=[X2B])
                            for tt in range(NT):
                                ps, psb = P.psum()
                                pe.add(lambda e, ps=ps, tt=tt: e.matmul(ps[:, :], bones[:, :], X2[:, tt * 512:(tt + 1) * 512], start=True, stop=True),
                                       r=[rcB, X2B], w=[psb])
                                dve.add(lambda e, ps=ps, tt=tt: e.tensor_scalar(out=X3[:, tt * 512:(tt + 1) * 512], in0=ps[:, :], scalar1=1e-12,
                                                                               scalar2=None, op0=ALU.add), r=[psb], w=[X3B])
                            act.add(lambda e: e.activation(out=X3[:, :], in_=X3[:, :], func=AF.Sqrt), r=[X3B], w=[X3B])
                            dve.add(lambda e: e.reciprocal(out=X3[:, :], in_=X3[:, :]), r=[X3B], w=[X3B])
                            dve.add(lambda e: e.scalar_tensor_tensor(out=X1[:, :], in0=X1[:, :], scalar=-1.0, in1=X3[:, :], op0=ALU.mult, op1=ALU.mult),
                                    r=[X1B, X3B], w=[X1B])
                            store(X1, X1B, "a", c)
                            for d in range(2):
                                dve.add(lambda e, d=d: e.scalar_tensor_tensor(out=X2[:, :], in0=X1[:, :], scalar=-1.0, in1=ad[:, d, :],
                                                                             op0=ALU.mult, op1=ALU.mult), r=[X1B, adB], w=[X2B])
                                store(X2, X2B, f"b{d}", c)
                            dve.add(lambda e, c=c: e.tensor_scalar(out=X3[:, :], in0=ad[:, 0, :], scalar1=vec[:, 1, c:c + 1], scalar2=omka[:, c:c + 1],
                                                                  op0=ALU.mult, op1=ALU.add), r=[adB, rcB], w=[X3B])
                            dve.add(lambda e: e.tensor_tensor(out=X3[:, :], in0=X3[:, :], in1=kcx[:, :], op=ALU.mult), r=[X3B, kcB], w=[X3B])
                            store(X3, X3B, "kd0", c)
                            dve.add(lambda e, c=c: e.tensor_scalar(out=X2[:, :], in0=ad[:, 1, :], scalar1=vec[:, 1, c:c + 1], scalar2=omka[:, c:c + 1],
                                                                  op0=ALU.mult, op1=ALU.add), r=[adB, rcB], w=[X2B])
                            dve.add(lambda e: e.tensor_tensor(out=X2[:, :], in0=X2[:, :], in1=kcx[:, :], op=ALU.mult), r=[X2B, kcB], w=[X2B])
                            store(X2, X2B, "kd1", c)
                            dve.add(lambda e: e.tensor_tensor(out=X3[:, :], in0=X3[:, :], in1=X2[:, :], op=ALU.add), r=[X3B, X2B], w=[X3B])
                            dve.add(lambda e, c=c: e.scalar_tensor_tensor(out=X3[:, :], in0=rc[:, :], scalar=vec[:, 2, c:c + 1], in1=X3[:, :],
                                                                         op0=ALU.mult, op1=ALU.mult), r=[rcB_, rcB, X3B], w=[X3B])
                            for tt in range(NT):
                                ps, psb = P.psum()
                                pe.add(lambda e, ps=ps, tt=tt: e.matmul(ps[:, :], bones[:, :], X3[:, tt * 512:(tt + 1) * 512], start=True, stop=True),
                                       r=[rcB, X3B], w=[psb])
                                dve.add(lambda e, ps=ps, tt=tt: e.tensor_tensor(out=X2[:, tt * 512:(tt + 1) * 512], in0=ps[:, :],
                                                                               in1=vc[:, tt * 512:(tt + 1) * 512], op=ALU.mult), r=[psb, vcB], w=[X2B])
                            store(X2, X2B, "bon", c)
                        S.barrier()


                def chunk_scan_and_out():
                    L = LS if isA else LP
                    nseq = T // L
                    NCH = L // 64
                    NH = min(NCH, 4)
                    NHALF = NCH // NH
                    with ExitStack() as s2:
                        def A2(shape, dt, name):
                            return s2.enter_context(P.sb(shape, dt, name))
                        W = NH * 64
                        inp5 = A2([128, 5, W], F32, "inp5")
                        cs = A2([128, NH, 64], F32, "cs"); e1 = A2([128, NH, 64], F32, "e1"); e2 = A2([128, NH, 64], F32, "e2")
                        ones = A2([128, 64], F32, "ones")
                        Bbd = A2([128, NH, 2, 64], BF16, "Bbd"); Kbd = A2([128, NH, 2, 64], BF16, "Kbd")
                        Xa = A2([128, NH, 2, 128], BF16, "Xa"); Ya = A2([128, NH, 2, 128], BF16, "Ya")
                        Xo32 = A2([128, NH, 128], BF16, "Xo32"); Yo32 = A2([128, NH, 128], BF16, "Yo32"); Yo64 = A2([128, NH, 128], BF16, "Yo64")
                        Tn = A2([128, NH, 128], BF16, "Tn"); PQ = A2([128, NH, 2, 128], BF16, "PQ")
                        hmk = A2([128, 3, 128], BF16, "hmk"); hmf = A2([128, 3, 128], F32, "hmf")
                        SETS = []
                        for si in range(2):
                            SETS.append((A2([128, NH, 2, 64], BF16, f"AR{si}"), A2([128, NH, 2, 64], BF16, f"Abd{si}"), A2([128, NH], F32, f"PC{si}"),
                                         A2([128, NH, 128], BF16, f"Tt{si}"), A2([128, NH, 2, 64], BF16, f"Akb{si}"),
                                         A2([128, NH, 64], BF16, f"Arb{si}"), A2([128, NH, 64], BF16, f"Ark{si}"),
                                         A2([128, NH, 128], BF16, f"Btb{si}"), A2([128, NH, 128], BF16, f"Ktb{si}"),
                                         A2([128, NH, 64], BF16, f"Vst{si}"), A2([128, NH, 2, 64], BF16, f"Vbd{si}")))
                        M0 = A2([128, 64], F32, "M0"); M0b = A2([128, 64], BF16, "M0b"); M0bd = A2([128, 2, 64], BF16, "M0bd")
                        Gb = A2([128, 64], BF16, "Gb"); Ub = A2([128, 64], BF16, "Ub"); Ubd = A2([128, 2, 64], BF16, "Ubd")
                        ybuf = A2([128, W], F32, "ybuf")
                        mk = A2([128, 2, 128], F32, "mk")
                        sbd = A2([128, 128], F32, "sbd2"); sst = A2([128, 64], F32, "sst2")
                        BSH = {n_: Buf(n_ + which) for n_ in ("inp5", "cs", "e1", "e2", "Bbd", "Kbd", "Xa", "Ya", "Xo32", "Yo32", "Yo64", "Tn", "PQ",
                                                               "M0", "M0b", "M0bd", "Gb", "Ub", "Ubd", "ybuf", "mk", "sbd2", "sst2")}
                        BSET = [{n_: Buf(f"{n_}{si}{which}") for n_ in ("AR", "Abd", "PC", "Tt", "Akb", "Arb", "Ark", "Btb", "Ktb", "Vst", "Vbd")}
                                for si in range(2)]
                        B_ = BSH
                        sp.dma(lambda e: e.dma_start(out=mk[:, :, :], in_=rwmask_d[:, :, :]), dbuf=B_["mk"], w=[B_["mk"]])
                        sp.dma(lambda e: e.dma_start(out=hmf[:, :, :], in_=hmask_d[:, :, :]), dbuf=B_["mk"], w=[B_["mk"]])
                        dve.add(lambda e: e.tensor_copy(out=hmk[:, :, :], in_=hmf[:, :, :]), r=[B_["mk"]], w=[B_["mk"]])
                        dve.add(lambda e: e.memset(ones[:, :], 1.0), w=[B_["mk"]])
                        for tz, nm in ((Bbd[:, :, :, :], "Bbd"), (Kbd[:, :, :, :], "Kbd"), (Xa[:, :, :, :], "Xa"), (Ya[:, :, :, :], "Ya")):
                            dve.add(lambda e, tz=tz: e.memset(tz, 0.0), w=[BSH[nm]])
                        for si in range(2):
                            for idx, nm in ((1, "Abd"), (4, "Akb"), (10, "Vbd")):
                                dve.add(lambda e, tz=SETS[si][idx]: e.memset(tz[:, :, :, :], 0.0), w=[BSET[si][nm]])
                        dve.add(lambda e: e.memset(M0bd[:, :, :], 0.0), w=[B_["M0bd"]])
                        dve.add(lambda e: e.memset(Ubd[:, :, :], 0.0), w=[B_["Ubd"]])
                        dve.add(lambda e: e.memset(sbd[:, :], 0.0), w=[B_["sbd2"]])
                        ei = {"k": 0}

                        def evac(fn_act, fn_dve, r, w):
                            ei["k"] += 1
                            if fn_act is not None:
                                act.add(fn_act, r=r, w=w)
                            else:
                                dve.add(fn_dve, r=r, w=w)

                        def bd_place(dst, dstB, src_fn, srcB, dve_only=False):
                            for hh in range(2):
                                sl = slice(hh * 64, (hh + 1) * 64)
                                act.add(lambda e, sl=sl, hh=hh: e.copy(out=dst[sl, hh, :], in_=src_fn(sl)), r=srcB, w=[dstB])

                        def prep_gen(u, si):
                            sq_, d, c, hv, first, last = u
                            AR, Abd, PC, Tt, Akb, Arb, Ark, Btb, Ktb, Vst, Vbd = SETS[si]
                            B_ = dict(BSH)
                            B_.update(BSET[si])
                            s0 = sq_ * L
                            nlist = list(range(NH)) if d == 0 else list(range(NH - 1, -1, -1))
                            t0 = s0 + hv * W
                            keys5 = ("a", "r", f"lw{d}", f"b{d}", f"kd{d}")
                            for j, key in enumerate(keys5):
                                sp.dma(lambda e, j=j, key=key: e.dma_start(out=inp5[:, j, :], in_=SC[key][c, :, t0:t0 + W]), dbuf=B_["inp5"],
                                       r=[SCB[key]], w=[B_["inp5"]])
                            for n in range(NH):
                                for hh in range(2):
                                    sp.dma(lambda e, n=n, hh=hh: e.dma_start(
                                        out=Vst[hh * 64:(hh + 1) * 64, n, :],
                                        in_=SC["vtm"][t0 + n * 64:t0 + (n + 1) * 64, (2 * c + hh) * 64:(2 * c + hh + 1) * 64]),
                                        dbuf=B_["Vst"], r=[SCB["vtm"]], w=[B_["Vst"]])
                            v3 = lambda j: inp5[:, j, :].rearrange("p (n t) -> p n t", n=NH)
                            for n in range(NH):
                                dve.add(lambda e, n=n: e.tensor_tensor_scan(out=cs[:, n, :], data0=ones[:, :], data1=inp5[:, 2, n * 64:(n + 1) * 64],
                                                                            initial=0.0, op0=ALU.mult, op1=ALU.add),
                                        r=[B_["inp5"], B_["mk"]], w=[B_["cs"]])
                            if d == 1:
                                dve.add(lambda e: e.tensor_tensor(out=e1[:, :, :], in0=v3(2), in1=cs[:, :, :], op=ALU.subtract),
                                        r=[B_["inp5"], B_["cs"]], w=[B_["e1"]])
                                dve.add(lambda e: e.tensor_copy(out=e2[:, :, 0:1], in_=cs[:, :, 63:64]), r=[B_["cs"]], w=[B_["e2"]])
                                dve.add(lambda e: e.tensor_tensor(out=cs[:, :, :], in0=e1[:, :, :], in1=bcast(e2[:, :, 0:1], [128, NH, 64]), op=ALU.add),
                                        r=[B_["e1"], B_["e2"]], w=[B_["cs"]])
                            lastpos = 63 if d == 0 else 0
                            act.add(lambda e: e.activation(out=PC[:, :], in_=cs[:, :, lastpos], func=AF.Exp), r=[B_["cs"]], w=[B_["PC"]])
                            dve.add(lambda e: e.tensor_tensor(out=e1[:, :, :], in0=cs[:, :, :], in1=v3(2), op=ALU.subtract), r=[B_["cs"], B_["inp5"]], w=[B_["e1"]])
                            act.add(lambda e: e.activation(out=e1[:, :, :], in_=e1[:, :, :], func=AF.Exp), r=[B_["e1"]], w=[B_["e1"]])
                            pool.add(lambda e: e.tensor_tensor(out=AR[:, :, 0, :], in0=v3(0), in1=e1[:, :, :], op=ALU.mult), r=[B_["inp5"], B_["e1"]], w=[B_["AR"]])
                            act.add(lambda e: e.activation(out=e2[:, :, :], in_=cs[:, :, :], func=AF.Exp), r=[B_["cs"]], w=[B_["e2"]])
                            pool.add(lambda e: e.tensor_tensor(out=AR[:, :, 1, :], in0=v3(1), in1=e2[:, :, :], op=ALU.mult), r=[B_["inp5"], B_["e2"]], w=[B_["AR"]])
                            act.add(lambda e: e.activation(out=e1[:, :, :], in_=cs[:, :, :], func=AF.Exp, scale=-1.0), r=[B_["cs"], B_["AR"]], w=[B_["e1"]])
                            for hh in range(2):
                                sl = slice(hh * 64, (hh + 1) * 64)
                                pool.add(lambda e, sl=sl, hh=hh: e.tensor_copy(out=Abd[sl, :, hh, :], in_=AR[sl, :, 0, :]), r=[B_["AR"]], w=[B_["Abd"]])
                                pool.add(lambda e, sl=sl, hh=hh: e.tensor_tensor(out=Bbd[sl, :, hh, :], in0=v3(3)[sl], in1=e1[sl, :, :], op=ALU.mult),
                                        r=[B_["inp5"], B_["e1"]], w=[B_["Bbd"]])
                                pool.add(lambda e, sl=sl, hh=hh: e.tensor_tensor(out=Kbd[sl, :, hh, :], in0=v3(4)[sl], in1=e1[sl, :, :], op=ALU.mult),
                                        r=[B_["inp5"], B_["e1"]], w=[B_["Kbd"]])
                                pool.add(lambda e, sl=sl, hh=hh: e.tensor_copy(out=Vbd[sl, :, hh, :], in_=Vst[sl, :, :]), r=[B_["Vst"]], w=[B_["Vbd"]])
                            f2 = lambda t_, n: t_[:, n, :, :].rearrange("p a b -> p (a b)")
                            yield
                            for n in range(NH):
                                ps1, p1b = P.psum()
                                pe.add(lambda e, ps1=ps1, n=n: e.matmul(ps1[:, 0:128], f2(Bbd, n), f2(AR, n), start=True, stop=True),
                                       r=[B_["Bbd"], B_["AR"]], w=[p1b])
                                for hh in range(2):
                                    sl = slice(hh * 64, (hh + 1) * 64)
                                    dve.add(lambda e, ps1=ps1, n=n, sl=sl, hh=hh: e.tensor_tensor(
                                        out=Xa[sl, n, 0, hh * 64:(hh + 1) * 64], in0=ps1[sl, 0:64], in1=mk[sl, d, 0:64], op=ALU.mult),
                                        r=[p1b, B_["mk"]], w=[B_["Xa"]])
                                dve.add(lambda e, ps1=ps1, n=n: e.tensor_tensor(out=Arb[:, n, :], in0=ps1[:, 64:128], in1=mk[:, d, 64:128], op=ALU.mult),
                                        r=[p1b, B_["mk"]], w=[B_["Arb"]])
                                ps2, p2b = P.psum()
                                pe.add(lambda e, ps2=ps2, n=n: e.matmul(ps2[:, 0:128], f2(Kbd, n), f2(AR, n), start=True, stop=True),
                                       r=[B_["Kbd"], B_["AR"]], w=[p2b])
                                for hh in range(2):
                                    sl = slice(hh * 64, (hh + 1) * 64)
                                    dve.add(lambda e, ps2=ps2, n=n, sl=sl, hh=hh: e.tensor_tensor(
                                        out=Akb[sl, n, hh, :], in0=ps2[sl, 0:64], in1=mk[sl, d, 0:64], op=ALU.mult), r=[p2b, B_["mk"]], w=[B_["Akb"]])
                                dve.add(lambda e, ps2=ps2, n=n: e.tensor_tensor(out=Ark[:, n, :], in0=ps2[:, 64:128], in1=mk[:, d, 64:128], op=ALU.mult),
                                        r=[p2b, B_["mk"]], w=[B_["Ark"]])
                                for src_fn, srcN, dst, dstN in ((lambda n=n: f2(Bbd, n), "Bbd", Btb, "Btb"), (lambda n=n: f2(Kbd, n), "Kbd", Ktb, "Ktb")):
                                    ps3, p3b = P.psum()
                                    pe.add(lambda e, ps3=ps3, src_fn=src_fn: e.matmul(ps3[:, 0:128], src_fn(), identb[:, :], start=True, stop=True),
                                           r=[B_[srcN], c2], w=[p3b])
                                    evac(lambda e, ps3=ps3, dst=dst, n=n: e.copy(out=dst[:, n, :], in_=ps3[:, 0:128]),
                                         lambda e, ps3=ps3, dst=dst, n=n: e.tensor_copy(out=dst[:, n, :], in_=ps3[:, 0:128]), r=[p3b], w=[B_[dstN]])
                                ps4, p4b = P.psum()
                                pe.add(lambda e, ps4=ps4, n=n: e.matmul(ps4[:, 0:128], Xa[:, n, 0, :], identb[:, :], start=True, stop=True),
                                       r=[B_["Xa"], c2], w=[p4b])
                                evac(lambda e, ps4=ps4, n=n: e.copy(out=Ya[:, n, 0, :], in_=ps4[:, 0:128]),
                                     lambda e, ps4=ps4, n=n: e.tensor_copy(out=Ya[:, n, 0, :], in_=ps4[:, 0:128]), r=[p4b], w=[B_["Ya"]])
                                yield
                            def mm_evac(n, lhs_fn, rhs_fn, rB, dst_fn, dstB, add_fn=None, addB=None):
                                ps_, psb_ = P.psum()
                                pe.add(lambda e, ps_=ps_: e.matmul(ps_[:, 0:128], lhs_fn(), rhs_fn(), start=True, stop=True), r=rB, w=[psb_])
                                if add_fn is None:
                                    evac(lambda e, ps_=ps_: e.copy(out=dst_fn(), in_=ps_[:, 0:128]),
                                         lambda e, ps_=ps_: e.tensor_copy(out=dst_fn(), in_=ps_[:, 0:128]), r=[psb_], w=[dstB])
                                else:
                                    dve.add(lambda e, ps_=ps_: e.tensor_tensor(out=dst_fn(), in0=ps_[:, 0:128], in1=add_fn(), op=ALU.add),
                                            r=[psb_, addB], w=[dstB])
                            for n in range(NH):
                                pool.add(lambda e, n=n: e.tensor_tensor(out=Xo32[:, n, :], in0=Xa[:, n, 0, :], in1=hmk[:, 1, :], op=ALU.mult),
                                        r=[B_["Xa"], B_["mk"]], w=[B_["Xo32"]])
                                pool.add(lambda e, n=n: e.tensor_tensor(out=Yo32[:, n, :], in0=Ya[:, n, 0, :], in1=hmk[:, 1, :], op=ALU.mult),
                                        r=[B_["Ya"], B_["mk"]], w=[B_["Yo32"]])
                                pool.add(lambda e, n=n: e.tensor_tensor(out=Yo64[:, n, :], in0=Ya[:, n, 0, :], in1=hmk[:, 2, :], op=ALU.mult),
                                        r=[B_["Ya"], B_["mk"]], w=[B_["Yo64"]])
                                pool.add(lambda e, n=n: e.tensor_tensor(out=Xa[:, n, 0, :], in0=Xa[:, n, 0, :], in1=hmk[:, 0, :], op=ALU.mult),
                                        r=[B_["Xa"], B_["mk"], B_["Xo32"]], w=[B_["Xa"]])
                                pool.add(lambda e, n=n: e.tensor_tensor(out=Ya[:, n, 0, :], in0=Ya[:, n, 0, :], in1=hmk[:, 0, :], op=ALU.mult),
                                        r=[B_["Ya"], B_["mk"], B_["Yo32"], B_["Yo64"]], w=[B_["Ya"]])
                                pool.add(lambda e, n=n: e.tensor_tensor(out=Tt[:, n, :], in0=Xa[:, n, 0, :], in1=identb[:, :], op=ALU.add),
                                        r=[B_["Xa"], c2], w=[B_["Tt"]])
                                pool.add(lambda e, n=n: e.tensor_tensor(out=Tn[:, n, :], in0=Ya[:, n, 0, :], in1=identb[:, :], op=ALU.add),
                                        r=[B_["Ya"], c2], w=[B_["Tn"]])
                            yield
                            for m in range(1, 4):
                                yield
                                cur, prv = m % 2, (m - 1) % 2
                                for n in range(NH):
                                    mm_evac(n, lambda n=n, prv=prv: Ya[:, n, prv, :], lambda n=n, prv=prv: Xa[:, n, prv, :], [B_["Xa"], B_["Ya"]],
                                            lambda n=n, cur=cur: Xa[:, n, cur, :], B_["Xa"])
                                    mm_evac(n, lambda n=n, prv=prv: Xa[:, n, prv, :], lambda n=n, prv=prv: Ya[:, n, prv, :], [B_["Xa"], B_["Ya"]],
                                            lambda n=n, cur=cur: Ya[:, n, cur, :], B_["Ya"])
                                for n in range(NH):
                                    mm_evac(n, lambda n=n, cur=cur: Ya[:, n, cur, :], lambda n=n: Tt[:, n, :], [B_["Ya"], B_["Tt"]],
                                            lambda n=n: Tt[:, n, :], B_["Tt"], add_fn=lambda n=n: Tt[:, n, :], addB=B_["Tt"])
                                    mm_evac(n, lambda n=n, cur=cur: Xa[:, n, cur, :], lambda n=n: Tn[:, n, :], [B_["Xa"], B_["Tn"]],
                                            lambda n=n: Tn[:, n, :], B_["Tn"], add_fn=lambda n=n: Tn[:, n, :], addB=B_["Tn"])
                            yield
                            for n in range(NH):
                                mm_evac(n, lambda n=n: Yo32[:, n, :], lambda n=n: Tt[:, n, :], [B_["Yo32"], B_["Tt"]], lambda n=n: PQ[:, n, 0, :], B_["PQ"])
                                mm_evac(n, lambda n=n: Xo32[:, n, :], lambda n=n: Tn[:, n, :], [B_["Xo32"], B_["Tn"]], lambda n=n: PQ[:, n, 1, :], B_["PQ"])
                            for n in range(NH):
                                ps_a, psb_a = P.psum()
                                pe.add(lambda e, ps_a=ps_a, n=n: e.matmul(ps_a[:, 0:128], Tn[:, n, :], PQ[:, n, 0, :], start=True, stop=True),
                                       r=[B_["Tn"], B_["PQ"]], w=[psb_a])
                                ps_b, psb_b = P.psum()
                                pe.add(lambda e, ps_b=ps_b, n=n: e.matmul(ps_b[:, 0:128], Tt[:, n, :], PQ[:, n, 1, :], start=True, stop=True),
                                       r=[B_["Tt"], B_["PQ"]], w=[psb_b])
                                dve.add(lambda e, ps_a=ps_a, n=n: e.tensor_tensor(out=Tt[:, n, :], in0=ps_a[:, 0:128], in1=Tt[:, n, :], op=ALU.add),
                                        r=[psb_a, B_["Tt"]], w=[B_["Tt"]])
                                dve.add(lambda e, ps_b=ps_b, n=n: e.tensor_tensor(out=Tn[:, n, :], in0=ps_b[:, 0:128], in1=Tn[:, n, :], op=ALU.add),
                                        r=[psb_b, B_["Tn"]], w=[B_["Tn"]])
                            yield
                            for n in range(NH):
                                mm_evac(n, lambda n=n: Yo64[:, n, :], lambda n=n: Tt[:, n, :], [B_["Yo64"], B_["Tt"]], lambda n=n: PQ[:, n, 0, :], B_["PQ"])
                            for n in range(NH):
                                mm_evac(n, lambda n=n: Tn[:, n, :], lambda n=n: PQ[:, n, 0, :], [B_["Tn"], B_["PQ"]],
                                        lambda n=n: Tt[:, n, :], B_["Tt"], add_fn=lambda n=n: Tt[:, n, :], addB=B_["Tt"])

                        def chain_gen(u, si):
                            sq_, d, c, hv, first, last = u
                            AR, Abd, PC, Tt, Akb, Arb, Ark, Btb, Ktb, Vst, Vbd = SETS[si]
                            B_ = dict(BSH)
                            B_.update(BSET[si])
                            s0 = sq_ * L
                            nlist = list(range(NH)) if d == 0 else list(range(NH - 1, -1, -1))
                            t0 = s0 + hv * W
                            f2 = lambda t_, n: t_[:, n, :, :].rearrange("p a b -> p (a b)")
                            if first:
                                if isA:
                                    for hh in range(2):
                                        sp.dma(lambda e, hh=hh: e.dma_start(out=sbd[hh * 64:(hh + 1) * 64, hh * 64:(hh + 1) * 64],
                                                                            in_=st_d[d, 2 * c + hh, :, :]), dbuf=B_["sbd2"], w=[B_["sbd2"]])
                                    ps, psb = P.psum()
                                    pe.add(lambda e, ps=ps: e.transpose(ps[:, 0:128], sbd[:, :], ident[:, :]), r=[B_["sbd2"], constB], w=[psb])
                                    act.add(lambda e, ps=ps: e.copy(out=M0[0:64, :], in_=ps[0:64, 0:64]), r=[psb], w=[B_["M0"]])
                                    act.add(lambda e, ps=ps: e.copy(out=M0[64:128, :], in_=ps[64:128, 64:128]), r=[psb], w=[B_["M0"]])
                                else:
                                    dve.add(lambda e: e.memset(M0[:, :], 0.0), w=[B_["M0"]])
                                act.add(lambda e: e.copy(out=M0b[:, :], in_=M0[:, :]), r=[B_["M0"]], w=[B_["M0b"]])
                                bd_place(M0bd, B_["M0bd"], lambda sl: M0[sl, :], [B_["M0"]])
                            for n in nlist:
                                psG, pGb = P.psum()
                                pe.add(lambda e, psG=psG, n=n: e.matmul(psG[:, 0:64], f2(Abd, n), M0b[:, :], start=True, stop=False),
                                       r=[B_["Abd"], B_["M0b"]], w=[pGb], inc=False)
                                pe.add(lambda e, psG=psG, n=n: e.matmul(psG[:, 0:64], f2(Akb, n), Vst[:, n, :], start=False, stop=True),
                                       r=[B_["Akb"], B_["Vst"]], w=[pGb])
                                act.add(lambda e, psG=psG: e.copy(out=Gb[:, :], in_=psG[:, 0:64]), r=[pGb], w=[B_["Gb"]])
                                psU, pUb = P.psum()
                                pe.add(lambda e, psU=psU, n=n: e.matmul(psU[:, 0:64], Tt[:, n, :], Gb[:, :], start=True, stop=True),
                                       r=[B_["Tt"], B_["Gb"]], w=[pUb])
                                act.add(lambda e, psU=psU: e.copy(out=Ub[:, :], in_=psU[:, 0:64]), r=[pUb], w=[B_["Ub"]])
                                bd_place(Ubd, B_["Ubd"], lambda sl: Ub[sl, :], [B_["Ub"]])
                                psY, pYb = P.psum()
                                pe.add(lambda e, psY=psY, n=n: e.matmul(psY[:, 0:64], M0bd[:, :, :].rearrange("p a b -> p (a b)"), AR[:, n, 1, :],
                                                                        start=True, stop=False), r=[B_["M0bd"], B_["AR"]], w=[pYb], inc=False)
                                pe.add(lambda e, psY=psY, n=n: e.matmul(psY[:, 0:64], Ubd[:, :, :].rearrange("p a b -> p (a b)"), Arb[:, n, :],
                                                                        start=False, stop=False), r=[B_["Ubd"], B_["Arb"]], w=[pYb], inc=False)
                                pe.add(lambda e, psY=psY, n=n: e.matmul(psY[:, 0:64], f2(Vbd, n), Ark[:, n, :], start=False, stop=True),
                                       r=[B_["Vbd"], B_["Ark"]], w=[pYb])
                                act.add(lambda e, psY=psY, n=n: e.copy(out=ybuf[:, n * 64:(n + 1) * 64], in_=psY[:, 0:64]), r=[pYb], w=[B_["ybuf"]])
                                psM, pMb = P.psum()
                                pe.add(lambda e, psM=psM, n=n: e.matmul(psM[:, 0:64], Btb[:, n, :], Ub[:, :], start=True, stop=False),
                                       r=[B_["Btb"], B_["Ub"]], w=[pMb], inc=False)
                                pe.add(lambda e, psM=psM, n=n: e.matmul(psM[:, 0:64], Ktb[:, n, :], Vst[:, n, :], start=False, stop=True),
                                       r=[B_["Ktb"], B_["Vst"]], w=[pMb])
                                dve.add(lambda e, n=n: e.tensor_scalar(out=M0[:, :], in0=M0[:, :], scalar1=PC[:, n:n + 1], scalar2=None, op0=ALU.mult),
                                        r=[B_["M0"], B_["PC"]], w=[B_["M0"]])
                                dve.add(lambda e, psM=psM, n=n: e.scalar_tensor_tensor(out=M0[:, :], in0=psM[:, 0:64], scalar=PC[:, n:n + 1], in1=M0[:, :],
                                                                                      op0=ALU.mult, op1=ALU.add), r=[pMb, B_["PC"], B_["M0"]], w=[B_["M0"]])
                                act.add(lambda e: e.copy(out=M0b[:, :], in_=M0[:, :]), r=[B_["M0"]], w=[B_["M0b"]])
                                bd_place(M0bd, B_["M0bd"], lambda sl: M0[sl, :], [B_["M0"]])
                                yield
                            sp.dma(lambda e: e.dma_start(out=SC[f"yf{d}"][c, :, t0:t0 + W], in_=ybuf[:, :]), dbuf=B_["ybuf"], r=[B_["ybuf"]],
                                   w=[SCB[f"yf{d}"]])
                            if last and not isA:
                                for hh in range(2):
                                    sl = slice(hh * 64, (hh + 1) * 64)
                                    dve.add(lambda e, sl=sl: e.tensor_copy(out=sbd[sl, sl], in_=M0[sl, :]), r=[B_["M0"]], w=[B_["sbd2"]])
                                ps, psb = P.psum()
                                pe.add(lambda e, ps=ps: e.transpose(ps[:, 0:128], sbd[:, :], ident[:, :]), r=[B_["sbd2"], constB], w=[psb])
                                act.add(lambda e, ps=ps: e.copy(out=sst[0:64, :], in_=ps[0:64, 0:64]), r=[psb], w=[B_["sst2"]])
                                act.add(lambda e, ps=ps: e.copy(out=sst[64:128, :], in_=ps[64:128, 64:128]), r=[psb], w=[B_["sst2"]])
                                sp.dma(lambda e: e.dma_start(out=nst[sq_, d, 2 * c:2 * c + 2, :, :].rearrange("h v k -> (h v) k"), in_=sst[:, :]),
                                       dbuf=B_["sst2"], r=[B_["sst2"]])

                        units = []
                        for sq_ in range(nseq):
                            for d in range(2):
                                for c in range(KC):
                                    hvs = list(range(NHALF)) if d == 0 else list(range(NHALF - 1, -1, -1))
                                    for ih, hv in enumerate(hvs):
                                        units.append((sq_, d, c, hv, ih == 0, ih == NHALF - 1))
                        for _ in prep_gen(units[0], 0):
                            pass
                        for i_, u in enumerate(units):
                            cg = chain_gen(u, i_ % 2)
                            pg = prep_gen(units[i_ + 1], (i_ + 1) % 2) if i_ + 1 < len(units) else None
                            while cg is not None or pg is not None:
                                if cg is not None:
                                    try:
                                        next(cg)
                                    except StopIteration:
                                        cg = None
                                if pg is not None:
                                    try:
                                        next(pg)
                                    except StopIteration:
                                        pg = None
                        S.barrier()

                    with ExitStack() as s3:
                        def A3(shape, dt, name):
                            return s3.enter_context(P.sb(shape, dt, name))
                        ya = A3([128, 2, 512], F32, "ya"); yb = A3([128, 2, 512], F32, "yb")
                        bt = A3([128, 2, 512], F32, "bt3"); gtt = A3([128, 2, 512], F32, "gt3")
                        yc = A3([128, 512], F32, "yc3"); ysq = A3([128, 512], F32, "ysq3"); rs = A3([128, 512], F32, "rs3")
                        opt = A3([128, KC, 512], BF16, "opt3")
                        lnw = A3([128, 2, KC], F32, "lnw3")
                        ldB = [Buf(f"s3ld{which}0"), Buf(f"s3ld{which}1")]
                        ycB, ysqB, rsB, optB, lnB = Buf("yc3"), Buf("ysq3"), Buf("rs3"), Buf("opt3"), Buf(f"lnw3{which}")
                        sp.dma(lambda e: e.dma_start(out=lnw[:, :, :], in_=rwln2_d[:, :, :]), dbuf=lnB, w=[lnB])
                        li = 0
                        for tt in range(NT):
                            cols = slice(tt * 512, (tt + 1) * 512)
                            for c in range(KC):
                                k = li % 2
                                li += 1
                                for dstt, key in ((ya, "yf0"), (yb, "yf1"), (bt, "bon"), (gtt, "g")):
                                    sp.dma(lambda e, dstt=dstt, key=key, c=c, k=k, cols=cols: e.dma_start(out=dstt[:, k, :], in_=SC[key][c, :, cols]),
                                           dbuf=ldB[k], r=[SCB[key]], w=[ldB[k]])
                                dve.add(lambda e, k=k: e.tensor_tensor(out=yc[:, :], in0=ya[:, k, :], in1=yb[:, k, :], op=ALU.add), r=[ldB[k]], w=[ycB])
                                ps, psb = P.psum()
                                pe.add(lambda e, ps=ps: e.matmul(ps[:, :], bones[:, :], yc[:, :], start=True, stop=True), r=[rcB, ycB], w=[psb])
                                dve.add(lambda e, ps=ps: e.scalar_tensor_tensor(out=yc[:, :], in0=ps[:, :], scalar=-1.0 / 64, in1=yc[:, :],
                                                                               op0=ALU.mult, op1=ALU.add), r=[psb, ycB], w=[ycB])
                                act.add(lambda e: e.activation(out=ysq[:, :], in_=yc[:, :], func=AF.Square), r=[ycB], w=[ysqB])
                                ps2, ps2b = P.psum()
                                pe.add(lambda e, ps2=ps2: e.matmul(ps2[:, :], bones[:, :], ysq[:, :], start=True, stop=True), r=[rcB, ysqB], w=[ps2b])
                                dve.add(lambda e, ps2=ps2: e.tensor_scalar(out=rs[:, :], in0=ps2[:, :], scalar1=1.0 / 64, scalar2=64e-5,
                                                                          op0=ALU.mult, op1=ALU.add), r=[ps2b], w=[rsB])
                                act.add(lambda e: e.activation(out=rs[:, :], in_=rs[:, :], func=AF.Sqrt), r=[rsB], w=[rsB])
                                dve.add(lambda e: e.reciprocal(out=rs[:, :], in_=rs[:, :]), r=[rsB], w=[rsB])
                                dve.add(lambda e, c=c: e.scalar_tensor_tensor(out=yc[:, :], in0=yc[:, :], scalar=lnw[:, 0, c:c + 1], in1=rs[:, :],
                                                                             op0=ALU.mult, op1=ALU.mult), r=[ycB, rsB, lnB], w=[ycB])
                                dve.add(lambda e, c=c, k=k: e.scalar_tensor_tensor(out=yc[:, :], in0=yc[:, :], scalar=lnw[:, 1, c:c + 1], in1=bt[:, k, :],
                                                                                  op0=ALU.add, op1=ALU.add), r=[ycB, lnB, ldB[k]], w=[ycB])
                                dve.add(lambda e, c=c, k=k: e.tensor_tensor(out=opt[:, c, :], in0=yc[:, :], in1=gtt[:, k, :], op=ALU.mult),
                                        r=[ycB, ldB[k]], w=[optB])
                            for g in range(4):
                                wv, wb, wi = P.wget(lambda dt, g=g: dt["rw_wo"][:, g * 512:(g + 1) * 512].rearrange("(kc p) n -> p kc n", p=128), (KC, 512))
                                for nn in range(4):
                                    oc = g * 4 + nn
                                    ps, psb = P.psum()
                                    for kc in range(KC):
                                        pe.add(lambda e, ps=ps, kc=kc, nn=nn, wv=wv: e.matmul(ps[:, :], wv[:, kc, nn * 128:(nn + 1) * 128], opt[:, kc, :],
                                                                                            start=(kc == 0), stop=(kc == KC - 1)),
                                               r=[wb, optB], w=[psb], inc=(kc == KC - 1))
                                    dve.add(lambda e, ps=ps, oc=oc, tt=tt: e.scalar_tensor_tensor(
                                        out=xt[:, oc, tt * 512:(tt + 1) * 512], in0=ps[:, :], scalar=normc[:, 2, oc:oc + 1],
                                        in1=xt[:, oc, tt * 512:(tt + 1) * 512], op0=ALU.mult, op1=ALU.add),
                                        r=[psb, normcB, xBt[tt]], w=[xBt[tt]])
                                P.wrel(wi)
                        S.barrier()

                SB = 16
                if CFG.get("rw_scan", "chunk") == "chunk":
                    chunk_scan_and_out()
                    return
                with ExitStack() as s2:
                    M = A([128, 16, 64], F32, "M", s2)
                    T1 = A([128, 16, 64], F32, "T1", s2)
                    T2 = A([128, 16, 64], F32, "T2", s2)
                    t4 = A([128, 16, 64], BF16, "t4", s2)
                    inx = A([128, 2, 5, 16 * SB], F32, "inx", s2)
                    vblk = A([128, 2, 1024], BF16, "vblk", s2)
                    sel = A([128, 64, 128], BF16, "sel", s2)
                    yst = A([128, 512], F32, "yst", s2)
                    sbd = A([128, 8, 128], F32, "sbd", s2)
                    sst = A([128, 16, 64] if not isA else [128, 1, 4], F32, "sst", s2)
                    ohb = A([128, 64 + 128], BF16, "ohb", s2)
                    MB, T1B_, T2B, t4B, selB, ystB, sbdB, sstB = (Buf("M"), Buf("T1"), Buf("T2"), Buf("t4"), Buf("sel"), Buf(f"yst{which}"),
                                                                  Buf(f"sbd{which}"), Buf(f"sst{which}"))
                    inB = [Buf(f"inx{which}0"), Buf(f"inx{which}1")]
                    vbB = [Buf(f"vblk{which}0"), Buf(f"vblk{which}1")]
                    dve.add(lambda e: e.tensor_copy(out=ohb[:, :], in_=ohm[:, 0:192]), r=[rcB], w=[selB])
                    dve.add(lambda e: e.tensor_tensor(out=sel[:, :, :], in0=bcast(ohb[:, 64:192].unsqueeze(1), [128, 64, 128]),
                                                      in1=bcast(ohb[:, 0:64].unsqueeze(2), [128, 64, 128]), op=ALU.mult), r=[selB], w=[selB])
                    saP, saB2 = P.ps_pair[0], [P.ps_b[0], P.ps_b[1]]
                    vbP, vbB2 = P.ps_pair[1], [P.ps_b[2], P.ps_b[3]]
                    yP = [P.ps_pair[2], P.ps_pair[3]]
                    yB2 = [[P.ps_b[4], P.ps_b[5]], [P.ps_b[6], P.ps_b[7]]]
                    keys = lambda d: ("a", "r", f"w{d}", f"b{d}", f"kd{d}")

                    def scan(sq_, d):
                        s0 = sq_ * L
                        if isA:
                            for h8 in range(2):
                                dve.add(lambda e: e.memset(sbd[:, :, :], 0.0), w=[sbdB])
                                for hh in range(2):
                                    sp.dma(lambda e, hh=hh, h8=h8: e.dma_start(
                                        out=sbd[hh * 64:(hh + 1) * 64, :, hh * 64:(hh + 1) * 64],
                                        in_=st_d[d].rearrange("(hj hh) v k -> hh v hj k", hh=2)[hh][:, h8 * 8:(h8 + 1) * 8, :]), dbuf=sbdB, w=[sbdB])
                                for h_ in range(8):
                                    hj = h8 * 8 + h_
                                    ps, psb = P.psum()
                                    pe.add(lambda e, ps=ps, h_=h_: e.transpose(ps[:, 0:128], sbd[:, h_, :], ident[:, :]), r=[sbdB, constB], w=[psb])
                                    act.add(lambda e, ps=ps, hj=hj: e.copy(out=M[0:64, hj, :], in_=ps[0:64, 0:64]), r=[psb], w=[MB])
                                    act.add(lambda e, ps=ps, hj=hj: e.copy(out=M[64:128, hj, :], in_=ps[64:128, 64:128]), r=[psb], w=[MB])
                        else:
                            dve.add(lambda e: e.memset(M[:, :, :], 0.0), w=[MB])

                        def load_in(blk, par):
                            t0 = s0 + blk * SB
                            for j, key in enumerate(keys(d)):
                                sp.dma(lambda e, j=j, key=key, t0=t0, par=par: e.dma_start(
                                    out=inx[:, par, j, :].rearrange("p (c t) -> p c t", c=16),
                                    in_=SC[key][:, :, t0:t0 + SB].rearrange("c p t -> p c t")), dbuf=inB[par], r=[SCB[key]], w=[inB[par]])

                        def load_v(b64, par):
                            t0 = s0 + b64 * 64
                            for hh in range(2):
                                sp.dma(lambda e, hh=hh, t0=t0, par=par: e.dma_start(
                                    out=vblk[hh * 64:(hh + 1) * 64, par, :].rearrange("p (hj v) -> p hj v", hj=16),
                                    in_=SC["vtm"][t0:t0 + 64, :].rearrange("t (hj hh v) -> hh t hj v", hh=2, v=64)[hh]),
                                    dbuf=vbB[par], r=[SCB["vtm"]], w=[vbB[par]])

                        tseq = list(range(L)) if d == 0 else list(range(L - 1, -1, -1))
                        load_in(tseq[0] // SB, 0)
                        load_v(tseq[0] // 64, 0)
                        nb16 = 0
                        nb64 = 0
                        for i, t in enumerate(tseq):
                            if i % SB == 0:
                                par = nb16 % 2
                                nb16 += 1
                                if i + SB < L:
                                    load_in(tseq[i + SB] // SB, 1 - par)
                            if i % 64 == 0:
                                vpar = nb64 % 2
                                nb64 += 1
                                if i + 64 < L:
                                    load_v(tseq[i + 64] // 64, 1 - vpar)
                            tj = t % SB
                            tl = t % 64

                            def bc(j, par=par, tj=tj):
                                return bcast(inx[:, par, j, :].rearrange("p (c t) -> p c t", c=16)[:, :, tj:tj + 1], [128, 16, 64])
                            dve.add(lambda e, bc=bc: e.tensor_tensor(out=T1[:, :, :], in0=M[:, :, :], in1=bc(0), op=ALU.mult),
                                    r=[MB, inB[par]], w=[T1B_])
                            for hf in range(2):
                                pe.add(lambda e, hf=hf: e.matmul(saP[:, hf * 512:(hf + 1) * 512], bones[:, :], T1[:, hf * 8:(hf + 1) * 8, :],
                                                                 start=True, stop=True), r=[rcB, T1B_], w=saB2, inc=(hf == 1))
                            dve.add(lambda e, bc=bc: e.tensor_tensor(out=M[:, :, :], in0=M[:, :, :], in1=bc(2), op=ALU.mult),
                                    r=[MB, inB[par]], w=[MB])
                            for hf in range(2):
                                pe.add(lambda e, hf=hf, tl=tl, vpar=vpar: e.matmul(vbP[:, hf * 512:(hf + 1) * 512], sel[:, tl, :],
                                                                                 vblk[:, vpar, hf * 512:(hf + 1) * 512], start=True, stop=True),
                                       r=[selB, vbB[vpar]], w=vbB2, inc=(hf == 1))
                            dve.add(lambda e, bc=bc: e.tensor_tensor(out=T1[:, :, :], in0=saP[:, :].rearrange("p (c v) -> p c v", c=16), in1=bc(3),
                                                                    op=ALU.mult), r=saB2 + [inB[par]], w=[T1B_])
                            dve.add(lambda e: e.tensor_tensor(out=M[:, :, :], in0=M[:, :, :], in1=T1[:, :, :], op=ALU.add), r=[MB, T1B_], w=[MB])
                            dve.add(lambda e, bc=bc: e.tensor_tensor(out=T2[:, :, :], in0=vbP[:, :].rearrange("p (c v) -> p c v", c=16), in1=bc(4),
                                                                    op=ALU.mult), r=vbB2 + [inB[par]], w=[T2B])
                            dve.add(lambda e: e.tensor_tensor(out=M[:, :, :], in0=M[:, :, :], in1=T2[:, :, :], op=ALU.add), r=[MB, T2B], w=[MB])
                            dve.add(lambda e, bc=bc: e.tensor_tensor(out=t4[:, :, :], in0=M[:, :, :], in1=bc(1), op=ALU.mult),
                                    r=[MB, inB[par]], w=[t4B])
                            yp = (nb64 - 1) % 2
                            for hf in range(2):
                                pe.add(lambda e, hf=hf, tl=tl, yp=yp, i=i: e.matmul(
                                    yP[yp][:, hf * 512:(hf + 1) * 512], zsel[:, 126 - 2 * tl:254 - 2 * tl], t4[:, hf * 8:(hf + 1) * 8, :],
                                    start=(i % 64 == 0), stop=(i % 64 == 63)), r=[rcB, t4B], w=yB2[yp], inc=(hf == 1))
                            if i % 64 == 63:
                                b64g = (s0 + t) // 64
                                for hf in range(2):
                                    act.add(lambda e, yp=yp, hf=hf: e.copy(out=yst[:, :], in_=yP[yp][:, hf * 512:(hf + 1) * 512]), r=yB2[yp], w=[ystB])
                                    sp.dma(lambda e, b64g=b64g, hf=hf: e.dma_start(out=SC[f"y{d}"][b64g, :, hf * 512:(hf + 1) * 512], in_=yst[:, :]),
                                           dbuf=ystB, r=[ystB], w=[SCB[f"y{d}"]])
                        if not isA:
                            for h8 in range(2):
                                dve.add(lambda e: e.memset(sbd[:, :, :], 0.0), w=[sbdB])
                                dve.add(lambda e, h8=h8: e.tensor_copy(out=sbd[0:64, :, 0:64], in_=M[0:64, h8 * 8:(h8 + 1) * 8, :]), r=[MB], w=[sbdB])
                                dve.add(lambda e, h8=h8: e.tensor_copy(out=sbd[64:128, :, 64:128], in_=M[64:128, h8 * 8:(h8 + 1) * 8, :]), r=[MB], w=[sbdB])
                                for h_ in range(8):
                                    hj = h8 * 8 + h_
                                    ps, psb = P.psum()
                                    pe.add(lambda e, ps=ps, h_=h_: e.transpose(ps[:, 0:128], sbd[:, h_, :], ident[:, :]), r=[sbdB, constB], w=[psb])
                                    act.add(lambda e, ps=ps, hj=hj: e.copy(out=sst[0:64, hj, :], in_=ps[0:64, 0:64]), r=[psb], w=[sstB])
                                    act.add(lambda e, ps=ps, hj=hj: e.copy(out=sst[64:128, hj, :], in_=ps[64:128, 64:128]), r=[psb], w=[sstB])
                            for hh in range(2):
                                sp.dma(lambda e, hh=hh: e.dma_start(
                                    out=nst[sq_, d].rearrange("(hj hh) v k -> hh v hj k", hh=2)[hh], in_=sst[hh * 64:(hh + 1) * 64, :, :]),
                                    dbuf=sstB, r=[sstB])

                    for sq_ in range(nseq):
                        for d in range(2):
                            scan(sq_, d)
                    S.barrier()

                with ExitStack() as s3:
                    y0 = A([128, 2, 1024], F32, "y0", s3); y1 = A([128, 2, 1024], F32, "y1", s3)
                    yc = A([128, 16, 64], F32, "yc", s3); ysq = A([128, 16, 64], F32, "ysq", s3)
                    st1 = A([128, 16], F32, "st1", s3); st2 = A([128, 16], F32, "st2", s3)
                    opt = A([128, KC, 512], BF16, "opt", s3)
                    bt = A([128, 2, 512], F32, "bt", s3); gtt = A([128, 2, 512], F32, "gtt", s3)
                    ot = A([128, 512], F32, "ot", s3)
                    yB_ = [Buf(f"yl{which}0"), Buf(f"yl{which}1")]
                    ycB, ysqB, stB_, optB, otB = Buf("yc"), Buf("ysq"), Buf("st12"), Buf("opt"), Buf("ot")
                    bgB = [Buf(f"bg{which}0"), Buf(f"bg{which}1")]
                    li = 0
                    for tt in range(NT):
                        for b8 in range(8):
                            b64g = tt * 8 + b8
                            k = li % 2
                            li += 1
                            sp.dma(lambda e, k=k, b64g=b64g: e.dma_start(out=y0[:, k, :], in_=SC["y0"][b64g, :, :]), dbuf=yB_[k], r=[SCB["y0"]], w=[yB_[k]])
                            sp.dma(lambda e, k=k, b64g=b64g: e.dma_start(out=y1[:, k, :], in_=SC["y1"][b64g, :, :]), dbuf=yB_[k], r=[SCB["y1"]], w=[yB_[k]])
                            y0v = y0[:, k, :].rearrange("p (c v) -> p c v", c=16)
                            y1v = y1[:, k, :].rearrange("p (c v) -> p c v", c=16)
                            dve.add(lambda e, y0v=y0v, y1v=y1v: e.tensor_tensor(out=yc[:, :, :], in0=y0v, in1=y1v, op=ALU.add), r=[yB_[k]], w=[ycB])
                            dve.add(lambda e: e.tensor_reduce(out=st1[:, :], in_=yc[:, :, :], axis=AX.X, op=ALU.add), r=[ycB], w=[stB_])
                            dve.add(lambda e: e.tensor_scalar(out=st1[:, :], in0=st1[:, :], scalar1=1.0 / 64, scalar2=None, op0=ALU.mult), r=[stB_], w=[stB_])
                            dve.add(lambda e: e.tensor_tensor(out=yc[:, :, :], in0=yc[:, :, :], in1=bcast(st1[:, :].unsqueeze(2), [128, 16, 64]),
                                                              op=ALU.subtract), r=[ycB, stB_], w=[ycB])
                            act.add(lambda e: e.activation(out=ysq[:, :, :], in_=yc[:, :, :], func=AF.Square), r=[ycB], w=[ysqB])
                            dve.add(lambda e: e.tensor_reduce(out=st2[:, :], in_=ysq[:, :, :], axis=AX.X, op=ALU.add), r=[ysqB], w=[stB_])
                            dve.add(lambda e: e.tensor_scalar(out=st2[:, :], in0=st2[:, :], scalar1=1.0 / 64, scalar2=64e-5, op0=ALU.mult, op1=ALU.add),
                                    r=[stB_], w=[stB_])
                            act.add(lambda e: e.activation(out=st2[:, :], in_=st2[:, :], func=AF.Sqrt), r=[stB_], w=[stB_])
                            dve.add(lambda e: e.reciprocal(out=st2[:, :], in_=st2[:, :]), r=[stB_], w=[stB_])
                            dve.add(lambda e: e.tensor_tensor(out=yc[:, :, :], in0=yc[:, :, :], in1=bcast(st2[:, :].unsqueeze(2), [128, 16, 64]),
                                                              op=ALU.mult), r=[ycB, stB_], w=[ycB])
                            for qg in range(2):
                                ps, psb = P.psum()
                                for j in range(4):
                                    q = qg * 4 + j
                                    pe.add(lambda e, ps=ps, j=j, q=q: e.transpose(ps[:, j * 128:(j + 1) * 128], yc[:, 2 * q:2 * q + 2, :], ident[:, :]),
                                           r=[ycB, constB], w=[psb], inc=(j == 3))
                                for j in range(4):
                                    q = qg * 4 + j
                                    act.add(lambda e, ps=ps, j=j, q=q, b8=b8: e.copy(
                                        out=opt[:, 2 * q:2 * q + 2, b8 * 64:(b8 + 1) * 64],
                                        in_=ps[:, j * 128:(j + 1) * 128].rearrange("p (tl hh) -> p hh tl", hh=2)), r=[psb], w=[optB])
                        for pc in range(KC):
                            q, hh = pc // 2, pc % 2
                            k = pc % 2
                            for e_ in range(2):
                                cstd = 2 * q + e_
                                for dstt, key in ((bt, "bon"), (gtt, "g")):
                                    sp.dma(lambda e, dstt=dstt, key=key, cstd=cstd, e_=e_, hh=hh, k=k, tt=tt: e.dma_start(
                                        out=dstt[e_ * 64:(e_ + 1) * 64, k, :], in_=SC[key][cstd, hh * 64:(hh + 1) * 64, tt * 512:(tt + 1) * 512]),
                                        dbuf=bgB[k], r=[SCB[key]], w=[bgB[k]])
                            dve.add(lambda e, pc=pc: e.tensor_scalar(out=ot[:, :], in0=opt[:, pc, :], scalar1=lnp[:, 0, pc:pc + 1],
                                                                    scalar2=lnp[:, 1, pc:pc + 1], op0=ALU.mult, op1=ALU.add), r=[optB, rcB], w=[otB])
                            dve.add(lambda e, k=k: e.tensor_tensor(out=ot[:, :], in0=ot[:, :], in1=bt[:, k, :], op=ALU.add), r=[otB, bgB[k]], w=[otB])
                            dve.add(lambda e, k=k, pc=pc: e.tensor_tensor(out=opt[:, pc, :], in0=ot[:, :], in1=gtt[:, k, :], op=ALU.mult),
                                    r=[otB, bgB[k]], w=[optB])
                        for g in range(4):
                            wv, wb, wi = P.wget(lambda dt, g=g: dt["rw_wo_perm"][:, g * 512:(g + 1) * 512].rearrange("(kc p) n -> p kc n", p=128),
                                                (KC, 512))
                            for nn in range(4):
                                oc = g * 4 + nn
                                ps, psb = P.psum()
                                for pc in range(KC):
                                    pe.add(lambda e, ps=ps, pc=pc, nn=nn, wv=wv: e.matmul(ps[:, :], wv[:, pc, nn * 128:(nn + 1) * 128], opt[:, pc, :],
                                                                                        start=(pc == 0), stop=(pc == KC - 1)),
                                           r=[wb, optB], w=[psb], inc=(pc == KC - 1))
                                dve.add(lambda e, ps=ps, oc=oc, tt=tt: e.scalar_tensor_tensor(
                                    out=xt[:, oc, tt * 512:(tt + 1) * 512], in0=ps[:, :], scalar=normc[:, 2, oc:oc + 1],
                                    in1=xt[:, oc, tt * 512:(tt + 1) * 512], op0=ALU.mult, op1=ALU.add),
                                    r=[psb, normcB, xBt[tt]], w=[xBt[tt]])
                            P.wrel(wi)
                    S.barrier()

        MIXERS[2] = rwkv

        for l in range(depth):
            kind = l % 3
            prep_norm(l)
            with P.sb([128, 4, 512], BF16, "sq") as sq, P.sb([128, 512], F32, "rstd") as rstd, \
                    P.sb([128, 2, 512], F32, "ntmp") as ntmp:
                scrB = [Buf(f"nscr{i}") for i in range(7)]
                if CFG["mixers"][kind]:
                    norm_mod(0, (sq, rstd, ntmp), scrB)
                S.barrier()
            if CFG["mixers"][kind]:
                MIXERS[kind](l)
            with P.sb([128, 4, 512], BF16, "sq") as sq, P.sb([128, 512], F32, "rstd") as rstd, \
                    P.sb([128, 2, 512], F32, "ntmp") as ntmp:
                scrB = [Buf(f"nscr{i}") for i in range(7)]
                norm_mod(1, (sq, rstd, ntmp), scrB)
                S.barrier()
            dump(f"h2_{which}{l}", ht[:, :, 0:T], [128, KC, T], BF16)
            dump(f"normc_{which}{l}", normc[:, :, :], [128, 6, KC])
            mlp(l)
            dump(f"x_{which}{l}", xt[:, :, 0:T], [128, KC, T])

        with P.sb([128, 4, 512], BF16, "sq") as sq, P.sb([128, 512], F32, "rstd") as rstd, \
                P.sb([128, 2, KC, 128], F32, "yfm") as yfm, P.sb([128, 2, D], F32, "ost") as ost:
            scrB = [Buf(f"fscr{i}") for i in range(7)]
            yB = [Buf("yfm0"), Buf("yfm1")]
            oB = [Buf(f"ost{which}0"), Buf(f"ost{which}1")]
            oi = 0
            for tt in range(NT):
                rms_rstd(tt, (sq, rstd), scrB)
                for tb in range(4):
                    sl = oi % 2
                    oi += 1
                    t0 = tt * 512 + tb * 128
                    for c in range(KC):
                        dve.add(lambda e, c=c, t0=t0, tb=tb, sl=sl: e.scalar_tensor_tensor(
                            out=yfm[:, sl, c, :], in0=xt[:, c, t0:t0 + 128], scalar=fngt[:, c:c + 1],
                            in1=rstd[:, tb * 128:(tb + 1) * 128], op0=ALU.mult, op1=ALU.mult),
                            r=[xBt[tt], constB, scrB[4]], w=[yB[sl]])
                    for cg in range(4):
                        ps, psb = P.psum()
                        for j in range(4):
                            c = cg * 4 + j
                            pe.add(lambda e, ps=ps, j=j, c=c, sl=sl: e.transpose(
                                ps[:, j * 128:(j + 1) * 128], yfm[:, sl, c, :], ident[:, :]),
                                r=[yB[sl], constB], w=[psb], inc=(j == 3))
                        if cg % 2 == 0:
                            act.add(lambda e, ps=ps, cg=cg, sl=sl: e.copy(out=ost[:, sl, cg * 512:(cg + 1) * 512], in_=ps[:, :]),
                                    r=[psb], w=[oB[sl]])
                        else:
                            dve.add(lambda e, ps=ps, cg=cg, sl=sl: e.tensor_copy(out=ost[:, sl, cg * 512:(cg + 1) * 512], in_=ps[:, :]),
                                    r=[psb], w=[oB[sl]])
                    sp.dma(lambda e, sl=sl, t0=t0: e.dma_start(out=yout[t0:t0 + 128, :], in_=ost[:, sl, :]),
                           dbuf=oB[sl], r=[oB[sl]])
            S.barrier()

    def locals_ns():
        return None

    MIXERS = {}

    for which in CFG["passes"]:
        with ExitStack() as pes:
            run_pass(which, pes)

    if not CFG["mixers"][2]:
        with P.sb([128, 4096], F32, "zst") as zst:
            zB = Buf("zst")
            dve.add(lambda e: e.memset(zst[:, :], 0.0), w=[zB])
            sp.dma(lambda e: e.dma_start(out=nst.rearrange("a b h v k -> (a b h) (v k)"), in_=zst[:, :]), dbuf=zB, r=[zB])
            S.barrier()

    S.barrier()
    for e in S.engs:
        pass
    es_ops = {e.name: e.ops for e in S.engs}
    return nc, es, P, es_ops


def finish_program(nc, es, ops):
    with nc.Block() as block:
        @block.tensor
        def _(e):
            replay(e, ops["pe"])

        @block.scalar
        def _(e):
            replay(e, ops["act"])

        @block.vector
        def _(e):
            replay(e, ops["dve"])

        @block.gpsimd
        def _(e):
            replay(e, ops["pool"])

        @block.sync
        def _(e):
            replay(e, ops["sp"])
    es.__exit__(None, None, None)
    return nc


_CACHE = {}


def get_program():
    key = (CFG["depth"], CFG["mixers"], CFG["passes"])
    if key not in _CACHE:
        nc0, es0, P0, _ = build_program(True, None)
        es0.__exit__(None, None, None)
        reqs = P0.reqs
        nc, es, P, ops = build_program(False, reqs)
        finish_program(nc, es, ops)
        _CACHE[key] = nc
    return _CACHE[key]


def _consts():
    f32 = np.float32
    t = np.arange(LS)
    row = (t // 64).astype(np.float64)
    col = (t % 64).astype(np.float64)
    inv = 10000.0 ** (-(np.arange(0, 64, 2, dtype=np.float64)) / 64.0)
    p = np.arange(128)
    quarter, i = p // 32, p % 32
    pos = np.where(quarter[:, None] < 2, row[None, :], col[None, :])
    ang = (pos.astype(f32) * inv.astype(f32)[i][:, None]).astype(f32)
    ropeC = np.cos(ang).astype(f32)
    ropeS = np.sin(ang).astype(f32)
    perm = np.zeros((128, 128), f32)
    for m in range(128):
        if (m // 32) % 2 == 0:
            perm[m + 32, m] = -1.0
        else:
            perm[m - 32, m] = 1.0
    kk = np.arange(128)[:, None]
    qq = np.arange(128)[None, :]
    mL = (kk >= qq).astype(f32)
    mU = (kk <= qq).astype(f32)
    masks = np.stack([np.tile(mL, (1, 4)), np.tile(mU, (1, 4))], axis=1)
    out = {"ropeC": ropeC, "ropeS": ropeS, "permS": perm, "masks": np.ascontiguousarray(masks)}
    pp = np.arange(128)
    out["bones"] = (pp[:, None] // 64 == pp[None, :] // 64).astype(f32)
    jj = (pp % 64)[:, None]
    ii = np.arange(64)[None, :]
    mf = np.concatenate([(ii > jj), (ii >= jj)], axis=1).astype(f32)
    mb = np.concatenate([(ii < jj), (ii <= jj)], axis=1).astype(f32)
    out["rwmask"] = np.ascontiguousarray(np.stack([mf, mb], axis=1))
    pr = pp[:, None]
    pc_ = pp[None, :]
    same_head = (pr // 64 == pc_ // 64)
    m16 = same_head & ((pr % 64) // 16 == (pc_ % 64) // 16)
    m32 = same_head & ((pr % 64) // 32 == (pc_ % 64) // 32) & ~m16
    m64 = same_head & ((pr % 64) // 32 != (pc_ % 64) // 32)
    out["hmask"] = np.ascontiguousarray(np.stack([m16, m32, m64], axis=1).astype(f32))
    oh = (pp[:, None] % 64 == np.arange(64)[None, :]).astype(f32)
    hmk = (pp[:, None] // 64 == pp[None, :] // 64).astype(f32)
    zz = np.zeros((128, 254), f32)
    zz[:, 126] = (pp < 64)
    zz[:, 127] = (pp >= 64)
    out["ohm"] = np.ascontiguousarray(np.concatenate([oh, hmk, zz], axis=1))
    deltas = np.linspace(math.log(1e-2) / 1.5, math.log(1e-2) / 0.3, D, dtype=f32)
    out["absd"] = np.ascontiguousarray(np.broadcast_to(np.abs(deltas)[None, :], (128, D)).astype(f32))
    for LL in (LS, LP):
        tt = np.linspace(0.0, 1.0, LL, dtype=f32)
        w = (2 * np.pi * np.arange(LL, dtype=f32) / LL).astype(f32)
        fr = np.linspace(1e-4, 15, 16, dtype=f32)
        zemb = np.concatenate([tt[:, None], np.cos(fr[None, :] * w[:, None]), -np.sin(fr[None, :] * w[:, None])], axis=-1)
        out[f"zemb{LL}"] = np.ascontiguousarray(zemb.T.astype(f32))
        out[f"tneg{LL}"] = np.ascontiguousarray((-tt).reshape(LL // 128, 128).T.astype(f32))
        m0 = np.ones(LL, f32)
        m0[0] = 0.0
        m0 = m0.reshape(LL // 128, 128).T
        out[f"m0_{LL}"] = np.ascontiguousarray(np.stack([m0, -m0], axis=-1).astype(f32))
        ph = np.pi * (2 * np.arange(LL, dtype=np.float64) + 1) / (4 * LL)
        out[f"phi{LL}"] = np.ascontiguousarray(np.stack([np.cos(ph), np.sin(ph)], axis=-1).reshape(LL // 128, 128, 2).transpose(1, 0, 2).astype(f32))
        ff = np.arange(LL, dtype=np.float64)
        th = np.pi * np.outer(2 * ff + 1, 2 * ff + 1) / (4 * LL)
        out[f"dftC{LL}"] = np.cos(th).astype(f32)
        out[f"dftS{LL}"] = np.sin(th).astype(f32)
    return out


def fm(v):
    v = np.asarray(v, dtype=np.float32)
    lead = v.shape[:-1]
    k = v.shape[-1] // 128
    v = v.reshape(lead + (k, 128))
    v = np.moveaxis(v, -1, 0)
    return np.ascontiguousarray(v)


def kernel(**inp):
    nc = get_program()
    f32 = np.float32
    shared = {
        "ada_w": np.ascontiguousarray(inp["ada_w"], dtype=f32),
        "ada_b": fm(inp["ada_b"]),
        "n1g": fm(inp["norm1_g"]),
        "n2g": fm(inp["norm2_g"]),
        "fng": fm(inp["final_norm_g"]),
        "mlp_w1": np.ascontiguousarray(inp["mlp_w1"], dtype=f32),
        "mlp_w2": np.ascontiguousarray(inp["mlp_w2"], dtype=f32),
        "ident": np.eye(128, dtype=f32),
        "attn_wqkv": np.ascontiguousarray(inp["attn_wqkv"], dtype=f32),
        "attn_wo": np.ascontiguousarray(inp["attn_wo"], dtype=f32),
        "sinkb": np.ascontiguousarray(np.broadcast_to(np.asarray(inp["attn_sink"], dtype=f32)[None], (128, 2, 16))),
    }
    shared.update(_consts())
    shared.update({
        "hy_w_in": np.ascontiguousarray(inp["hy_w_in"], dtype=f32),
        "hy_w_out": np.ascontiguousarray(inp["hy_w_out"], dtype=f32),
        "hy_cw": fm(inp["hy_conv_w"][0]),
        "hy_cb": fm(inp["hy_conv_b"][0]),
        "hy_fb": fm(inp["hy_bias"][0]),
        "hy_f_w1": np.ascontiguousarray(inp["hy_f_w1"][0], dtype=f32),
        "hy_f_w2": np.ascontiguousarray(inp["hy_f_w2"][0], dtype=f32),
        "hy_f_w3": np.ascontiguousarray(inp["hy_f_w3"][0], dtype=f32),
        "hy_bf": np.ascontiguousarray(np.stack([inp["hy_f_b1"][0], inp["hy_f_b2"][0], inp["hy_f_b3"][0], inp["hy_f_freq"][0]], axis=1), dtype=f32),
        "hy_f_wout": np.ascontiguousarray(inp["hy_f_wout"][0].reshape(64, 2, D), dtype=f32),
    })
    perm = np.zeros(D, dtype=np.int64)
    for pc in range(KC):
        q, hh = pc // 2, pc % 2
        for e_ in range(2):
            for v_ in range(64):
                perm[pc * 128 + e_ * 64 + v_] = ((2 * q + e_) * 2 + hh) * 64 + v_
    shared.update({
        "rw_wr": np.ascontiguousarray(inp["rw_wr"][0], dtype=f32), "rw_wk": np.ascontiguousarray(inp["rw_wk"][0], dtype=f32),
        "rw_wv": np.ascontiguousarray(inp["rw_wv"][0], dtype=f32),
        "rw_wo_perm": np.ascontiguousarray(inp["rw_wo"][0][perm, :], dtype=f32),
        "rw_w1": np.ascontiguousarray(inp["rw_w1"][0], dtype=f32), "rw_a1": np.ascontiguousarray(inp["rw_a1"][0], dtype=f32),
        "rw_g1": np.ascontiguousarray(inp["rw_g1"][0], dtype=f32),
        "rw_w2": np.ascontiguousarray(inp["rw_w2"][0], dtype=f32), "rw_a2": np.ascontiguousarray(inp["rw_a2"][0], dtype=f32),
        "rw_g2": np.ascontiguousarray(inp["rw_g2"][0], dtype=f32),
        "rw_mu": fm(inp["rw_mu"][0]), "rw_w0": fm(inp["rw_w0"][0]), "rw_a0": fm(inp["rw_a0"][0]),
        "rw_vec": fm(np.stack([inp["rw_k_k"][0], inp["rw_k_a"][0], inp["rw_r_k"][0].reshape(D)])),
        "rw_lnp": fm(np.stack([inp["rw_ln_w"][0][perm], inp["rw_ln_b"][0][perm]])),
        "rw_ln2": fm(np.stack([inp["rw_ln_w"][0], inp["rw_ln_b"][0]])),
        "rw_wo": np.ascontiguousarray(inp["rw_wo"][0], dtype=f32),
    })
    in_maps = []
    for i in range(NCORES):
        m = dict(shared)
        m["x_p"] = np.ascontiguousarray(inp["x_prompt"][2 * i:2 * i + 2].reshape(2 * LP, D), dtype=f32)
        m["x_s"] = np.ascontiguousarray(inp["x_sample"][i], dtype=f32)
        cv = np.stack([inp["c_ctx"], inp["c"][i]], axis=-1).astype(f32)
        m["cvec"] = np.ascontiguousarray(cv.reshape(KC, 128, 2).transpose(1, 0, 2))
        m["st"] = np.ascontiguousarray(inp["state_rwkv"][i, 0], dtype=f32)
        m["ck"] = np.ascontiguousarray(inp["cache_attn_k"][i].reshape(2, PAST, 512), dtype=f32)
        m["cvv"] = np.ascontiguousarray(inp["cache_attn_v"][i].reshape(2, PAST, 512), dtype=f32)
        in_maps.append(m)
    ncr = CFG["ncores"]
    res = run_bass_kernel_spmd(nc, in_maps[:ncr], core_ids=list(range(ncr)))
    R = list(res.results)
    while len(R) < NCORES:
        R.append(R[0])
    global LAST
    LAST = R
    y_prompt = np.stack([R[i]["y_p"].reshape(2, LP, D) for i in range(NCORES)]).reshape(16, LP, D)
    y_sample = np.stack([R[i]["y_s"] for i in range(NCORES)])
    nk = np.concatenate([R[i]["nk"].reshape(2, 2, LP, 4, 128) for i in range(NCORES)], axis=0)
    nv = np.concatenate([R[i]["nv"].reshape(2, 2, LP, 4, 128) for i in range(NCORES)], axis=0)
    nst = np.concatenate([R[i]["nst"].reshape(2, 1, 2, 32, 64, 64) for i in range(NCORES)], axis=0)
    return (y_prompt.astype(f32), y_sample.astype(f32), nk.astype(f32), nv.astype(f32), nst.astype(f32))
```

```python
import math
from contextlib import ExitStack
import numpy as np
import concourse.bass as bass
import concourse.mybir as mybir
from concourse.bass_utils import run_bass_kernel_spmd

F32 = mybir.dt.float32
BF16 = mybir.dt.bfloat16
ALU = mybir.AluOpType
AF = mybir.ActivationFunctionType
AX = mybir.AxisListType

D = 2048
KC = 16
DFF = 8192
DEPTH = 4
NCORES = 8
LS = 1024
LP = 256
PAST = 512
EPS = 1e-6
WSLOT = 8192
NSLOTS = 3

CFG = {"depth": 4, "mixers": (True, True, True), "passes": ("A", "B"), "ncores": 8}


class Buf:
    __slots__ = ("name", "w", "r", "dsem", "dcnt")

    def __init__(self, name=""):
        self.name = name
        self.w = {}
        self.r = {}
        self.dsem = None
        self.dcnt = 0


class Eng:
    EPOCH = 30000

    def __init__(self, sch, name, self_sync):
        self.sch = sch
        self.name = name
        self.ops = []
        self.cnt = 0
        self.sems = []
        self.waited = {}
        self.self_sync = self_sync
        self.pending = False

    def _ticket(self, n):
        e = (n - 1) // self.EPOCH
        while len(self.sems) <= e:
            self.sems.append(self.sch.new_sem(f"{self.name}{len(self.sems)}"))
        return (self.sems[e], (n - 1) % self.EPOCH + 1)

    def _wait_deps(self, r, w):
        deps = {}

        def merge(d):
            for k, (sem, val) in d.items():
                if k not in deps or deps[k][1] < val:
                    deps[k] = (sem, val)

        for b in r:
            merge(b.w)
        for b in w:
            merge(b.w)
            merge(b.r)
        own = {id(x) for x in self.sems}
        for k, (sem, val) in deps.items():
            if k in own and not self.self_sync:
                continue
            if self.waited.get(k, 0) >= val:
                continue
            self.waited[k] = val
            if not self.sch.dry:
                self.ops.append(("w", sem, val))

    def wait_ticket(self, sem, val):
        k = id(sem)
        if self.waited.get(k, 0) >= val:
            return
        self.waited[k] = val
        if not self.sch.dry:
            self.ops.append(("w", sem, val))

    def add(self, fn, r=(), w=(), inc=True):
        self._wait_deps(r, w)
        n = self.cnt + 1
        tk = self._ticket(n)
        key = id(tk[0])
        if inc:
            self.cnt = n
            self.pending = False
        else:
            self.pending = True
        if not self.sch.dry:
            self.ops.append(("i", fn, tk[0] if inc else None, 1))
        for b in w:
            b.w = {key: tk}
            b.r = {}
        for b in r:
            if b in w:
                continue
            if key not in b.r or b.r[key][1] < tk[1]:
                b.r[key] = tk

    def dma(self, fn, dbuf, r=(), w=()):
        self._wait_deps(r, w)
        if dbuf.dsem is None:
            dbuf.dsem = self.sch.new_sem("d" + dbuf.name)
            self.sch.dma_bufs.append(dbuf)
        dbuf.dcnt += 16
        tk = (dbuf.dsem, dbuf.dcnt)
        key = id(dbuf.dsem)
        if not self.sch.dry:
            self.ops.append(("i", fn, dbuf.dsem, 16))
        for b in w:
            b.w = {key: tk}
            b.r = {}
        for b in r:
            if b in w:
                continue
            if key not in b.r or b.r[key][1] < tk[1]:
                b.r[key] = tk


class Sched:
    def __init__(self, nc, es, dry):
        self.nc = nc
        self.es = es
        self.dry = dry
        self.nsem = 0
        self.dma_bufs = []
        self.pe = Eng(self, "pe", False)
        self.act = Eng(self, "act", True)
        self.dve = Eng(self, "dve", True)
        self.pool = Eng(self, "pool", True)
        self.sp = Eng(self, "sp", True)
        self.engs = [self.pe, self.act, self.dve, self.pool, self.sp]
        self.uid = 0

    def new_sem(self, name):
        self.nsem += 1
        if self.dry:
            return object()
        self.uid += 1
        return self.es.enter_context(self.nc.semaphore(f"s{self.uid}_{name}"))

    def barrier(self, include_dma=True):
        assert not self.pe.pending
        for e in self.engs:
            for f in self.engs:
                if f is e or f.cnt == 0:
                    continue
                sem, val = f._ticket(f.cnt)
                e.wait_ticket(sem, val)
            if include_dma:
                for b in self.dma_bufs:
                    if b.name.startswith("wslot"):
                        continue
                    e.wait_ticket(b.dsem, b.dcnt)


def replay(e, ops):
    for op in ops:
        if op[0] == "w":
            e.wait_ge(op[1], op[2])
        else:
            ins = op[1](e)
            if op[2] is not None:
                ins.then_inc(op[2], op[3])


class Prog:
    def __init__(self, nc, es, dry, reqs):
        self.nc = nc
        self.es = es
        self.S = Sched(nc, es, dry)
        self.dry = dry
        self.reqs = reqs if reqs is not None else []
        self.have_reqs = reqs is not None
        self.next_use = 0
        self.next_issue = 0
        self.tn = 0
        self.ps_i = 0
        self.done = set()

    def sb(self, shape, dt, name="t"):
        self.tn += 1
        return self.nc.sbuf_tensor(f"{name}_{self.tn}_{int(self.dry)}", list(shape), dt)

    def sbp(self, shape, dt, name="t"):
        return self.es.enter_context(self.sb(shape, dt, name))

    def psum(self):
        i = self.ps_i
        self.ps_i = (self.ps_i + 1) % 5
        return self.ps_t[i], self.ps_b[i]

    def psf(self, i):
        return self.ps_t[i], self.ps_b[i]

    def wget(self, src_fn, shape):
        a, b = shape
        assert a * b <= WSLOT
        if not self.have_reqs:
            self.reqs.append((src_fn, shape))
            idx = len(self.reqs) - 1
        else:
            idx = self.next_use
            self.next_use += 1
            self._try_issue()
            assert self.next_issue > idx, "weight slot deadlock: too many live slots"
        slot = idx % NSLOTS
        view = self.wslots[slot][:, 0:a * b].rearrange("p (a b) -> p a b", a=a)
        return view, self.wbufs[slot], idx

    def wrel(self, idx):
        if not self.have_reqs:
            return
        self.done.add(idx)
        self._try_issue()

    def _try_issue(self):
        while self.next_issue < len(self.reqs) and self.next_issue < self.next_use + NSLOTS and \
                (self.next_issue < NSLOTS or (self.next_issue - NSLOTS) in self.done):
            self._wissue(self.next_issue)
            self.next_issue += 1

    def _wissue(self, i):
        src_fn, (a, b) = self.reqs[i]
        slot = i % NSLOTS
        view = self.wslots[slot][:, 0:a * b].rearrange("p (a b) -> p a b", a=a)
        src = src_fn(self.dt)
        self.S.pool.dma(lambda e, o=view, s=src: e.dma_start(out=o, in_=s),
                        dbuf=self.wbufs[slot], w=[self.wbufs[slot]])


def bcast(ap, shape):
    return ap.to_broadcast(list(shape))


def build_program(dry, reqs):
    nc = bass.Bass("TRN2", target_bir_lowering=False)
    es = ExitStack()
    P = Prog(nc, es, dry, reqs)
    S = P.S
    pe, act, dve, pool, sp = S.pe, S.act, S.dve, S.pool, S.sp
    depth = CFG["depth"]

    P.dt = {}

    def din(name, shape, dt=F32):
        P.dt[name] = nc.dram_tensor(name, list(shape), dt, kind="ExternalInput").ap()
        return P.dt[name]

    def dout(name, shape, dt=F32):
        return nc.dram_tensor(name, list(shape), dt, kind="ExternalOutput").ap()

    x_p = din("x_p", [2 * LP, D])
    x_s = din("x_s", [LS, D])
    cvec = din("cvec", [128, KC, 2])
    ada_w = din("ada_w", [DEPTH, D, 6 * D])
    ada_b = din("ada_b", [128, DEPTH, 96])
    n1g = din("n1g", [128, DEPTH, KC])
    n2g = din("n2g", [128, DEPTH, KC])
    fng = din("fng", [128, KC])
    mlp_w1 = din("mlp_w1", [DEPTH, D, DFF])
    mlp_w2 = din("mlp_w2", [DEPTH, DFF, D])
    ident_d = din("ident", [128, 128])
    din("attn_wqkv", [2, D, 3072])
    din("attn_wo", [2, D, D])
    sink_d = din("sinkb", [128, 2, 16])
    ck_d = din("ck", [2, PAST, 512])
    cvv_d = din("cvv", [2, PAST, 512])
    ropeC_d = din("ropeC", [128, LS])
    ropeS_d = din("ropeS", [128, LS])
    perm_d = din("permS", [128, 128])
    mask_d = din("masks", [128, 2, 512])
    din("hy_w_in", [1, D, 3 * D])
    din("hy_w_out", [1, D, D])
    hycw_d = din("hy_cw", [128, 3, 48])
    hycb_d = din("hy_cb", [128, 48])
    hyfb_d = din("hy_fb", [128, KC])
    hyw1_d = din("hy_f_w1", [33, 64])
    hyw2_d = din("hy_f_w2", [64, 64])
    hyw3_d = din("hy_f_w3", [64, 64])
    hybf_d = din("hy_bf", [64, 4])
    hywo_d = din("hy_f_wout", [64, 2, D])
    absd_d = din("absd", [128, D])
    for nm in ("rw_wr", "rw_wk", "rw_wv", "rw_wo_perm"):
        din(nm, [D, D])
    din("rw_w1", [2, D, 96])
    din("rw_a1", [2, D, 96])
    din("rw_g1", [D, 256])
    rw_w2_d = din("rw_w2", [2, 96, D])
    rw_a2_d = din("rw_a2", [2, 96, D])
    rw_g2_d = din("rw_g2", [256, D])
    rwmu_d = din("rw_mu", [128, 6, KC])
    rww0_d = din("rw_w0", [128, 2, KC])
    rwa0_d = din("rw_a0", [128, 2, KC])
    rwvec_d = din("rw_vec", [128, 3, KC])
    rwln_d = din("rw_lnp", [128, 2, KC])
    rwln2_d = din("rw_ln2", [128, 2, KC])
    din("rw_wo", [D, D])
    st_d = din("st", [2, 32, 64, 64])
    bones_d = din("bones", [128, 128])
    ohm_d = din("ohm", [128, 64 + 128 + 254])

    def scr(name, shape, dt=F32):
        return nc.dram_tensor("scr_" + name, list(shape), dt).ap()
    SC = {k: scr(k, [KC, 128, LS]) for k in ("r", "k", "v", "a", "w0", "w1", "b0", "b1", "kd0", "kd1", "bon", "g")}
    SC["vtm"] = scr("vtm", [LS, D], BF16)
    for k_ in ("lw0", "lw1", "yf0", "yf1"):
        SC[k_] = scr(k_, [KC, 128, LS])
    rwmask_d = din("rwmask", [128, 2, 128])
    hmask_d = din("hmask", [128, 3, 128])
    SC["y0"] = scr("y0", [LS // 64, 128, 1024])
    SC["y1"] = scr("y1", [LS // 64, 128, 1024])
    SCB = {k: Buf("scr_" + k) for k in SC}
    HC = {}
    for LL in (LS, LP):
        HC[LL] = dict(zemb=din(f"zemb{LL}", [33, LL]), tneg=din(f"tneg{LL}", [128, LL // 128]),
                      m0=din(f"m0_{LL}", [128, LL // 128, 2]), phi=din(f"phi{LL}", [128, LL // 128, 2]))
        din(f"dftC{LL}", [LL, LL])
        din(f"dftS{LL}", [LL, LL])

    y_p = dout("y_p", [2 * LP, D])
    y_s = dout("y_s", [LS, D])
    nk = dout("nk", [2, 2, LP, 512])
    nv = dout("nv", [2, 2, LP, 512])
    nst = dout("nst", [2, 2, 32, 64, 64])

    es.__enter__()
    dumpB = Buf("dump")

    def dump(name, ap, shape, dtype=F32):
        if name not in CFG.get("dump", ()):
            return
        S.barrier()
        o = dout("dbg_" + name, shape, dtype)
        sp.dma(lambda e: e.dma_start(out=o, in_=ap), dbuf=dumpB, w=[dumpB])
        S.barrier()
    xB = Buf("x")
    P.wslots = [P.sbp([128, WSLOT], BF16, f"wslot{i}") for i in range(NSLOTS)]
    P.wbufs = [Buf(f"wslot{i}") for i in range(NSLOTS)]
    ident = P.sbp([128, 128], F32, "ident")
    identb = P.sbp([128, 128], BF16, "identb")
    onesb = P.sbp([128, 128], BF16, "onesb")
    mods = P.sbp([128, DEPTH, 96, 2], F32, "mods")
    n1gt = P.sbp([128, DEPTH, KC], F32, "n1g")
    n2gt = P.sbp([128, DEPTH, KC], F32, "n2g")
    fngt = P.sbp([128, KC], F32, "fng")
    cv_t = P.sbp([128, KC, 2], F32, "cv")
    scond = P.sbp([128, KC, 2], BF16, "scond")
    normc = P.sbp([128, 6, KC], F32, "normc")
    permb = P.sbp([128, 128], BF16, "permb")
    maskb = P.sbp([128, 2, 512], BF16, "maskb")
    sinke = P.sbp([128, 2, 16], F32, "sinke")
    constB = Buf("const")
    modsB = Buf("mods")
    normcB = Buf("normc")
    NTmax = LS // 512
    xBt = [Buf(f"x{t}") for t in range(NTmax)]
    hBt = [Buf(f"h{t}") for t in range(NTmax)]

    P.ps_pair = [P.es.enter_context(nc.psum_tensor(f"pp{i}_{int(dry)}", [128, 1024], F32)) for i in range(4)]
    P.ps_t = [P.ps_pair[i // 2][:, (i % 2) * 512:(i % 2 + 1) * 512] for i in range(8)]
    P.ps_b = [Buf(f"ps{i}") for i in range(8)]

    sp.dma(lambda e: e.dma_start(out=ident[:, :], in_=ident_d[:, :]), dbuf=constB, w=[constB])
    sp.dma(lambda e: e.dma_start(out=n1gt[:, :, :], in_=n1g[:, :, :]), dbuf=constB, w=[constB])
    sp.dma(lambda e: e.dma_start(out=n2gt[:, :, :], in_=n2g[:, :, :]), dbuf=constB, w=[constB])
    sp.dma(lambda e: e.dma_start(out=fngt[:, :], in_=fng[:, :]), dbuf=constB, w=[constB])
    sp.dma(lambda e: e.dma_start(out=cv_t[:, :, :], in_=cvec[:, :, :]), dbuf=constB, w=[constB])
    c2 = Buf("c2")
    with P.sb([128, 128], F32, "permf") as permf, P.sb([128, 2, 512], F32, "maskf") as maskf:
        tmpB = Buf("tmpc")
        sp.dma(lambda e: e.dma_start(out=permf[:, :], in_=perm_d[:, :]), dbuf=tmpB, w=[tmpB])
        sp.dma(lambda e: e.dma_start(out=maskf[:, :, :], in_=mask_d[:, :, :]), dbuf=tmpB, w=[tmpB])
        sp.dma(lambda e: e.dma_start(out=sinke[:, :, :], in_=sink_d[:, :, :]), dbuf=tmpB, w=[tmpB])
        dve.add(lambda e: e.tensor_copy(out=permb[:, :], in_=permf[:, :]), r=[tmpB], w=[c2])
        dve.add(lambda e: e.tensor_copy(out=maskb[:, :, :], in_=maskf[:, :, :]), r=[tmpB], w=[c2])
        act.add(lambda e: e.activation(out=sinke[:, :, :], in_=sinke[:, :, :], func=AF.Exp), r=[tmpB], w=[tmpB])
        S.barrier()
    dve.add(lambda e: e.tensor_copy(out=identb[:, :], in_=ident[:, :]), r=[constB], w=[c2])
    dve.add(lambda e: e.memset(onesb[:, :], 1.0), w=[c2])
    act.add(lambda e: e.activation(out=scond[:, :, :], in_=cv_t[:, :, :], func=AF.Silu), r=[constB], w=[c2])

    adab_cm = P.sb([128, DEPTH, 96], F32, "adab")
    adab = adab_cm.__enter__()
    sp.dma(lambda e: e.dma_start(out=adab[:, :, :], in_=ada_b[:, :, :]), dbuf=constB, w=[constB])
    for l in range(depth):
        for g in range(24):
            wv, wb, wi = P.wget(lambda dt, l=l, g=g: dt["ada_w"][l, :, g * 512:(g + 1) * 512].rearrange("(kc p) n -> p kc n", p=128),
                            (KC, 512))
            ps, psb = P.psum()
            for nn in range(4):
                for kc in range(KC):
                    pe.add(lambda e, ps=ps, wv=wv, nn=nn, kc=kc: e.matmul(
                        ps[:, nn * 2:(nn + 1) * 2], wv[:, kc, nn * 128:(nn + 1) * 128], scond[:, kc, :],
                        start=(kc == 0), stop=(kc == KC - 1)),
                        r=[wb, c2], w=[psb], inc=(kc == KC - 1 and nn == 3))
            dve.add(lambda e, ps=ps, l=l, g=g: e.tensor_tensor(
                out=mods[:, l, g * 4:(g + 1) * 4, :],
                in0=ps[:, 0:8].rearrange("p (a b) -> p a b", a=4),
                in1=bcast(adab[:, l, g * 4:(g + 1) * 4].unsqueeze(2), [128, 4, 2]), op=ALU.add),
                r=[psb, constB], w=[modsB])
            P.wrel(wi)

    S.barrier()
    adab_cm.__exit__(None, None, None)
    dump("mods", mods[:, 0:depth, :, :], [128, depth, 96, 2])
    def run_pass(which, pes):
        isA = which == "A"
        T = LS if isA else 2 * LP
        NT = T // 512
        wsel = 1 if isA else 0
        xin = x_s if isA else x_p
        yout = y_s if isA else y_p
        xt = pes.enter_context(P.sb([128, KC, T], F32, "x"))
        ht = pes.enter_context(P.sb([128, KC, T], BF16, "h"))

        def xs(c0, c1, t0, t1):
            return xt[:, c0:c1, t0:t1]

        with P.sb([128, 2, D], F32, "stage") as stage:
            stB = [Buf(f"stage{which}0"), Buf(f"stage{which}1")]
            for tb in range(T // 128):
                sl = tb % 2
                sp.dma(lambda e, sl=sl, tb=tb: e.dma_start(out=stage[:, sl, :], in_=xin[tb * 128:(tb + 1) * 128, :]),
                       dbuf=stB[sl], w=[stB[sl]])
                for cg in range(4):
                    ps, psb = P.psum()
                    for j in range(4):
                        c = cg * 4 + j
                        pe.add(lambda e, ps=ps, j=j, c=c, sl=sl: e.transpose(
                            ps[:, j * 128:(j + 1) * 128], stage[:, sl, c * 128:(c + 1) * 128], ident[:, :]),
                            r=[stB[sl], constB], w=[psb], inc=(j == 3))
                    eng = act if cg % 2 == 0 else dve
                    if eng is act:
                        act.add(lambda e, ps=ps, cg=cg, tb=tb: e.copy(
                            out=xt[:, cg * 4:(cg + 1) * 4, tb * 128:(tb + 1) * 128],
                            in_=ps[:, :].rearrange("p (a b) -> p a b", a=4)),
                            r=[psb], w=[xBt[tb // 4]])
                    else:
                        dve.add(lambda e, ps=ps, cg=cg, tb=tb: e.tensor_copy(
                            out=xt[:, cg * 4:(cg + 1) * 4, tb * 128:(tb + 1) * 128],
                            in_=ps[:, :].rearrange("p (a b) -> p a b", a=4)),
                            r=[psb], w=[xBt[tb // 4]])
            S.barrier()

        def prep_norm(l):
            def md(j):
                return mods[:, l, j * 16:(j + 1) * 16, wsel]
            for half, gt in ((0, n1gt), (1, n2gt)):
                dve.add(lambda e, half=half, gt=gt: e.scalar_tensor_tensor(
                    out=normc[:, 3 * half + 0, :], in0=md(3 * half + 1), scalar=1.0, in1=gt[:, l, :],
                    op0=ALU.add, op1=ALU.mult), r=[modsB, constB], w=[normcB])
                dve.add(lambda e, half=half: e.tensor_copy(out=normc[:, 3 * half + 1, :], in_=md(3 * half + 0)),
                        r=[modsB], w=[normcB])
                dve.add(lambda e, half=half: e.tensor_copy(out=normc[:, 3 * half + 2, :], in_=md(3 * half + 2)),
                        r=[modsB], w=[normcB])

        def rms_rstd(tt, scr, scrB):
            sq, rstd = scr
            ps, psb = P.psum()
            for c in range(KC):
                k = c % 4
                act.add(lambda e, c=c, k=k, tt=tt: e.activation(
                    out=sq[:, k, :], in_=xt[:, c, tt * 512:(tt + 1) * 512], func=AF.Square),
                    r=[xBt[tt]], w=[scrB[k]])
                pe.add(lambda e, ps=ps, c=c, k=k: e.matmul(ps[:, :], onesb[:, :], sq[:, k, :],
                                                           start=(c == 0), stop=(c == KC - 1)),
                       r=[scrB[k], c2], w=[psb], inc=True)
            dve.add(lambda e, ps=ps: e.tensor_scalar(out=rstd[:, :], in0=ps[:, :], scalar1=1.0 / D, scalar2=EPS,
                                                     op0=ALU.mult, op1=ALU.add), r=[psb], w=[scrB[4]])
            act.add(lambda e: e.activation(out=rstd[:, :], in_=rstd[:, :], func=AF.Sqrt), r=[scrB[4]], w=[scrB[4]])
            dve.add(lambda e: e.reciprocal(out=rstd[:, :], in_=rstd[:, :]), r=[scrB[4]], w=[scrB[4]])
            return rstd

        def norm_mod(half, scr, scrB):
            sq, rstd, tmp = scr
            for tt in range(NT):
                rms_rstd(tt, (sq, rstd), scrB)
                for c in range(KC):
                    k = c % 2
                    dve.add(lambda e, c=c, k=k, tt=tt: e.scalar_tensor_tensor(
                        out=tmp[:, k, :], in0=xt[:, c, tt * 512:(tt + 1) * 512], scalar=normc[:, 3 * half, c:c + 1],
                        in1=rstd[:, :], op0=ALU.mult, op1=ALU.mult),
                        r=[xBt[tt], normcB, scrB[4]], w=[scrB[5 + k]])
                    act.add(lambda e, c=c, k=k, tt=tt: e.activation(
                        out=ht[:, c, tt * 512:(tt + 1) * 512], in_=tmp[:, k, :], func=AF.Identity,
                        bias=normc[:, 3 * half + 1, c:c + 1], scale=1.0),
                        r=[scrB[5 + k], normcB], w=[hBt[tt]])

        def mlp(l):
            with P.sb([128, 2, 4, T], BF16, "hid") as hid, P.sb([128, 2, 512], F32, "sqt") as sqt:
                hidB = [[Buf(f"hid{a}{t}") for t in range(NT)] for a in range(2)]
                sqB = [Buf("sqt0"), Buf("sqt1")]
                si = 0
                for g in range(DFF // 512):
                    hb = g % 2
                    w1v, w1b, w1i = P.wget(lambda dt, g=g: dt["mlp_w1"][l, :, g * 512:(g + 1) * 512].rearrange(
                        "(kc p) n -> p kc n", p=128), (KC, 512))
                    w2v, w2b, w2i = P.wget(lambda dt, g=g: dt["mlp_w2"][l, g * 512:(g + 1) * 512, :].rearrange(
                        "(kc p) n -> p kc n", p=128), (4, D))
                    for nn in range(4):
                        for tt in range(NT):
                            ps, psb = P.psum()
                            for kc in range(KC):
                                pe.add(lambda e, ps=ps, nn=nn, kc=kc, tt=tt, w1v=w1v: e.matmul(
                                    ps[:, :], w1v[:, kc, nn * 128:(nn + 1) * 128], ht[:, kc, tt * 512:(tt + 1) * 512],
                                    start=(kc == 0), stop=(kc == KC - 1)),
                                    r=[w1b, hBt[tt]], w=[psb], inc=(kc == KC - 1))
                            k = si % 2
                            si += 1
                            act.add(lambda e, ps=ps, k=k: e.activation(out=sqt[:, k, :], in_=ps[:, :], func=AF.Square),
                                    r=[psb], w=[sqB[k]])
                            dve.add(lambda e, ps=ps, k=k, hb=hb, nn=nn, tt=tt: e.scalar_tensor_tensor(
                                out=hid[:, hb, nn, tt * 512:(tt + 1) * 512], in0=ps[:, :], scalar=0.0, in1=sqt[:, k, :],
                                op0=ALU.is_gt, op1=ALU.mult), r=[psb, sqB[k]], w=[hidB[hb][tt]])
                    P.wrel(w1i)
                    for oc in range(KC):
                        for tt in range(NT):
                            ps, psb = P.psum()
                            for kc in range(4):
                                pe.add(lambda e, ps=ps, oc=oc, kc=kc, tt=tt, w2v=w2v, hb=hb: e.matmul(
                                    ps[:, :], w2v[:, kc, oc * 128:(oc + 1) * 128], hid[:, hb, kc, tt * 512:(tt + 1) * 512],
                                    start=(kc == 0), stop=(kc == 3)),
                                    r=[w2b, hidB[hb][tt]], w=[psb], inc=(kc == 3))
                            dve.add(lambda e, ps=ps, oc=oc, tt=tt: e.scalar_tensor_tensor(
                                out=xt[:, oc, tt * 512:(tt + 1) * 512], in0=ps[:, :], scalar=normc[:, 5, oc:oc + 1],
                                in1=xt[:, oc, tt * 512:(tt + 1) * 512], op0=ALU.mult, op1=ALU.add),
                                r=[psb, normcB, xBt[tt]], w=[xBt[tt]])
                    P.wrel(w2i)
                S.barrier()


        def attn(l):
            ja = l // 3
            NB = T // 128
            SCALE = 128.0 ** -0.5
            ctxs = []
            ctxs.append(P.sb([128, T], BF16, "kT")); ctxs.append(P.sb([128, NB, 128], BF16, "vtm"))
            ctxs.append(P.sb([128, 4, T], BF16, "qg")); ctxs.append(P.sb([128, 4, T], BF16, "og"))
            ctxs.append(P.sb([128, 2, 512], BF16, "ebuf")); ctxs.append(P.sb([128, 4, 128], F32, "den"))
            ctxs.append(P.sb([128, 512], BF16, "rawb")); ctxs.append(P.sb([128, 2, 512], F32, "rtmp"))
            ctxs.append(P.sb([128, 4, 512] if not isA else [128, 1, 4], F32, "kst"))
            ctxs.append(P.sb([128, 4, 512] if not isA else [128, 1, 4], F32, "vst"))
            if isA:
                ctxs.append(P.sb([128, LS], F32, "ropeC")); ctxs.append(P.sb([128, LS], F32, "ropeS"))
                ctxs.append(P.sb([128, 4, 128], F32, "ckst")); ctxs.append(P.sb([128, PAST], BF16, "ctxk"))
                ctxs.append(P.sb([128, 4, 128], BF16, "ctxv"))
            with ExitStack() as les:
                tl = [les.enter_context(c) for c in ctxs]
                kT, vtm, qg, og, ebuf, den, rawb, rtmp, kst, vst = tl[:10]
                kTB, vtmB, qgB, ogB, denB, rawB = Buf("kT"), Buf("vtm"), Buf("qg"), Buf("og"), Buf("den"), Buf("rawb")
                eB = [Buf("e0"), Buf("e1")]
                rtB = [Buf("rt0"), Buf("rt1")]
                kstB, vstB = Buf(f"kst{which}{l}"), Buf(f"vst{which}{l}")
                if isA:
                    ropeC, ropeS, ckst, ctxk, ctxv = tl[10:]
                    ropeB, ckstB, ctxkB, ctxvB = Buf(f"rope{l}"), Buf(f"ckst{l}"), Buf("ctxk"), Buf(f"ctxv{l}")
                    sp.dma(lambda e: e.dma_start(out=ropeC[:, :], in_=ropeC_d[:, :]), dbuf=ropeB, w=[ropeB])
                    sp.dma(lambda e: e.dma_start(out=ropeS[:, :], in_=ropeS_d[:, :]), dbuf=ropeB, w=[ropeB])

                def evac_rope(ps, psb, out_ap, outB, tt):
                    if not isA:
                        act.add(lambda e: e.copy(out=out_ap, in_=ps[:, :]), r=[psb], w=[outB])
                        return
                    act.add(lambda e: e.copy(out=rawb[:, :], in_=ps[:, :]), r=[psb], w=[rawB])
                    pp, ppb = P.psum()
                    pe.add(lambda e: e.matmul(pp[:, :], permb[:, :], rawb[:, :], start=True, stop=True),
                           r=[c2, rawB], w=[ppb])
                    dve.add(lambda e: e.tensor_tensor(out=rtmp[:, 0, :], in0=pp[:, :], in1=ropeS[:, tt * 512:(tt + 1) * 512],
                                                      op=ALU.mult), r=[ppb, ropeB], w=[rtB[0]])
                    dve.add(lambda e: e.tensor_tensor(out=rtmp[:, 1, :], in0=ps[:, :], in1=ropeC[:, tt * 512:(tt + 1) * 512],
                                                      op=ALU.mult), r=[psb, ropeB, rawB], w=[rtB[1]])
                    dve.add(lambda e: e.tensor_tensor(out=out_ap, in0=rtmp[:, 0, :], in1=rtmp[:, 1, :], op=ALU.add),
                            r=[rtB[0], rtB[1]], w=[outB])

                for g in range(4):
                    wk, wkb, wki = P.wget(lambda dt, g=g: dt["attn_wqkv"][ja, :, 2048 + g * 128:2048 + (g + 1) * 128].rearrange(
                        "(kc p) n -> p kc n", p=128), (KC, 128))
                    wv, wvb, wvi = P.wget(lambda dt, g=g: dt["attn_wqkv"][ja, :, 2560 + g * 128:2560 + (g + 1) * 128].rearrange(
                        "(kc p) n -> p kc n", p=128), (KC, 128))
                    for tt in range(NT):
                        ps, psb = P.psum()
                        for kc in range(KC):
                            pe.add(lambda e, ps=ps, kc=kc, tt=tt, wk=wk: e.matmul(
                                ps[:, :], wk[:, kc, :], ht[:, kc, tt * 512:(tt + 1) * 512], start=(kc == 0), stop=(kc == KC - 1)),
                                r=[wkb, hBt[tt]], w=[psb], inc=(kc == KC - 1))
                        evac_rope(ps, psb, kT[:, tt * 512:(tt + 1) * 512], kTB, tt)
                    for tq in range(NB // 4):
                        for src_w, src_b, dst, dstB, stg, stgB, dram in ((wk, wkb, None, None, kst, kstB, nk),
                                                                       (wv, wvb, vtm, vtmB, vst, vstB, nv)):
                            if dst is None and isA:
                                continue
                            ps, psb = P.psum()
                            for j in range(4):
                                tb = tq * 4 + j
                                for kc in range(KC):
                                    pe.add(lambda e, ps=ps, kc=kc, tb=tb, j=j, src_w=src_w: e.matmul(
                                        ps[:, j * 128:(j + 1) * 128], ht[:, kc, tb * 128:(tb + 1) * 128], src_w[:, kc, :],
                                        start=(kc == 0), stop=(kc == KC - 1)),
                                        r=[src_b, hBt[tb // 4]], w=[psb], inc=(kc == KC - 1 and j == 3))
                            if not isA:
                                dve.add(lambda e, ps=ps, stg=stg, g=g: e.tensor_copy(
                                    out=stg[:, :, g * 128:(g + 1) * 128], in_=ps[:, :].rearrange("p (a b) -> p a b", a=4)),
                                    r=[psb], w=[stgB])
                                if dst is not None:
                                    act.add(lambda e, stg=stg, tq=tq, dst=dst, g=g: e.copy(
                                        out=dst[:, tq * 4:(tq + 1) * 4, :], in_=stg[:, :, g * 128:(g + 1) * 128]),
                                        r=[stgB], w=[dstB])
                            elif dst is not None:
                                act.add(lambda e, ps=ps, tq=tq, dst=dst: e.copy(
                                    out=dst[:, tq * 4:(tq + 1) * 4, :], in_=ps[:, :].rearrange("p (a b) -> p a b", a=4)),
                                    r=[psb], w=[dstB])
                    P.wrel(wki)
                    P.wrel(wvi)
                    if CFG.get("attn_stop", 9) <= 1:
                        continue
                    wq, wqb, wqi = P.wget(lambda dt, g=g: dt["attn_wqkv"][ja, :, g * 512:(g + 1) * 512].rearrange(
                        "(kc p) n -> p kc n", p=128), (KC, 512))
                    for nn in range(4):
                        for tt in range(NT):
                            ps, psb = P.psum()
                            for kc in range(KC):
                                pe.add(lambda e, ps=ps, kc=kc, tt=tt, nn=nn, wq=wq: e.matmul(
                                    ps[:, :], wq[:, kc, nn * 128:(nn + 1) * 128], ht[:, kc, tt * 512:(tt + 1) * 512],
                                    start=(kc == 0), stop=(kc == KC - 1)),
                                    r=[wqb, hBt[tt]], w=[psb], inc=(kc == KC - 1))
                            evac_rope(ps, psb, qg[:, nn, tt * 512:(tt + 1) * 512], qgB, tt)
                    P.wrel(wqi)
                    if CFG.get("attn_stop", 9) <= 2:
                        continue
                    if isA:
                        sp.dma(lambda e, g=g: e.dma_start(out=ckst[:, :, :], in_=ck_d[ja, :, g * 128:(g + 1) * 128].rearrange(
                            "(sb p) d -> p sb d", p=128)), dbuf=ckstB, w=[ckstB])
                        ps, psb = P.psum()
                        for sbk in range(4):
                            pe.add(lambda e, ps=ps, sbk=sbk: e.transpose(ps[:, sbk * 128:(sbk + 1) * 128], ckst[:, sbk, :], ident[:, :]),
                                   r=[ckstB, constB], w=[psb], inc=(sbk == 3))
                        act.add(lambda e, ps=ps: e.copy(out=ctxk[:, :], in_=ps[:, :]), r=[psb], w=[ctxkB])
                        pool.dma(lambda e, g=g: e.dma_start(out=ctxv[:, :, :], in_=cvv_d[ja, :, g * 128:(g + 1) * 128].rearrange(
                            "(sb p) d -> p sb d", p=128)), dbuf=ctxvB, w=[ctxvB])
                    qblocks = []
                    if isA:
                        for jq in range(NB):
                            kb = []
                            for jj in (jq - 1, jq, jq + 1):
                                if 0 <= jj < NB:
                                    m = None if jj == jq else (0 if jj < jq else 1)
                                    kb.append((kT[:, jj * 128:(jj + 1) * 128], kTB, vtm[:, jj, :], vtmB, m))
                            for sbk in range(4):
                                kb.append((ctxk[:, sbk * 128:(sbk + 1) * 128], ctxkB, ctxv[:, sbk, :], ctxvB, None))
                            qblocks.append((jq, kb))
                    else:
                        for sq_ in range(2):
                            for qb in range(2):
                                kb = [(kT[:, jj * 128:(jj + 1) * 128], kTB, vtm[:, jj, :], vtmB, None)
                                      for jj in (sq_ * 2, sq_ * 2 + 1)]
                                qblocks.append((sq_ * 2 + qb, kb))
                    ei = 0
                    for tqb, kb in qblocks:
                        q0 = tqb * 128
                        ops_, opb = P.psf(5)
                        dps, dpb = P.psf(6)
                        for i, (kap, kB_, vap, vB_, m) in enumerate(kb):
                            first, last = (i == 0), (i == len(kb) - 1)
                            sps, spb = P.psum()
                            pe.add(lambda e, sps=sps, kap=kap, q0=q0: e.matmul(sps[:, :], kap, qg[:, :, q0:q0 + 128],
                                                                             start=True, stop=True),
                                   r=[kB_, qgB], w=[spb])
                            k = ei % 2
                            ei += 1
                            act.add(lambda e, sps=sps, k=k: e.activation(out=ebuf[:, k, :], in_=sps[:, :], func=AF.Exp, scale=SCALE),
                                    r=[spb], w=[eB[k]])
                            if m is not None:
                                dve.add(lambda e, k=k, m=m: e.tensor_tensor(out=ebuf[:, k, :], in0=ebuf[:, k, :], in1=maskb[:, m, :],
                                                                          op=ALU.mult), r=[eB[k], c2], w=[eB[k]])
                            pe.add(lambda e, ops_=ops_, vap=vap, k=k, first=first, last=last: e.matmul(
                                ops_[:, :], vap, ebuf[:, k, :], start=first, stop=last), r=[vB_, eB[k]], w=[opb], inc=False)
                            pe.add(lambda e, dps=dps, k=k, first=first, last=last: e.matmul(
                                dps[:, :], onesb[:, :], ebuf[:, k, :], start=first, stop=last), r=[c2, eB[k]], w=[dpb], inc=True)
                        dve.add(lambda e, dps=dps, g=g: e.tensor_tensor(
                            out=den[:, :, :], in0=dps[:, :].rearrange("p (a b) -> p a b", a=4),
                            in1=bcast(sinke[:, ja, g * 4:(g + 1) * 4].unsqueeze(2), [128, 4, 128]), op=ALU.add),
                            r=[dpb, tmpB], w=[denB])
                        dve.add(lambda e: e.reciprocal(out=den[:, :, :], in_=den[:, :, :]), r=[denB], w=[denB])
                        dve.add(lambda e, ops_=ops_, q0=q0: e.tensor_tensor(
                            out=og[:, :, q0:q0 + 128], in0=ops_[:, :].rearrange("p (a b) -> p a b", a=4), in1=den[:, :, :],
                            op=ALU.mult), r=[opb, denB], w=[ogB])
                    if CFG.get("attn_stop", 9) <= 3:
                        continue
                    wo, wob, woi = P.wget(lambda dt, g=g: dt["attn_wo"][ja, g * 512:(g + 1) * 512, :].rearrange(
                        "(kc p) n -> p kc n", p=128), (4, D))
                    for oc in range(KC):
                        for tt in range(NT):
                            ps, psb = P.psum()
                            for kc in range(4):
                                pe.add(lambda e, ps=ps, oc=oc, kc=kc, tt=tt, wo=wo: e.matmul(
                                    ps[:, :], wo[:, kc, oc * 128:(oc + 1) * 128], og[:, kc, tt * 512:(tt + 1) * 512],
                                    start=(kc == 0), stop=(kc == 3)), r=[wob, ogB], w=[psb], inc=(kc == 3))
                            dve.add(lambda e, ps=ps, oc=oc, tt=tt: e.scalar_tensor_tensor(
                                out=xt[:, oc, tt * 512:(tt + 1) * 512], in0=ps[:, :], scalar=normc[:, 2, oc:oc + 1],
                                in1=xt[:, oc, tt * 512:(tt + 1) * 512], op0=ALU.mult, op1=ALU.add),
                                r=[psb, normcB, xBt[tt]], w=[xBt[tt]])
                    P.wrel(woi)
                if not isA:
                    for stg, stgB, dram in ((kst, kstB, nk), (vst, vstB, nv)):
                        for tb in range(4):
                            sq_, t0 = tb // 2, (tb % 2) * 128
                            sp.dma(lambda e, stg=stg, tb=tb, sq_=sq_, t0=t0, dram=dram: e.dma_start(
                                out=dram[sq_, ja, t0:t0 + 128, :], in_=stg[:, tb, :]), dbuf=stgB, r=[stgB])
                S.barrier()

        MIXERS[0] = attn


        def hyena(l):
            L = LS if isA else LP
            nseq = T // L
            NBL = L // 128
            NBT = T // 128
            hc = HC[L]
            PI = math.pi
            with ExitStack() as les:
                def A(shape, dt, name):
                    return les.enter_context(P.sb(shape, dt, name))
                hd = A([128, 2, L], F32, "hd")
                usb = A([128, nseq, L + 2], F32, "usb")
                x1c = A([128, nseq, L], F32, "x1c")
                z32 = A([128, nseq, L], F32, "z32")
                zb = A([128, T], BF16, "zb")
                x0c = A([128, T], BF16, "x0c")
                ztm = A([128, NBT, 128], BF16, "ztm")
                he = A([128, NBL, 128], BF16, "he")
                ho = A([128, NBL, 128], BF16, "ho")
                Hr = A([128, NBL, 128], BF16, "Hr")
                Hi = A([128, NBL, 128], BF16, "Hi")
                Yr = A([128, NBT, 128], BF16, "Yr")
                Wi = A([128, NBT, 128], BF16, "Wi")
                gout = A([128, T], BF16, "gout")
                woutc = A([64, 2, 128], F32, "woutc")
                absdc = A([128, 128], F32, "absdc")
                dec = A([128, 128], F32, "dec")
                tA = A([128, 512], F32, "tA")
                tAB = Buf("tA")
                tB = A([128, 512], F32, "tB")
                kflt, kfB = tA, tAB
                fw1 = A([33, 64], F32, "fw1")
                fw2 = A([64, 64], F32, "fw2")
                fw3 = A([64, 64], F32, "fw3")
                fbf = A([64, 4], F32, "fbf")
                cw = A([128, 3, 48], F32, "cw")
                cb = A([128, 48], F32, "cb")
                fbv = A([128, KC], F32, "fbv")
                tneg = A([128, NBL], F32, "tneg")
                m0 = A([128, NBL, 2], F32, "m0")
                phi = A([128, NBL, 2], F32, "phi")
                negpi = A([128, 1], F32, "negpi")
                kint = A([64, 512], mybir.dt.int32, "kint")
                kiB = Buf("kint")
                hcB = Buf(f"hyc{which}")
                hdB = [Buf("hd0"), Buf("hd1")]
                usbB, x1B, z32B, zbB, x0B, ztmB = Buf("usb"), Buf("x1c"), Buf("z32"), Buf("zb"), Buf("x0c"), Buf("ztm")
                heB, hoB, HB, YB, goutB = Buf("he"), Buf("ho"), Buf("H"), Buf("Y"), Buf("gout")
                wocB, adcB, decB, tBB = Buf(f"woc{which}"), Buf(f"adc{which}"), Buf("dec"), Buf("tB")
                for dst, src in ((fw1[:, :], hyw1_d), (fw2[:, :], hyw2_d), (fw3[:, :], hyw3_d), (fbf[:, :], hybf_d),
                                 (cw[:, :, :], hycw_d), (cb[:, :], hycb_d), (fbv[:, :], hyfb_d), (tneg[:, :], hc["tneg"]),
                                 (m0[:, :, :], hc["m0"]), (phi[:, :, :], hc["phi"])):
                    sp.dma(lambda e, dst=dst, src=src: e.dma_start(out=dst, in_=src), dbuf=hcB, w=[hcB])
                sp.dma(lambda e: e.dma_start(out=hd[0:33, 1, :], in_=hc["zemb"]), dbuf=hcB, w=[hcB, hdB[1]])
                dve.add(lambda e: e.memset(negpi[:, :], -PI), w=[hcB])
                dve.add(lambda e: e.memset(usb[:, :, :], 0.0), w=[usbB])
                CW = min(512, L)
                for li, (wt, kin, src_i, dst_i) in enumerate(((fw1, 33, 1, 0), (fw2, 64, 0, 1), (fw3, 64, 1, 0))):
                    for ct in range(L // CW):
                        ps, psb = P.psum()
                        pe.add(lambda e, ps=ps, wt=wt, kin=kin, src_i=src_i, ct=ct: e.matmul(
                            ps[0:64, 0:CW], wt[0:kin, :], hd[0:kin, src_i, ct * CW:(ct + 1) * CW], start=True, stop=True),
                            r=[hcB, hdB[src_i]], w=[psb])
                        dsl = hd[0:64, dst_i, ct * CW:(ct + 1) * CW]
                        dve.add(lambda e, ps=ps, dsl=dsl, li=li: e.tensor_scalar(
                            out=dsl, in0=ps[0:64, 0:CW], scalar1=fbf[:, li:li + 1], scalar2=fbf[:, 3:4],
                            op0=ALU.add, op1=ALU.mult), r=[psb, hcB], w=[hdB[dst_i]])
                        dve.add(lambda e, dsl=dsl: e.tensor_scalar(out=dsl, in0=dsl, scalar1=1.0 / (2.0 * PI), scalar2=8.5,
                                                                  op0=ALU.mult, op1=ALU.add), r=[hdB[dst_i]], w=[hdB[dst_i]])
                        dve.add(lambda e, dsl=dsl: e.tensor_copy(out=kint[0:64, 0:CW], in_=dsl), r=[hdB[dst_i]], w=[kiB])
                        dve.add(lambda e: e.tensor_copy(out=kflt[0:64, 0:CW], in_=kint[0:64, 0:CW]), r=[kiB], w=[kfB])
                        dve.add(lambda e, dsl=dsl: e.tensor_tensor(out=dsl, in0=dsl, in1=kflt[0:64, 0:CW], op=ALU.subtract),
                                r=[hdB[dst_i], kfB], w=[hdB[dst_i]])
                        dve.add(lambda e, dsl=dsl: e.tensor_scalar(out=kflt[0:64, 0:CW], in0=dsl, scalar1=0.0, scalar2=None, op0=ALU.is_lt),
                                r=[hdB[dst_i]], w=[kfB])
                        dve.add(lambda e, dsl=dsl: e.tensor_tensor(out=dsl, in0=dsl, in1=kflt[0:64, 0:CW], op=ALU.add),
                                r=[hdB[dst_i], kfB], w=[hdB[dst_i]])
                        act.add(lambda e, dsl=dsl: e.activation(out=dsl, in_=dsl, func=AF.Sin, bias=negpi[0:64, :], scale=2.0 * PI),
                                r=[hdB[dst_i], hcB], w=[hdB[dst_i]])
                hd3B = hdB[0]

                def proj_conv(c, col0, dst_fn):
                    wv, wb, wi = P.wget(lambda dt, c=c, col0=col0: dt["hy_w_in"][0, :, col0 + c * 128:col0 + (c + 1) * 128].rearrange(
                        "(kc p) n -> p kc n", p=128), (KC, 128))
                    for tt in range(NT):
                        ps, psb = P.psum()
                        for kc in range(KC):
                            pe.add(lambda e, ps=ps, kc=kc, tt=tt, wv=wv: e.matmul(
                                ps[:, :], wv[:, kc, :], ht[:, kc, tt * 512:(tt + 1) * 512], start=(kc == 0), stop=(kc == KC - 1)),
                                r=[wb, hBt[tt]], w=[psb], inc=(kc == KC - 1))
                        if isA:
                            act.add(lambda e, ps=ps, tt=tt: e.copy(out=usb[:, 0, 1 + tt * 512:1 + (tt + 1) * 512], in_=ps[:, :]),
                                    r=[psb], w=[usbB])
                        else:
                            act.add(lambda e, ps=ps: e.copy(out=usb[:, :, 1:L + 1], in_=ps[:, :].rearrange("p (a b) -> p a b", a=2)),
                                    r=[psb], w=[usbB])
                    P.wrel(wi)
                    ch = col0 // 128 + c
                    dve.add(lambda e: e.tensor_scalar(out=tmpc[:, :, :], in0=usb[:, :, 0:L], scalar1=cw[:, 0, ch:ch + 1],
                                                      scalar2=cb[:, ch:ch + 1], op0=ALU.mult, op1=ALU.add),
                            r=[usbB, hcB], w=[tmpcB])
                    dve.add(lambda e: e.scalar_tensor_tensor(out=tmpc[:, :, :], in0=usb[:, :, 1:L + 1], scalar=cw[:, 1, ch:ch + 1],
                                                             in1=tmpc[:, :, :], op0=ALU.mult, op1=ALU.add),
                            r=[usbB, hcB, tmpcB], w=[tmpcB])
                    dst_ap, dstB = dst_fn()
                    dve.add(lambda e: e.scalar_tensor_tensor(out=dst_ap, in0=usb[:, :, 2:L + 2], scalar=cw[:, 2, ch:ch + 1],
                                                             in1=tmpc[:, :, :], op0=ALU.mult, op1=ALU.add),
                            r=[usbB, hcB, tmpcB], w=[dstB])

                tmpc = les.enter_context(P.sb([128, nseq, L], F32, "tmpc"))
                tmpcB = Buf("tmpc")

                for c in range(KC):
                    proj_conv(c, 2048, lambda: (x1c[:, :, :], x1B))
                    proj_conv(c, 4096, lambda: (z32[:, :, :], z32B))
                    dve.add(lambda e: e.tensor_tensor(out=z32[:, :, :], in0=z32[:, :, :], in1=x1c[:, :, :], op=ALU.mult),
                            r=[z32B, x1B], w=[z32B])
                    act.add(lambda e: e.copy(out=zb[:, :].rearrange("p (a b) -> p a b", a=nseq), in_=z32[:, :, :]), r=[z32B], w=[zbB])
                    proj_conv(c, 0, lambda: (x0c[:, :].rearrange("p (a b) -> p a b", a=nseq), x0B))
                    z32f = z32[:, :, :].rearrange("p a b -> p (a b)")
                    for tq in range(NBT // 4):
                        ps, psb = P.psum()
                        for j in range(4):
                            tb = tq * 4 + j
                            pe.add(lambda e, ps=ps, j=j, tb=tb: e.transpose(ps[:, j * 128:(j + 1) * 128], z32f[:, tb * 128:(tb + 1) * 128],
                                                                            ident[:, :]), r=[z32B, constB], w=[psb], inc=(j == 3))
                        act.add(lambda e, ps=ps, tq=tq: e.copy(out=ztm[:, tq * 4:(tq + 1) * 4, :],
                                                              in_=ps[:, :].rearrange("p (a b) -> p a b", a=4)), r=[psb], w=[ztmB])
                    sp.dma(lambda e, c=c: e.dma_start(out=woutc[:, :, :], in_=hywo_d[:, :, c * 128:(c + 1) * 128]), dbuf=wocB, w=[wocB])
                    sp.dma(lambda e, c=c: e.dma_start(out=absdc[:, :], in_=absd_d[:, c * 128:(c + 1) * 128]), dbuf=adcB, w=[adcB])
                    for nb in range(NBL):
                        ps, psb = P.psum()
                        for dd in range(2):
                            pe.add(lambda e, ps=ps, dd=dd, nb=nb: e.matmul(
                                ps[:, dd * 128:(dd + 1) * 128], hd[0:64, 0, nb * 128:(nb + 1) * 128], woutc[:, dd, :],
                                start=True, stop=True), r=[hd3B, wocB], w=[psb], inc=(dd == 1))
                        act.add(lambda e, nb=nb: e.activation(out=dec[:, :], in_=absdc[:, :], func=AF.Exp, scale=tneg[:, nb:nb + 1]),
                                r=[adcB, hcB], w=[decB])
                        dve.add(lambda e, ps=ps: e.tensor_tensor(out=tA[:, 0:128], in0=ps[:, 0:128], in1=dec[:, :], op=ALU.mult),
                                r=[psb, decB], w=[tAB])
                        dve.add(lambda e, ps=ps: e.tensor_tensor(out=tB[:, 0:128], in0=ps[:, 128:256], in1=dec[:, :], op=ALU.mult),
                                r=[psb, decB], w=[tBB])
                        dve.add(lambda e, nb=nb: e.scalar_tensor_tensor(out=he[:, nb, :], in0=tB[:, 0:128], scalar=m0[:, nb, 0:1],
                                                                        in1=tA[:, 0:128], op0=ALU.mult, op1=ALU.add),
                                r=[tAB, tBB, hcB], w=[heB])
                        dve.add(lambda e, nb=nb: e.scalar_tensor_tensor(out=ho[:, nb, :], in0=tB[:, 0:128], scalar=m0[:, nb, 1:2],
                                                                        in1=tA[:, 0:128], op0=ALU.mult, op1=ALU.add),
                                r=[tAB, tBB, hcB], w=[hoB])
                    Ct, CtB, Cti = P.wget(lambda dt: dt[f"dftC{L}"][:, :].rearrange("(tb p) f -> p tb f", p=128), (NBL, L))
                    St, StB, Sti = P.wget(lambda dt: dt[f"dftS{L}"][:, :].rearrange("(tb p) f -> p tb f", p=128), (NBL, L))
                    for fo in range(NBL):
                        ps, psb = P.psum()
                        for gi, (tab, tabB, src, srcB) in enumerate(((Ct, CtB, he, heB), (St, StB, he, heB),
                                                                    (Ct, CtB, ho, hoB), (St, StB, ho, hoB))):
                            for nb in range(NBL):
                                pe.add(lambda e, ps=ps, gi=gi, tab=tab, src=src, nb=nb, fo=fo: e.matmul(
                                    ps[:, gi * 128:(gi + 1) * 128], tab[:, nb, fo * 128:(fo + 1) * 128], src[:, nb, :],
                                    start=(nb == 0), stop=(nb == NBL - 1)), r=[tabB, srcB], w=[psb],
                                    inc=(nb == NBL - 1 and gi == 3))
                        dve.add(lambda e, ps=ps, fo=fo: e.tensor_scalar(out=tA[:, 0:128], in0=ps[:, 128:256], scalar1=phi[:, fo, 1:2],
                                                                       scalar2=None, op0=ALU.mult), r=[psb, hcB], w=[tAB])
                        dve.add(lambda e, ps=ps, fo=fo: e.scalar_tensor_tensor(out=Hr[:, fo, :], in0=ps[:, 0:128], scalar=phi[:, fo, 0:1],
                                                                               in1=tA[:, 0:128], op0=ALU.mult, op1=ALU.add),
                                r=[psb, hcB, tAB], w=[HB])
                        dve.add(lambda e, ps=ps, fo=fo: e.tensor_scalar(out=tB[:, 0:128], in0=ps[:, 384:512], scalar1=phi[:, fo, 0:1],
                                                                       scalar2=None, op0=ALU.mult), r=[psb, hcB], w=[tBB])
                        dve.add(lambda e, ps=ps, fo=fo: e.scalar_tensor_tensor(out=Hi[:, fo, :], in0=ps[:, 256:384], scalar=phi[:, fo, 1:2],
                                                                               in1=tB[:, 0:128], op0=ALU.mult, op1=ALU.subtract),
                                r=[psb, hcB, tBB], w=[HB])
                    for sq_ in range(nseq):
                        for fo in range(NBL):
                            ps, psb = P.psum()
                            for gi, (tab, tabB) in enumerate(((Ct, CtB), (St, StB))):
                                for tb in range(NBL):
                                    pe.add(lambda e, ps=ps, gi=gi, tab=tab, tb=tb, fo=fo, sq_=sq_: e.matmul(
                                        ps[:, gi * 128:(gi + 1) * 128], tab[:, tb, fo * 128:(fo + 1) * 128], ztm[:, sq_ * NBL + tb, :],
                                        start=(tb == 0), stop=(tb == NBL - 1)), r=[tabB, ztmB], w=[psb],
                                        inc=(tb == NBL - 1 and gi == 1))
                            yi = sq_ * NBL + fo
                            dve.add(lambda e, ps=ps, fo=fo: e.tensor_tensor(out=tA[:, 0:128], in0=ps[:, 0:128], in1=Hr[:, fo, :], op=ALU.mult),
                                    r=[psb, HB], w=[tAB])
                            dve.add(lambda e, ps=ps, fo=fo: e.tensor_tensor(out=tA[:, 128:256], in0=ps[:, 128:256], in1=Hi[:, fo, :], op=ALU.mult),
                                    r=[psb, HB], w=[tAB])
                            dve.add(lambda e, yi=yi: e.tensor_tensor(out=Yr[:, yi, :], in0=tA[:, 0:128], in1=tA[:, 128:256], op=ALU.add),
                                    r=[tAB], w=[YB])
                            dve.add(lambda e, ps=ps, fo=fo: e.tensor_tensor(out=tB[:, 0:128], in0=ps[:, 128:256], in1=Hr[:, fo, :], op=ALU.mult),
                                    r=[psb, HB], w=[tBB])
                            dve.add(lambda e, ps=ps, fo=fo: e.tensor_tensor(out=tB[:, 128:256], in0=ps[:, 0:128], in1=Hi[:, fo, :], op=ALU.mult),
                                    r=[psb, HB], w=[tBB])
                            dve.add(lambda e, yi=yi: e.tensor_tensor(out=Wi[:, yi, :], in0=tB[:, 0:128], in1=tB[:, 128:256], op=ALU.subtract),
                                    r=[tBB], w=[YB])
                    NW = min(512, L)
                    for sq_ in range(nseq):
                        for tr in range(L // NW):
                            ps, psb = P.psum()
                            for fo in range(NBL):
                                pe.add(lambda e, ps=ps, fo=fo, sq_=sq_, tr=tr: e.matmul(
                                    ps[:, 0:NW], Yr[:, sq_ * NBL + fo, :], Ct[:, fo, tr * NW:(tr + 1) * NW], start=(fo == 0), stop=False),
                                    r=[YB, CtB], w=[psb], inc=False)
                                pe.add(lambda e, ps=ps, fo=fo, sq_=sq_, tr=tr: e.matmul(
                                    ps[:, 0:NW], Wi[:, sq_ * NBL + fo, :], St[:, fo, tr * NW:(tr + 1) * NW], start=False, stop=(fo == NBL - 1)),
                                    r=[YB, StB], w=[psb], inc=(fo == NBL - 1))
                            c0 = sq_ * L + tr * NW
                            dve.add(lambda e, c0=c0, c=c: e.tensor_scalar(out=tA[:, 0:NW], in0=zb[:, c0:c0 + NW], scalar1=fbv[:, c:c + 1],
                                                                         scalar2=None, op0=ALU.mult), r=[zbB, hcB], w=[tAB])
                            dve.add(lambda e, ps=ps: e.scalar_tensor_tensor(out=tB[:, 0:NW], in0=ps[:, 0:NW], scalar=1.0 / L, in1=tA[:, 0:NW],
                                                                            op0=ALU.mult, op1=ALU.add), r=[psb, tAB], w=[tBB])
                            dve.add(lambda e, c0=c0: e.tensor_tensor(out=gout[:, c0:c0 + NW], in0=tB[:, 0:NW], in1=x0c[:, c0:c0 + NW], op=ALU.mult),
                                    r=[tBB, x0B], w=[goutB])
                    P.wrel(Cti)
                    P.wrel(Sti)
                    wo, wob, woi = P.wget(lambda dt, c=c: dt["hy_w_out"][0, c * 128:(c + 1) * 128, :].rearrange(
                        "(kc p) n -> p kc n", p=128), (1, D))
                    for oc in range(KC):
                        for tt in range(NT):
                            ps, psb = P.psum()
                            pe.add(lambda e, ps=ps, oc=oc, tt=tt, wo=wo: e.matmul(
                                ps[:, :], wo[:, 0, oc * 128:(oc + 1) * 128], gout[:, tt * 512:(tt + 1) * 512], start=True, stop=True),
                                r=[wob, goutB], w=[psb])
                            dve.add(lambda e, ps=ps, oc=oc, tt=tt: e.scalar_tensor_tensor(
                                out=xt[:, oc, tt * 512:(tt + 1) * 512], in0=ps[:, :], scalar=normc[:, 2, oc:oc + 1],
                                in1=xt[:, oc, tt * 512:(tt + 1) * 512], op0=ALU.mult, op1=ALU.add),
                                r=[psb, normcB, xBt[tt]], w=[xBt[tt]])
                    P.wrel(woi)
                S.barrier()

        MIXERS[1] = hyena


        def rwkv(l):
            L = LS if isA else LP
            nseq = T // L
            NBT = T // 128
            E05 = math.exp(-0.5)
            allh = [hBt[t] for t in range(NT)]
            with ExitStack() as les:
                def A(shape, dt, name, st=les):
                    return st.enter_context(P.sb(shape, dt, name))
                mu = A([128, 6, KC], F32, "mu"); om = A([128, 6, KC], F32, "om"); hm = A([128, 6, KC], F32, "hm")
                w0t = A([128, 2, KC], F32, "w0t"); a0t = A([128, 2, KC], F32, "a0t")
                vec = A([128, 3, KC], F32, "vec"); omka = A([128, KC], F32, "omka")
                lnp = A([128, 2, KC], F32, "lnp")
                bones = A([128, 128], F32, "bones")
                ohm = A([128, 64 + 128 + 254], F32, "ohm")
                zsel = A([128, 254], BF16, "zsel")
                rcB = Buf(f"rwc{which}")
                for dst, src in ((mu[:, :, :], rwmu_d), (w0t[:, :, :], rww0_d), (a0t[:, :, :], rwa0_d), (vec[:, :, :], rwvec_d),
                                 (lnp[:, :, :], rwln_d), (bones[:, :], bones_d), (ohm[:, :], ohm_d)):
                    sp.dma(lambda e, dst=dst, src=src: e.dma_start(out=dst, in_=src), dbuf=rcB, w=[rcB])
                dve.add(lambda e: e.tensor_scalar(out=om[:, :, :], in0=mu[:, :, :], scalar1=-1.0, scalar2=1.0, op0=ALU.mult, op1=ALU.add),
                        r=[rcB], w=[rcB])
                dve.add(lambda e: e.tensor_scalar(out=hm[:, :, :], in0=mu[:, :, :], scalar1=0.5, scalar2=None, op0=ALU.mult), r=[rcB], w=[rcB])
                dve.add(lambda e: e.tensor_scalar(out=omka[:, :], in0=vec[:, 1, :], scalar1=-1.0, scalar2=1.0, op0=ALU.mult, op1=ALU.add),
                        r=[rcB], w=[rcB])
                dve.add(lambda e: e.tensor_copy(out=zsel[:, :], in_=ohm[:, 192:446]), r=[rcB], w=[rcB])
                S.barrier()

                with ExitStack() as s1:
                    tw = A([128, 2, T], BF16, "tw", s1); ta = A([128, 2, T], BF16, "ta", s1); tg = A([128, 2, T], BF16, "tg", s1)
                    twB, taB, tgB = Buf("tw"), Buf("ta"), Buf("tg")
                    with ExitStack() as s1a:
                        mixb = A([128, KC, T], BF16, "mixb", s1a)
                        t1 = A([128, T], F32, "t1", s1a)
                        stg = A([128, 1, 512], F32, "stg", s1a)
                        stgb = A([128, 1, 512], BF16, "stgb", s1a)
                        mixB, t1B = Buf("mixb"), Buf("t1")
                        stgB = [Buf(f"stg{which}0"), Buf(f"stg{which}1")]
                        stgbB = [Buf(f"stgb{which}0"), Buf(f"stgb{which}1")]
                        cnt = {"s": 0, "b": 0}

                        def build_mix(m):
                            for c in range(KC):
                                dve.add(lambda e, c=c: e.tensor_tensor(out=t1[:, 1:T - 1], in0=ht[:, c, 0:T - 2], in1=ht[:, c, 2:T], op=ALU.add),
                                        r=allh, w=[t1B])
                                for sq_ in range(nseq):
                                    s0 = sq_ * L
                                    dve.add(lambda e, c=c, s0=s0: e.tensor_copy(out=t1[:, s0:s0 + 1], in_=ht[:, c, s0 + 1:s0 + 2]), r=allh, w=[t1B])
                                    dve.add(lambda e, c=c, s0=s0: e.tensor_copy(out=t1[:, s0 + L - 1:s0 + L], in_=ht[:, c, s0 + L - 2:s0 + L - 1]),
                                            r=allh, w=[t1B])
                                dve.add(lambda e, c=c, m=m: e.tensor_scalar(out=t1[:, :], in0=t1[:, :], scalar1=hm[:, m, c:c + 1], scalar2=None,
                                                                           op0=ALU.mult), r=[t1B, rcB], w=[t1B])
                                dve.add(lambda e, c=c, m=m: e.scalar_tensor_tensor(out=mixb[:, c, :], in0=ht[:, c, 0:T], scalar=om[:, m, c:c + 1],
                                                                                  in1=t1[:, :], op0=ALU.mult, op1=ALU.add),
                                        r=allh + [t1B, rcB], w=[mixB])

                        def proj_to_scratch(wname, key, also_tm=False):
                            for g in range(4):
                                wv, wb, wi = P.wget(lambda dt, g=g: dt[wname][:, g * 512:(g + 1) * 512].rearrange("(kc p) n -> p kc n", p=128),
                                                    (KC, 512))
                                for nn in range(4):
                                    c = g * 4 + nn
                                    for tt in range(NT):
                                        ps, psb = P.psum()
                                        for kc in range(KC):
                                            pe.add(lambda e, ps=ps, kc=kc, tt=tt, nn=nn, wv=wv: e.matmul(
                                                ps[:, :], wv[:, kc, nn * 128:(nn + 1) * 128], mixb[:, kc, tt * 512:(tt + 1) * 512],
                                                start=(kc == 0), stop=(kc == KC - 1)), r=[wb, mixB], w=[psb], inc=(kc == KC - 1))
                                        k = 0
                                        act.add(lambda e, ps=ps, k=k: e.copy(out=stg[:, k, :], in_=ps[:, :]), r=[psb], w=[stgB[k]])
                                        sp.dma(lambda e, k=k, c=c, tt=tt: e.dma_start(out=SC[key][c, :, tt * 512:(tt + 1) * 512], in_=stg[:, k, :]),
                                               dbuf=stgB[k], r=[stgB[k]], w=[SCB[key]])
                                if also_tm:
                                    for tb in range(NBT):
                                        ps, psb = P.psum()
                                        for kc in range(KC):
                                            pe.add(lambda e, ps=ps, kc=kc, tb=tb, wv=wv: e.matmul(
                                                ps[:, :], mixb[:, kc, tb * 128:(tb + 1) * 128], wv[:, kc, :],
                                                start=(kc == 0), stop=(kc == KC - 1)), r=[wb, mixB], w=[psb], inc=(kc == KC - 1))
                                        k = 0
                                        act.add(lambda e, ps=ps, k=k: e.copy(out=stgb[:, k, :], in_=ps[:, :]), r=[psb], w=[stgbB[k]])
                                        sp.dma(lambda e, k=k, tb=tb, g=g: e.dma_start(
                                            out=SC["vtm"][tb * 128:(tb + 1) * 128, g * 512:(g + 1) * 512], in_=stgb[:, k, :]),
                                            dbuf=stgbB[k], r=[stgbB[k]], w=[SCB["vtm"]])
                                P.wrel(wi)

                        def lora1(wname, d, ncols, dst, dstB, func):
                            wv, wb, wi = P.wget(lambda dt: (dt[wname][d] if d is not None else dt[wname]).rearrange(
                                "(kc p) n -> p kc n", p=128), (KC, ncols))
                            for oc in range((ncols + 127) // 128):
                                mm_ = min(128, ncols - oc * 128)
                                for tt in range(NT):
                                    ps, psb = P.psum()
                                    for kc in range(KC):
                                        pe.add(lambda e, ps=ps, kc=kc, tt=tt, oc=oc, mm_=mm_, wv=wv: e.matmul(
                                            ps[0:mm_, :], wv[:, kc, oc * 128:oc * 128 + mm_], mixb[:, kc, tt * 512:(tt + 1) * 512],
                                            start=(kc == 0), stop=(kc == KC - 1)), r=[wb, mixB], w=[psb], inc=(kc == KC - 1))
                                    sl = dst(oc, mm_, tt)
                                    act.add(lambda e, ps=ps, sl=sl, mm_=mm_: e.activation(out=sl, in_=ps[0:mm_, :], func=func), r=[psb], w=[dstB])
                            P.wrel(wi)

                        build_mix(0)
                        proj_to_scratch("rw_wr", "r")
                        build_mix(2)
                        proj_to_scratch("rw_wk", "k")
                        build_mix(3)
                        proj_to_scratch("rw_wv", "v", also_tm=True)
                        build_mix(1)
                        for d in range(2):
                            lora1("rw_w1", d, 96, lambda oc, m_, tt, d=d: tw[0:96, d, tt * 512:(tt + 1) * 512], twB, AF.Tanh)
                        build_mix(4)
                        for d in range(2):
                            lora1("rw_a1", d, 96, lambda oc, m_, tt, d=d: ta[0:96, d, tt * 512:(tt + 1) * 512], taB, AF.Copy)
                        build_mix(5)
                        lora1("rw_g1", None, 256, lambda oc, m_, tt: tg[:, oc, tt * 512:(tt + 1) * 512], tgB, AF.Sigmoid)
                        S.barrier()

                    with ExitStack() as s1b:
                        rc = A([128, T], F32, "rc", s1b); kcx = A([128, T], F32, "kcx", s1b); vc = A([128, T], F32, "vc", s1b)
                        ad = A([128, 2, T], F32, "ad", s1b)
                        X1 = A([128, T], F32, "X1", s1b); X2 = A([128, T], F32, "X2", s1b); X3 = A([128, T], F32, "X3", s1b)
                        lw2 = A([128, 2, 128], BF16, "lw2", s1b); la2 = A([128, 2, 128], BF16, "la2", s1b); lg2 = A([128, 2, 128], BF16, "lg2", s1b)
                        rcB_, kcB, vcB, adB = Buf(f"rc{which}"), Buf(f"kc{which}"), Buf(f"vc{which}"), Buf("ad")
                        X1B, X2B, X3B = Buf(f"X1{which}"), Buf(f"X2{which}"), Buf(f"X3{which}")
                        l2B = Buf(f"l2{which}")

                        def store(X, XB, key, c):
                            sp.dma(lambda e: e.dma_start(out=SC[key][c, :, 0:T], in_=X[:, :]), dbuf=XB, r=[XB], w=[SCB[key]])

                        for c in range(KC):
                            for dstt, dB, key in ((rc, rcB_, "r"), (kcx, kcB, "k"), (vc, vcB, "v")):
                                sp.dma(lambda e, dstt=dstt, key=key, c=c: e.dma_start(out=dstt[:, :], in_=SC[key][c, :, 0:T]),
                                       dbuf=dB, r=[SCB[key]], w=[dB])
                            for d in range(2):
                                pool.dma(lambda e, d=d, c=c: e.dma_start(out=lw2[0:96, d, :], in_=rw_w2_d[d, :, c * 128:(c + 1) * 128]),
                                         dbuf=l2B, w=[l2B])
                                pool.dma(lambda e, d=d, c=c: e.dma_start(out=la2[0:96, d, :], in_=rw_a2_d[d, :, c * 128:(c + 1) * 128]),
                                         dbuf=l2B, w=[l2B])
                                pool.dma(lambda e, d=d, c=c: e.dma_start(out=lg2[:, d, :], in_=rw_g2_d[d * 128:(d + 1) * 128, c * 128:(c + 1) * 128]),
                                         dbuf=l2B, w=[l2B])
                            for d in range(2):
                                for tt in range(NT):
                                    ps, psb = P.psum()
                                    pe.add(lambda e, ps=ps, d=d, tt=tt: e.matmul(ps[:, :], lw2[0:96, d, :], tw[0:96, d, tt * 512:(tt + 1) * 512],
                                                                                start=True, stop=True), r=[l2B, twB], w=[psb])
                                    act.add(lambda e, ps=ps, d=d, tt=tt, c=c: e.activation(out=X1[:, tt * 512:(tt + 1) * 512], in_=ps[:, :],
                                                                                          func=AF.Sigmoid, bias=w0t[:, d, c:c + 1], scale=1.0),
                                            r=[psb, rcB], w=[X1B])
                                if CFG.get("rw_scan", "chunk") == "chunk":
                                    act.add(lambda e: e.activation(out=X1[:, :], in_=X1[:, :], func=AF.Copy, scale=-E05), r=[X1B], w=[X1B])
                                    store(X1, X1B, f"lw{d}", c)
                                else:
                                    act.add(lambda e: e.activation(out=X1[:, :], in_=X1[:, :], func=AF.Exp, scale=-E05), r=[X1B], w=[X1B])
                                    store(X1, X1B, f"w{d}", c)
                            for tt in range(NT):
                                for d in range(2):
                                    ps, psb = P.psum()
                                    pe.add(lambda e, ps=ps, d=d, tt=tt: e.matmul(ps[:, :], la2[0:96, d, :], ta[0:96, d, tt * 512:(tt + 1) * 512],
                                                                                start=True, stop=True), r=[l2B, taB], w=[psb])
                                    act.add(lambda e, ps=ps, d=d, tt=tt, c=c: e.activation(out=ad[:, d, tt * 512:(tt + 1) * 512], in_=ps[:, :],
                                                                                          func=AF.Sigmoid, bias=a0t[:, d, c:c + 1], scale=1.0),
                                            r=[psb, rcB], w=[adB])
                                ps, psb = P.psum()
                                for kc in range(2):
                                    pe.add(lambda e, ps=ps, kc=kc, tt=tt: e.matmul(ps[:, :], lg2[:, kc, :], tg[:, kc, tt * 512:(tt + 1) * 512],
                                                                                  start=(kc == 0), stop=(kc == 1)), r=[l2B, tgB], w=[psb], inc=(kc == 1))
                                act.add(lambda e, ps=ps, tt=tt: e.copy(out=X2[:, tt * 512:(tt + 1) * 512], in_=ps[:, :]), r=[psb], w=[X2B])
                            store(X2, X2B, "g", c)
                            dve.add(lambda e, c=c: e.tensor_scalar(out=X1[:, :], in0=kcx[:, :], scalar1=vec[:, 0, c:c + 1], scalar2=None, op0=ALU.mult),
                                    r=[kcB, rcB], w=[X1B])
                            act.add(lambda e: e.activation(out=X2[:, :], in_=X1[:, :], func=AF.Square), r=[X1B], w=[X2B])
                            for tt in range(NT):
                                ps, psb = P.psum()
                                pe.add(lambda e, ps=ps, tt=tt: e.matmul(ps[:, :], bones[:, :], X2[:, tt * 512:(tt + 1) * 512], start=True, stop=True),
                                       r=[rcB, X2B], w=[psb])
                                dve.add(lambda e, ps=ps, tt=tt: e.tensor_scalar(out=X3[:, tt * 512:(tt + 1) * 512], in0=ps[:, :], scalar1=1e-12,
                                                                               scalar2=None, op0=ALU.add), r=[psb], w=[X3B])
                            act.add(lambda e: e.activation(out=X3[:, :], in_=X3[:, :], func=AF.Sqrt), r=[X3B], w=[X3B])
                            dve.add(lambda e: e.reciprocal(out=X3[:, :], in_=X3[:, :]), r=[X3B], w=[X3B])
                            dve.add(lambda e: e.scalar_tensor_tensor(out=X1[:, :], in0=X1[:, :], scalar=-1.0, in1=X3[:, :], op0=ALU.mult, op1=ALU.mult),
                                    r=[X1B, X3B], w=[X1B])
                            store(X1, X1B, "a", c)
                            for d in range(2):
                                dve.add(lambda e, d=d: e.scalar_tensor_tensor(out=X2[:, :], in0=X1[:, :], scalar=-1.0, in1=ad[:, d, :],
                                                                             op0=ALU.mult, op1=ALU.mult), r=[X1B, adB], w=[X2B])
                                store(X2, X2B, f"b{d}", c)
                            dve.add(lambda e, c=c: e.tensor_scalar(out=X3[:, :], in0=ad[:, 0, :], scalar1=vec[:, 1, c:c + 1], scalar2=omka[:, c:c + 1],
                                                                  op0=ALU.mult, op1=ALU.add), r=[adB, rcB], w=[X3B])
                            dve.add(lambda e: e.tensor_tensor(out=X3[:, :], in0=X3[:, :], in1=kcx[:, :], op=ALU.mult), r=[X3B, kcB], w=[X3B])
                            store(X3, X3B, "kd0", c)
                            dve.add(lambda e, c=c: e.tensor_scalar(out=X2[:, :], in0=ad[:, 1, :], scalar1=vec[:, 1, c:c + 1], scalar2=omka[:, c:c + 1],
                                                                  op0=ALU.mult, op1=ALU.add), r=[adB, rcB], w=[X2B])
                            dve.add(lambda e: e.tensor_tensor(out=X2[:, :], in0=X2[:, :], in1=kcx[:, :], op=ALU.mult), r=[X2B, kcB], w=[X2B])
                            store(X2, X2B, "kd1", c)
                            dve.add(lambda e: e.tensor_tensor(out=X3[:, :], in0=X3[:, :], in1=X2[:, :], op=ALU.add), r=[X3B, X2B], w=[X3B])
                            dve.add(lambda e, c=c: e.scalar_tensor_tensor(out=X3[:, :], in0=rc[:, :], scalar=vec[:, 2, c:c + 1], in1=X3[:, :],
                                                                         op0=ALU.mult, op1=ALU.mult), r=[rcB_, rcB, X3B], w=[X3B])
                            for tt in range(NT):
                                ps, psb = P.psum()
                                pe.add(lambda e, ps=ps, tt=tt: e.matmul(ps[:, :], bones[:, :], X3[:, tt * 512:(tt + 1) * 512], start=True, stop=True),
                                       r=[rcB, X3B], w=[psb])
                                dve.add(lambda e, ps=ps, tt=tt: e.tensor_tensor(out=X2[:, tt * 512:(tt + 1) * 512], in0=ps[:, :],
                                                                               in1=vc[:, tt * 512:(tt + 1) * 512], op=ALU.mult), r=[psb, vcB], w=[X2B])
                            store(X2, X2B, "bon", c)
                        S.barrier()


                def chunk_scan_and_out():
                    L = LS if isA else LP
                    nseq = T // L
                    NCH = L // 64
                    NH = min(NCH, 4)
                    NHALF = NCH // NH
                    with ExitStack() as s2:
                        def A2(shape, dt, name):
                            return s2.enter_context(P.sb(shape, dt, name))
                        W = NH * 64
                        inp5 = A2([128, 5, W], F32, "inp5")
                        cs = A2([128, NH, 64], F32, "cs"); e1 = A2([128, NH, 64], F32, "e1"); e2 = A2([128, NH, 64], F32, "e2")
                        ones = A2([128, 64], F32, "ones")
                        Bbd = A2([128, NH, 2, 64], BF16, "Bbd"); Kbd = A2([128, NH, 2, 64], BF16, "Kbd")
                        Xa = A2([128, NH, 2, 128], BF16, "Xa"); Ya = A2([128, NH, 2, 128], BF16, "Ya")
                        Xo32 = A2([128, NH, 128], BF16, "Xo32"); Yo32 = A2([128, NH, 128], BF16, "Yo32"); Yo64 = A2([128, NH, 128], BF16, "Yo64")
                        Tn = A2([128, NH, 128], BF16, "Tn"); PQ = A2([128, NH, 2, 128], BF16, "PQ")
                        hmk = A2([128, 3, 128], BF16, "hmk"); hmf = A2([128, 3, 128], F32, "hmf")
                        SETS = []
                        for si in range(2):
                            SETS.append((A2([128, NH, 2, 64], BF16, f"AR{si}"), A2([128, NH, 2, 64], BF16, f"Abd{si}"), A2([128, NH], F32, f"PC{si}"),
                                         A2([128, NH, 128], BF16, f"Tt{si}"), A2([128, NH, 2, 64], BF16, f"Akb{si}"),
                                         A2([128, NH, 64], BF16, f"Arb{si}"), A2([128, NH, 64], BF16, f"Ark{si}"),
                                         A2([128, NH, 128], BF16, f"Btb{si}"), A2([128, NH, 128], BF16, f"Ktb{si}"),
                                         A2([128, NH, 64], BF16, f"Vst{si}"), A2([128, NH, 2, 64], BF16, f"Vbd{si}")))
                        M0 = A2([128, 64], F32, "M0"); M0b = A2([128, 64], BF16, "M0b"); M0bd = A2([128, 2, 64], BF16, "M0bd")
                        Gb = A2([128, 64], BF16, "Gb"); Ub = A2([128, 64], BF16, "Ub"); Ubd = A2([128, 2, 64], BF16, "Ubd")
                        ybuf = A2([128, W], F32, "ybuf")
                        mk = A2([128, 2, 128], F32, "mk")
                        sbd = A2([128, 128], F32, "sbd2"); sst = A2([128, 64], F32, "sst2")
                        BSH = {n_: Buf(n_ + which) for n_ in ("inp5", "cs", "e1", "e2", "Bbd", "Kbd", "Xa", "Ya", "Xo32", "Yo32", "Yo64", "Tn", "PQ",
                                                               "M0", "M0b", "M0bd", "Gb", "Ub", "Ubd", "ybuf", "mk", "sbd2", "sst2")}
                        BSET = [{n_: Buf(f"{n_}{si}{which}") for n_ in ("AR", "Abd", "PC", "Tt", "Akb", "Arb", "Ark", "Btb", "Ktb", "Vst", "Vbd")}
                                for si in range(2)]
                        B_ = BSH
                        sp.dma(lambda e: e.dma_start(out=mk[:, :, :], in_=rwmask_d[:, :, :]), dbuf=B_["mk"], w=[B_["mk"]])
                        sp.dma(lambda e: e.dma_start(out=hmf[:, :, :], in_=hmask_d[:, :, :]), dbuf=B_["mk"], w=[B_["mk"]])
                        dve.add(lambda e: e.tensor_copy(out=hmk[:, :, :], in_=hmf[:, :, :]), r=[B_["mk"]], w=[B_["mk"]])
                        dve.add(lambda e: e.memset(ones[:, :], 1.0), w=[B_["mk"]])
                        for tz, nm in ((Bbd[:, :, :, :], "Bbd"), (Kbd[:, :, :, :], "Kbd"), (Xa[:, :, :, :], "Xa"), (Ya[:, :, :, :], "Ya")):
                            dve.add(lambda e, tz=tz: e.memset(tz, 0.0), w=[BSH[nm]])
                        for si in range(2):
                            for idx, nm in ((1, "Abd"), (4, "Akb"), (10, "Vbd")):
                                dve.add(lambda e, tz=SETS[si][idx]: e.memset(tz[:, :, :, :], 0.0), w=[BSET[si][nm]])
                        dve.add(lambda e: e.memset(M0bd[:, :, :], 0.0), w=[B_["M0bd"]])
                        dve.add(lambda e: e.memset(Ubd[:, :, :], 0.0), w=[B_["Ubd"]])
                        dve.add(lambda e: e.memset(sbd[:, :], 0.0), w=[B_["sbd2"]])
                        ei = {"k": 0}

                        def evac(fn_act, fn_dve, r, w):
                            ei["k"] += 1
                            if ei["k"] % 2 == 0 and fn_act is not None:
                                act.add(fn_act, r=r, w=w)
                            else:
                                dve.add(fn_dve, r=r, w=w)

                        def bd_place(dst, dstB, src_fn, srcB, dve_only=False):
                            for hh in range(2):
                                sl = slice(hh * 64, (hh + 1) * 64)
                                dve.add(lambda e, sl=sl, hh=hh: e.tensor_copy(out=dst[sl, hh, :], in_=src_fn(sl)), r=srcB, w=[dstB])

                        def prep_gen(u, si):
                            sq_, d, c, hv, first, last = u
                            AR, Abd, PC, Tt, Akb, Arb, Ark, Btb, Ktb, Vst, Vbd = SETS[si]
                            B_ = dict(BSH)
                            B_.update(BSET[si])
                            s0 = sq_ * L
                            nlist = list(range(NH)) if d == 0 else list(range(NH - 1, -1, -1))
                            t0 = s0 + hv * W
                            keys5 = ("a", "r", f"lw{d}", f"b{d}", f"kd{d}")
                            for j, key in enumerate(keys5):
                                sp.dma(lambda e, j=j, key=key: e.dma_start(out=inp5[:, j, :], in_=SC[key][c, :, t0:t0 + W]), dbuf=B_["inp5"],
                                       r=[SCB[key]], w=[B_["inp5"]])
                            for n in range(NH):
                                for hh in range(2):
                                    sp.dma(lambda e, n=n, hh=hh: e.dma_start(
                                        out=Vst[hh * 64:(hh + 1) * 64, n, :],
                                        in_=SC["vtm"][t0 + n * 64:t0 + (n + 1) * 64, (2 * c + hh) * 64:(2 * c + hh + 1) * 64]),
                                        dbuf=B_["Vst"], r=[SCB["vtm"]], w=[B_["Vst"]])
                            v3 = lambda j: inp5[:, j, :].rearrange("p (n t) -> p n t", n=NH)
                            for n in range(NH):
                                dve.add(lambda e, n=n: e.tensor_tensor_scan(out=cs[:, n, :], data0=ones[:, :], data1=inp5[:, 2, n * 64:(n + 1) * 64],
                                                                            initial=0.0, op0=ALU.mult, op1=ALU.add),
                                        r=[B_["inp5"], B_["mk"]], w=[B_["cs"]])
                            if d == 1:
                                dve.add(lambda e: e.tensor_tensor(out=e1[:, :, :], in0=v3(2), in1=cs[:, :, :], op=ALU.subtract),
                                        r=[B_["inp5"], B_["cs"]], w=[B_["e1"]])
                                dve.add(lambda e: e.tensor_copy(out=e2[:, :, 0:1], in_=cs[:, :, 63:64]), r=[B_["cs"]], w=[B_["e2"]])
                                dve.add(lambda e: e.tensor_tensor(out=cs[:, :, :], in0=e1[:, :, :], in1=bcast(e2[:, :, 0:1], [128, NH, 64]), op=ALU.add),
                                        r=[B_["e1"], B_["e2"]], w=[B_["cs"]])
                            lastpos = 63 if d == 0 else 0
                            act.add(lambda e: e.activation(out=PC[:, :], in_=cs[:, :, lastpos], func=AF.Exp), r=[B_["cs"]], w=[B_["PC"]])
                            dve.add(lambda e: e.tensor_tensor(out=e1[:, :, :], in0=cs[:, :, :], in1=v3(2), op=ALU.subtract), r=[B_["cs"], B_["inp5"]], w=[B_["e1"]])
                            act.add(lambda e: e.activation(out=e1[:, :, :], in_=e1[:, :, :], func=AF.Exp), r=[B_["e1"]], w=[B_["e1"]])
                            dve.add(lambda e: e.tensor_tensor(out=AR[:, :, 0, :], in0=v3(0), in1=e1[:, :, :], op=ALU.mult), r=[B_["inp5"], B_["e1"]], w=[B_["AR"]])
                            act.add(lambda e: e.activation(out=e2[:, :, :], in_=cs[:, :, :], func=AF.Exp), r=[B_["cs"]], w=[B_["e2"]])
                            dve.add(lambda e: e.tensor_tensor(out=AR[:, :, 1, :], in0=v3(1), in1=e2[:, :, :], op=ALU.mult), r=[B_["inp5"], B_["e2"]], w=[B_["AR"]])
                            act.add(lambda e: e.activation(out=e1[:, :, :], in_=cs[:, :, :], func=AF.Exp, scale=-1.0), r=[B_["cs"], B_["AR"]], w=[B_["e1"]])
                            for hh in range(2):
                                sl = slice(hh * 64, (hh + 1) * 64)
                                dve.add(lambda e, sl=sl, hh=hh: e.tensor_copy(out=Abd[sl, :, hh, :], in_=AR[sl, :, 0, :]), r=[B_["AR"]], w=[B_["Abd"]])
                                dve.add(lambda e, sl=sl, hh=hh: e.tensor_tensor(out=Bbd[sl, :, hh, :], in0=v3(3)[sl], in1=e1[sl, :, :], op=ALU.mult),
                                        r=[B_["inp5"], B_["e1"]], w=[B_["Bbd"]])
                                dve.add(lambda e, sl=sl, hh=hh: e.tensor_tensor(out=Kbd[sl, :, hh, :], in0=v3(4)[sl], in1=e1[sl, :, :], op=ALU.mult),
                                        r=[B_["inp5"], B_["e1"]], w=[B_["Kbd"]])
                                dve.add(lambda e, sl=sl, hh=hh: e.tensor_copy(out=Vbd[sl, :, hh, :], in_=Vst[sl, :, :]), r=[B_["Vst"]], w=[B_["Vbd"]])
                            f2 = lambda t_, n: t_[:, n, :, :].rearrange("p a b -> p (a b)")
                            yield
                            def bank_mm(lhs_fn, rhs_fn, rB):
                                ps_, psb_ = P.psum()
                                for n in range(NH):
                                    pe.add(lambda e, ps_=ps_, n=n: e.matmul(ps_[:, n * 128:(n + 1) * 128], lhs_fn(n), rhs_fn(n), start=True, stop=True),
                                           r=rB, w=[psb_], inc=(n == NH - 1))
                                return ps_[:, 0:NH * 128].rearrange("p (n c) -> p n c", n=NH), psb_

                            def bcn(ap2, parts):
                                return bcast(ap2.unsqueeze(1), [parts, NH, ap2.shape[-1]])
                            p1, p1b = bank_mm(lambda n: f2(Bbd, n), lambda n: f2(AR, n), [B_["Bbd"], B_["AR"]])
                            for hh in range(2):
                                sl = slice(hh * 64, (hh + 1) * 64)
                                dve.add(lambda e, sl=sl, hh=hh: e.tensor_tensor(out=Xa[sl, :, 0, hh * 64:(hh + 1) * 64], in0=p1[sl, :, 0:64],
                                                                                in1=bcn(mk[sl, d, 0:64], 64), op=ALU.mult), r=[p1b, B_["mk"]], w=[B_["Xa"]])
                            dve.add(lambda e: e.tensor_tensor(out=Arb[:, :, :], in0=p1[:, :, 64:128], in1=bcn(mk[:, d, 64:128], 128), op=ALU.mult),
                                    r=[p1b, B_["mk"]], w=[B_["Arb"]])
                            p2, p2b = bank_mm(lambda n: f2(Kbd, n), lambda n: f2(AR, n), [B_["Kbd"], B_["AR"]])
                            for hh in range(2):
                                sl = slice(hh * 64, (hh + 1) * 64)
                                dve.add(lambda e, sl=sl, hh=hh: e.tensor_tensor(out=Akb[sl, :, hh, :], in0=p2[sl, :, 0:64], in1=bcn(mk[sl, d, 0:64], 64), op=ALU.mult),
                                        r=[p2b, B_["mk"]], w=[B_["Akb"]])
                            dve.add(lambda e: e.tensor_tensor(out=Ark[:, :, :], in0=p2[:, :, 64:128], in1=bcn(mk[:, d, 64:128], 128), op=ALU.mult),
                                    r=[p2b, B_["mk"]], w=[B_["Ark"]])
                            yield
                            p3, p3b = bank_mm(lambda n: f2(Bbd, n), lambda n: identb[:, :], [B_["Bbd"], c2])
                            act.add(lambda e: e.copy(out=Btb[:, :, :], in_=p3), r=[p3b], w=[B_["Btb"]])
                            p3k, p3kb = bank_mm(lambda n: f2(Kbd, n), lambda n: identb[:, :], [B_["Kbd"], c2])
                            act.add(lambda e: e.copy(out=Ktb[:, :, :], in_=p3k), r=[p3kb], w=[B_["Ktb"]])
                            p4, p4b = bank_mm(lambda n: Xa[:, n, 0, :], lambda n: identb[:, :], [B_["Xa"], c2])
                            act.add(lambda e: e.copy(out=Ya[:, :, 0, :], in_=p4), r=[p4b], w=[B_["Ya"]])
                            yield
                            hm = lambda k_: bcn(hmk[:, k_, :], 128)
                            dve.add(lambda e: e.tensor_tensor(out=Xo32[:, :, :], in0=Xa[:, :, 0, :], in1=hm(1), op=ALU.mult), r=[B_["Xa"], B_["mk"]], w=[B_["Xo32"]])
                            dve.add(lambda e: e.tensor_tensor(out=Yo32[:, :, :], in0=Ya[:, :, 0, :], in1=hm(1), op=ALU.mult), r=[B_["Ya"], B_["mk"]], w=[B_["Yo32"]])
                            dve.add(lambda e: e.tensor_tensor(out=Yo64[:, :, :], in0=Ya[:, :, 0, :], in1=hm(2), op=ALU.mult), r=[B_["Ya"], B_["mk"]], w=[B_["Yo64"]])
                            dve.add(lambda e: e.tensor_tensor(out=Xa[:, :, 0, :], in0=Xa[:, :, 0, :], in1=hm(0), op=ALU.mult),
                                    r=[B_["Xa"], B_["mk"], B_["Xo32"]], w=[B_["Xa"]])
                            dve.add(lambda e: e.tensor_tensor(out=Ya[:, :, 0, :], in0=Ya[:, :, 0, :], in1=hm(0), op=ALU.mult),
                                    r=[B_["Ya"], B_["mk"], B_["Yo32"], B_["Yo64"]], w=[B_["Ya"]])
                            dve.add(lambda e: e.tensor_tensor(out=Tt[:, :, :], in0=Xa[:, :, 0, :], in1=bcn(identb[:, :], 128), op=ALU.add), r=[B_["Xa"], c2], w=[B_["Tt"]])
                            dve.add(lambda e: e.tensor_tensor(out=Tn[:, :, :], in0=Ya[:, :, 0, :], in1=bcn(identb[:, :], 128), op=ALU.add), r=[B_["Ya"], c2], w=[B_["Tn"]])
                            yield

                            def acc_into(dst, dstB, psv, psvb):
                                dve.add(lambda e: e.tensor_tensor(out=dst[:, :, :], in0=psv, in1=dst[:, :, :], op=ALU.add), r=[psvb, dstB], w=[dstB])
                            for m in range(1, 4):
                                cur, prv = m % 2, (m - 1) % 2
                                px, pxb = bank_mm(lambda n, prv=prv: Ya[:, n, prv, :], lambda n, prv=prv: Xa[:, n, prv, :], [B_["Xa"], B_["Ya"]])
                                py, pyb = bank_mm(lambda n, prv=prv: Xa[:, n, prv, :], lambda n, prv=prv: Ya[:, n, prv, :], [B_["Xa"], B_["Ya"]])
                                act.add(lambda e, px=px, cur=cur: e.copy(out=Xa[:, :, cur, :], in_=px), r=[pxb], w=[B_["Xa"]])
                                act.add(lambda e, py=py, cur=cur: e.copy(out=Ya[:, :, cur, :], in_=py), r=[pyb], w=[B_["Ya"]])
                                yield
                                pt, ptb = bank_mm(lambda n, cur=cur: Ya[:, n, cur, :], lambda n: Tt[:, n, :], [B_["Ya"], B_["Tt"]])
                                pn, pnb = bank_mm(lambda n, cur=cur: Xa[:, n, cur, :], lambda n: Tn[:, n, :], [B_["Xa"], B_["Tn"]])
                                acc_into(Tt, B_["Tt"], pt, ptb)
                                acc_into(Tn, B_["Tn"], pn, pnb)
                                yield
                            pp_, ppb = bank_mm(lambda n: Yo32[:, n, :], lambda n: Tt[:, n, :], [B_["Yo32"], B_["Tt"]])
                            pq_, pqb = bank_mm(lambda n: Xo32[:, n, :], lambda n: Tn[:, n, :], [B_["Xo32"], B_["Tn"]])
                            act.add(lambda e: e.copy(out=PQ[:, :, 0, :], in_=pp_), r=[ppb], w=[B_["PQ"]])
                            act.add(lambda e: e.copy(out=PQ[:, :, 1, :], in_=pq_), r=[pqb], w=[B_["PQ"]])
                            yield
                            r1, r1b = bank_mm(lambda n: Tn[:, n, :], lambda n: PQ[:, n, 0, :], [B_["Tn"], B_["PQ"]])
                            r2, r2b = bank_mm(lambda n: Tt[:, n, :], lambda n: PQ[:, n, 1, :], [B_["Tt"], B_["PQ"]])
                            acc_into(Tt, B_["Tt"], r1, r1b)
                            acc_into(Tn, B_["Tn"], r2, r2b)
                            yield
                            p6, p6b = bank_mm(lambda n: Yo64[:, n, :], lambda n: Tt[:, n, :], [B_["Yo64"], B_["Tt"]])
                            act.add(lambda e: e.copy(out=PQ[:, :, 0, :], in_=p6), r=[p6b], w=[B_["PQ"]])
                            r6, r6b = bank_mm(lambda n: Tn[:, n, :], lambda n: PQ[:, n, 0, :], [B_["Tn"], B_["PQ"]])
                            acc_into(Tt, B_["Tt"], r6, r6b)

                        def chain_gen(u, si):
                            sq_, d, c, hv, first, last = u
                            AR, Abd, PC, Tt, Akb, Arb, Ark, Btb, Ktb, Vst, Vbd = SETS[si]
                            B_ = dict(BSH)
                            B_.update(BSET[si])
                            s0 = sq_ * L
                            nlist = list(range(NH)) if d == 0 else list(range(NH - 1, -1, -1))
                            t0 = s0 + hv * W
                            f2 = lambda t_, n: t_[:, n, :, :].rearrange("p a b -> p (a b)")
                            if first:
                                if isA:
                                    for hh in range(2):
                                        sp.dma(lambda e, hh=hh: e.dma_start(out=sbd[hh * 64:(hh + 1) * 64, hh * 64:(hh + 1) * 64],
                                                                            in_=st_d[d, 2 * c + hh, :, :]), dbuf=B_["sbd2"], w=[B_["sbd2"]])
                                    ps, psb = P.psum()
                                    pe.add(lambda e, ps=ps: e.transpose(ps[:, 0:128], sbd[:, :], ident[:, :]), r=[B_["sbd2"], constB], w=[psb])
                                    act.add(lambda e, ps=ps: e.copy(out=M0[0:64, :], in_=ps[0:64, 0:64]), r=[psb], w=[B_["M0"]])
                                    act.add(lambda e, ps=ps: e.copy(out=M0[64:128, :], in_=ps[64:128, 64:128]), r=[psb], w=[B_["M0"]])
                                else:
                                    dve.add(lambda e: e.memset(M0[:, :], 0.0), w=[B_["M0"]])
                                act.add(lambda e: e.copy(out=M0b[:, :], in_=M0[:, :]), r=[B_["M0"]], w=[B_["M0b"]])
                                bd_place(M0bd, B_["M0bd"], lambda sl: M0[sl, :], [B_["M0"]])
                            for n in nlist:
                                psG, pGb = P.psum()
                                pe.add(lambda e, psG=psG, n=n: e.matmul(psG[:, 0:64], f2(Abd, n), M0b[:, :], start=True, stop=False),
                                       r=[B_["Abd"], B_["M0b"]], w=[pGb], inc=False)
                                pe.add(lambda e, psG=psG, n=n: e.matmul(psG[:, 0:64], f2(Akb, n), Vst[:, n, :], start=False, stop=True),
                                       r=[B_["Akb"], B_["Vst"]], w=[pGb])
                                act.add(lambda e, psG=psG: e.copy(out=Gb[:, :], in_=psG[:, 0:64]), r=[pGb], w=[B_["Gb"]])
                                psU, pUb = P.psum()
                                pe.add(lambda e, psU=psU, n=n: e.matmul(psU[:, 0:64], Tt[:, n, :], Gb[:, :], start=True, stop=True),
                                       r=[B_["Tt"], B_["Gb"]], w=[pUb])
                                act.add(lambda e, psU=psU: e.copy(out=Ub[:, :], in_=psU[:, 0:64]), r=[pUb], w=[B_["Ub"]])
                                bd_place(Ubd, B_["Ubd"], lambda sl: Ub[sl, :], [B_["Ub"]])
                                psY, pYb = P.psum()
                                pe.add(lambda e, psY=psY, n=n: e.matmul(psY[:, 0:64], M0bd[:, :, :].rearrange("p a b -> p (a b)"), AR[:, n, 1, :],
                                                                        start=True, stop=False), r=[B_["M0bd"], B_["AR"]], w=[pYb], inc=False)
                                pe.add(lambda e, psY=psY, n=n: e.matmul(psY[:, 0:64], Ubd[:, :, :].rearrange("p a b -> p (a b)"), Arb[:, n, :],
                                                                        start=False, stop=False), r=[B_["Ubd"], B_["Arb"]], w=[pYb], inc=False)
                                pe.add(lambda e, psY=psY, n=n: e.matmul(psY[:, 0:64], f2(Vbd, n), Ark[:, n, :], start=False, stop=True),
                                       r=[B_["Vbd"], B_["Ark"]], w=[pYb])
                                act.add(lambda e, psY=psY, n=n: e.copy(out=ybuf[:, n * 64:(n + 1) * 64], in_=psY[:, 0:64]), r=[pYb], w=[B_["ybuf"]])
                                psM, pMb = P.psum()
                                pe.add(lambda e, psM=psM, n=n: e.matmul(psM[:, 0:64], Btb[:, n, :], Ub[:, :], start=True, stop=False),
                                       r=[B_["Btb"], B_["Ub"]], w=[pMb], inc=False)
                                pe.add(lambda e, psM=psM, n=n: e.matmul(psM[:, 0:64], Ktb[:, n, :], Vst[:, n, :], start=False, stop=True),
                                       r=[B_["Ktb"], B_["Vst"]], w=[pMb])
                                dve.add(lambda e, n=n: e.tensor_scalar(out=M0[:, :], in0=M0[:, :], scalar1=PC[:, n:n + 1], scalar2=None, op0=ALU.mult),
                                        r=[B_["M0"], B_["PC"]], w=[B_["M0"]])
                                dve.add(lambda e, psM=psM, n=n: e.scalar_tensor_tensor(out=M0[:, :], in0=psM[:, 0:64], scalar=PC[:, n:n + 1], in1=M0[:, :],
                                                                                      op0=ALU.mult, op1=ALU.add), r=[pMb, B_["PC"], B_["M0"]], w=[B_["M0"]])
                                act.add(lambda e: e.copy(out=M0b[:, :], in_=M0[:, :]), r=[B_["M0"]], w=[B_["M0b"]])
                                bd_place(M0bd, B_["M0bd"], lambda sl: M0[sl, :], [B_["M0"]])
                                yield
                            sp.dma(lambda e: e.dma_start(out=SC[f"yf{d}"][c, :, t0:t0 + W], in_=ybuf[:, :]), dbuf=B_["ybuf"], r=[B_["ybuf"]],
                                   w=[SCB[f"yf{d}"]])
                            if last and not isA:
                                for hh in range(2):
                                    sl = slice(hh * 64, (hh + 1) * 64)
                                    dve.add(lambda e, sl=sl: e.tensor_copy(out=sbd[sl, sl], in_=M0[sl, :]), r=[B_["M0"]], w=[B_["sbd2"]])
                                ps, psb = P.psum()
                                pe.add(lambda e, ps=ps: e.transpose(ps[:, 0:128], sbd[:, :], ident[:, :]), r=[B_["sbd2"], constB], w=[psb])
                                act.add(lambda e, ps=ps: e.copy(out=sst[0:64, :], in_=ps[0:64, 0:64]), r=[psb], w=[B_["sst2"]])
                                act.add(lambda e, ps=ps: e.copy(out=sst[64:128, :], in_=ps[64:128, 64:128]), r=[psb], w=[B_["sst2"]])
                                sp.dma(lambda e: e.dma_start(out=nst[sq_, d, 2 * c:2 * c + 2, :, :].rearrange("h v k -> (h v) k"), in_=sst[:, :]),
                                       dbuf=B_["sst2"], r=[B_["sst2"]])

                        units = []
                        for sq_ in range(nseq):
                            for d in range(2):
                                for c in range(KC):
                                    hvs = list(range(NHALF)) if d == 0 else list(range(NHALF - 1, -1, -1))
                                    for ih, hv in enumerate(hvs):
                                        units.append((sq_, d, c, hv, ih == 0, ih == NHALF - 1))
                        for _ in prep_gen(units[0], 0):
                            pass
                        for i_, u in enumerate(units):
                            cg = chain_gen(u, i_ % 2)
                            pg = prep_gen(units[i_ + 1], (i_ + 1) % 2) if i_ + 1 < len(units) else None
                            while cg is not None or pg is not None:
                                if cg is not None:
                                    try:
                                        next(cg)
                                    except StopIteration:
                                        cg = None
                                if pg is not None:
                                    try:
                                        next(pg)
                                    except StopIteration:
                                        pg = None
                        S.barrier()

                    with ExitStack() as s3:
                        def A3(shape, dt, name):
                            return s3.enter_context(P.sb(shape, dt, name))
                        ya = A3([128, 2, 512], F32, "ya"); yb = A3([128, 2, 512], F32, "yb")
                        bt = A3([128, 2, 512], F32, "bt3"); gtt = A3([128, 2, 512], F32, "gt3")
                        yc = A3([128, 512], F32, "yc3"); ysq = A3([128, 512], F32, "ysq3"); rs = A3([128, 512], F32, "rs3")
                        opt = A3([128, KC, 512], BF16, "opt3")
                        lnw = A3([128, 2, KC], F32, "lnw3")
                        ldB = [Buf(f"s3ld{which}0"), Buf(f"s3ld{which}1")]
                        ycB, ysqB, rsB, optB, lnB = Buf("yc3"), Buf("ysq3"), Buf("rs3"), Buf("opt3"), Buf(f"lnw3{which}")
                        sp.dma(lambda e: e.dma_start(out=lnw[:, :, :], in_=rwln2_d[:, :, :]), dbuf=lnB, w=[lnB])
                        li = 0
                        for tt in range(NT):
                            cols = slice(tt * 512, (tt + 1) * 512)
                            for c in range(KC):
                                k = li % 2
                                li += 1
                                for dstt, key in ((ya, "yf0"), (yb, "yf1"), (bt, "bon"), (gtt, "g")):
                                    sp.dma(lambda e, dstt=dstt, key=key, c=c, k=k, cols=cols: e.dma_start(out=dstt[:, k, :], in_=SC[key][c, :, cols]),
                                           dbuf=ldB[k], r=[SCB[key]], w=[ldB[k]])
                                dve.add(lambda e, k=k: e.tensor_tensor(out=yc[:, :], in0=ya[:, k, :], in1=yb[:, k, :], op=ALU.add), r=[ldB[k]], w=[ycB])
                                ps, psb = P.psum()
                                pe.add(lambda e, ps=ps: e.matmul(ps[:, :], bones[:, :], yc[:, :], start=True, stop=True), r=[rcB, ycB], w=[psb])
                                dve.add(lambda e, ps=ps: e.scalar_tensor_tensor(out=yc[:, :], in0=ps[:, :], scalar=-1.0 / 64, in1=yc[:, :],
                                                                               op0=ALU.mult, op1=ALU.add), r=[psb, ycB], w=[ycB])
                                act.add(lambda e: e.activation(out=ysq[:, :], in_=yc[:, :], func=AF.Square), r=[ycB], w=[ysqB])
                                ps2, ps2b = P.psum()
                                pe.add(lambda e, ps2=ps2: e.matmul(ps2[:, :], bones[:, :], ysq[:, :], start=True, stop=True), r=[rcB, ysqB], w=[ps2b])
                                dve.add(lambda e, ps2=ps2: e.tensor_scalar(out=rs[:, :], in0=ps2[:, :], scalar1=1.0 / 64, scalar2=64e-5,
                                                                          op0=ALU.mult, op1=ALU.add), r=[ps2b], w=[rsB])
                                act.add(lambda e: e.activation(out=rs[:, :], in_=rs[:, :], func=AF.Sqrt), r=[rsB], w=[rsB])
                                dve.add(lambda e: e.reciprocal(out=rs[:, :], in_=rs[:, :]), r=[rsB], w=[rsB])
                                dve.add(lambda e, c=c: e.scalar_tensor_tensor(out=yc[:, :], in0=yc[:, :], scalar=lnw[:, 0, c:c + 1], in1=rs[:, :],
                                                                             op0=ALU.mult, op1=ALU.mult), r=[ycB, rsB, lnB], w=[ycB])
                                dve.add(lambda e, c=c, k=k: e.scalar_tensor_tensor(out=yc[:, :], in0=yc[:, :], scalar=lnw[:, 1, c:c + 1], in1=bt[:, k, :],
                                                                                  op0=ALU.add, op1=ALU.add), r=[ycB, lnB, ldB[k]], w=[ycB])
                                dve.add(lambda e, c=c, k=k: e.tensor_tensor(out=opt[:, c, :], in0=yc[:, :], in1=gtt[:, k, :], op=ALU.mult),
                                        r=[ycB, ldB[k]], w=[optB])
                            for g in range(4):
                                wv, wb, wi = P.wget(lambda dt, g=g: dt["rw_wo"][:, g * 512:(g + 1) * 512].rearrange("(kc p) n -> p kc n", p=128), (KC, 512))
                                for nn in range(4):
                                    oc = g * 4 + nn
                                    ps, psb = P.psum()
                                    for kc in range(KC):
                                        pe.add(lambda e, ps=ps, kc=kc, nn=nn, wv=wv: e.matmul(ps[:, :], wv[:, kc, nn * 128:(nn + 1) * 128], opt[:, kc, :],
                                                                                            start=(kc == 0), stop=(kc == KC - 1)),
                                               r=[wb, optB], w=[psb], inc=(kc == KC - 1))
                                    dve.add(lambda e, ps=ps, oc=oc, tt=tt: e.scalar_tensor_tensor(
                                        out=xt[:, oc, tt * 512:(tt + 1) * 512], in0=ps[:, :], scalar=normc[:, 2, oc:oc + 1],
                                        in1=xt[:, oc, tt * 512:(tt + 1) * 512], op0=ALU.mult, op1=ALU.add),
                                        r=[psb, normcB, xBt[tt]], w=[xBt[tt]])
                                P.wrel(wi)
                        S.barrier()

                SB = 16
                if CFG.get("rw_scan", "chunk") == "chunk":
                    chunk_scan_and_out()
                    return
                with ExitStack() as s2:
                    M = A([128, 16, 64], F32, "M", s2)
                    T1 = A([128, 16, 64], F32, "T1", s2)
                    T2 = A([128, 16, 64], F32, "T2", s2)
                    t4 = A([128, 16, 64], BF16, "t4", s2)
                    inx = A([128, 2, 5, 16 * SB], F32, "inx", s2)
                    vblk = A([128, 2, 1024], BF16, "vblk", s2)
                    sel = A([128, 64, 128], BF16, "sel", s2)
                    yst = A([128, 512], F32, "yst", s2)
                    sbd = A([128, 8, 128], F32, "sbd", s2)
                    sst = A([128, 16, 64] if not isA else [128, 1, 4], F32, "sst", s2)
                    ohb = A([128, 64 + 128], BF16, "ohb", s2)
                    MB, T1B_, T2B, t4B, selB, ystB, sbdB, sstB = (Buf("M"), Buf("T1"), Buf("T2"), Buf("t4"), Buf("sel"), Buf(f"yst{which}"),
                                                                  Buf(f"sbd{which}"), Buf(f"sst{which}"))
                    inB = [Buf(f"inx{which}0"), Buf(f"inx{which}1")]
                    vbB = [Buf(f"vblk{which}0"), Buf(f"vblk{which}1")]
                    dve.add(lambda e: e.tensor_copy(out=ohb[:, :], in_=ohm[:, 0:192]), r=[rcB], w=[selB])
                    dve.add(lambda e: e.tensor_tensor(out=sel[:, :, :], in0=bcast(ohb[:, 64:192].unsqueeze(1), [128, 64, 128]),
                                                      in1=bcast(ohb[:, 0:64].unsqueeze(2), [128, 64, 128]), op=ALU.mult), r=[selB], w=[selB])
                    saP, saB2 = P.ps_pair[0], [P.ps_b[0], P.ps_b[1]]
                    vbP, vbB2 = P.ps_pair[1], [P.ps_b[2], P.ps_b[3]]
                    yP = [P.ps_pair[2], P.ps_pair[3]]
                    yB2 = [[P.ps_b[4], P.ps_b[5]], [P.ps_b[6], P.ps_b[7]]]
                    keys = lambda d: ("a", "r", f"w{d}", f"b{d}", f"kd{d}")

                    def scan(sq_, d):
                        s0 = sq_ * L
                        if isA:
                            for h8 in range(2):
                                dve.add(lambda e: e.memset(sbd[:, :, :], 0.0), w=[sbdB])
                                for hh in range(2):
                                    sp.dma(lambda e, hh=hh, h8=h8: e.dma_start(
                                        out=sbd[hh * 64:(hh + 1) * 64, :, hh * 64:(hh + 1) * 64],
                                        in_=st_d[d].rearrange("(hj hh) v k -> hh v hj k", hh=2)[hh][:, h8 * 8:(h8 + 1) * 8, :]), dbuf=sbdB, w=[sbdB])
                                for h_ in range(8):
                                    hj = h8 * 8 + h_
                                    ps, psb = P.psum()
                                    pe.add(lambda e, ps=ps, h_=h_: e.transpose(ps[:, 0:128], sbd[:, h_, :], ident[:, :]), r=[sbdB, constB], w=[psb])
                                    act.add(lambda e, ps=ps, hj=hj: e.copy(out=M[0:64, hj, :], in_=ps[0:64, 0:64]), r=[psb], w=[MB])
                                    act.add(lambda e, ps=ps, hj=hj: e.copy(out=M[64:128, hj, :], in_=ps[64:128, 64:128]), r=[psb], w=[MB])
                        else:
                            dve.add(lambda e: e.memset(M[:, :, :], 0.0), w=[MB])

                        def load_in(blk, par):
                            t0 = s0 + blk * SB
                            for j, key in enumerate(keys(d)):
                                sp.dma(lambda e, j=j, key=key, t0=t0, par=par: e.dma_start(
                                    out=inx[:, par, j, :].rearrange("p (c t) -> p c t", c=16),
                                    in_=SC[key][:, :, t0:t0 + SB].rearrange("c p t -> p c t")), dbuf=inB[par], r=[SCB[key]], w=[inB[par]])

                        def load_v(b64, par):
                            t0 = s0 + b64 * 64
                            for hh in range(2):
                                sp.dma(lambda e, hh=hh, t0=t0, par=par: e.dma_start(
                                    out=vblk[hh * 64:(hh + 1) * 64, par, :].rearrange("p (hj v) -> p hj v", hj=16),
                                    in_=SC["vtm"][t0:t0 + 64, :].rearrange("t (hj hh v) -> hh t hj v", hh=2, v=64)[hh]),
                                    dbuf=vbB[par], r=[SCB["vtm"]], w=[vbB[par]])

                        tseq = list(range(L)) if d == 0 else list(range(L - 1, -1, -1))
                        load_in(tseq[0] // SB, 0)
                        load_v(tseq[0] // 64, 0)
                        nb16 = 0
                        nb64 = 0
                        for i, t in enumerate(tseq):
                            if i % SB == 0:
                                par = nb16 % 2
                                nb16 += 1
                                if i + SB < L:
                                    load_in(tseq[i + SB] // SB, 1 - par)
                            if i % 64 == 0:
                                vpar = nb64 % 2
                                nb64 += 1
                                if i + 64 < L:
                                    load_v(tseq[i + 64] // 64, 1 - vpar)
                            tj = t % SB
                            tl = t % 64

                            def bc(j, par=par, tj=tj):
                                return bcast(inx[:, par, j, :].rearrange("p (c t) -> p c t", c=16)[:, :, tj:tj + 1], [128, 16, 64])
                            dve.add(lambda e, bc=bc: e.tensor_tensor(out=T1[:, :, :], in0=M[:, :, :], in1=bc(0), op=ALU.mult),
                                    r=[MB, inB[par]], w=[T1B_])
                            for hf in range(2):
                                pe.add(lambda e, hf=hf: e.matmul(saP[:, hf * 512:(hf + 1) * 512], bones[:, :], T1[:, hf * 8:(hf + 1) * 8, :],
                                                                 start=True, stop=True), r=[rcB, T1B_], w=saB2, inc=(hf == 1))
                            dve.add(lambda e, bc=bc: e.tensor_tensor(out=M[:, :, :], in0=M[:, :, :], in1=bc(2), op=ALU.mult),
                                    r=[MB, inB[par]], w=[MB])
                            for hf in range(2):
                                pe.add(lambda e, hf=hf, tl=tl, vpar=vpar: e.matmul(vbP[:, hf * 512:(hf + 1) * 512], sel[:, tl, :],
                                                                                 vblk[:, vpar, hf * 512:(hf + 1) * 512], start=True, stop=True),
                                       r=[selB, vbB[vpar]], w=vbB2, inc=(hf == 1))
                            dve.add(lambda e, bc=bc: e.tensor_tensor(out=T1[:, :, :], in0=saP[:, :].rearrange("p (c v) -> p c v", c=16), in1=bc(3),
                                                                    op=ALU.mult), r=saB2 + [inB[par]], w=[T1B_])
                            dve.add(lambda e: e.tensor_tensor(out=M[:, :, :], in0=M[:, :, :], in1=T1[:, :, :], op=ALU.add), r=[MB, T1B_], w=[MB])
                            dve.add(lambda e, bc=bc: e.tensor_tensor(out=T2[:, :, :], in0=vbP[:, :].rearrange("p (c v) -> p c v", c=16), in1=bc(4),
                                                                    op=ALU.mult), r=vbB2 + [inB[par]], w=[T2B])
                            dve.add(lambda e: e.tensor_tensor(out=M[:, :, :], in0=M[:, :, :], in1=T2[:, :, :], op=ALU.add), r=[MB, T2B], w=[MB])
                            dve.add(lambda e, bc=bc: e.tensor_tensor(out=t4[:, :, :], in0=M[:, :, :], in1=bc(1), op=ALU.mult),
                                    r=[MB, inB[par]], w=[t4B])
                            yp = (nb64 - 1) % 2
                            for hf in range(2):
                                pe.add(lambda e, hf=hf, tl=tl, yp=yp, i=i: e.matmul(
                                    yP[yp][:, hf * 512:(hf + 1) * 512], zsel[:, 126 - 2 * tl:254 - 2 * tl], t4[:, hf * 8:(hf + 1) * 8, :],
                                    start=(i % 64 == 0), stop=(i % 64 == 63)), r=[rcB, t4B], w=yB2[yp], inc=(hf == 1))
                            if i % 64 == 63:
                                b64g = (s0 + t) // 64
                                for hf in range(2):
                                    act.add(lambda e, yp=yp, hf=hf: e.copy(out=yst[:, :], in_=yP[yp][:, hf * 512:(hf + 1) * 512]), r=yB2[yp], w=[ystB])
                                    sp.dma(lambda e, b64g=b64g, hf=hf: e.dma_start(out=SC[f"y{d}"][b64g, :, hf * 512:(hf + 1) * 512], in_=yst[:, :]),
                                           dbuf=ystB, r=[ystB], w=[SCB[f"y{d}"]])
                        if not isA:
                            for h8 in range(2):
                                dve.add(lambda e: e.memset(sbd[:, :, :], 0.0), w=[sbdB])
                                dve.add(lambda e, h8=h8: e.tensor_copy(out=sbd[0:64, :, 0:64], in_=M[0:64, h8 * 8:(h8 + 1) * 8, :]), r=[MB], w=[sbdB])
                                dve.add(lambda e, h8=h8: e.tensor_copy(out=sbd[64:128, :, 64:128], in_=M[64:128, h8 * 8:(h8 + 1) * 8, :]), r=[MB], w=[sbdB])
                                for h_ in range(8):
                                    hj = h8 * 8 + h_
                                    ps, psb = P.psum()
                                    pe.add(lambda e, ps=ps, h_=h_: e.transpose(ps[:, 0:128], sbd[:, h_, :], ident[:, :]), r=[sbdB, constB], w=[psb])
                                    act.add(lambda e, ps=ps, hj=hj: e.copy(out=sst[0:64, hj, :], in_=ps[0:64, 0:64]), r=[psb], w=[sstB])
                                    act.add(lambda e, ps=ps, hj=hj: e.copy(out=sst[64:128, hj, :], in_=ps[64:128, 64:128]), r=[psb], w=[sstB])
                            for hh in range(2):
                                sp.dma(lambda e, hh=hh: e.dma_start(
                                    out=nst[sq_, d].rearrange("(hj hh) v k -> hh v hj k", hh=2)[hh], in_=sst[hh * 64:(hh + 1) * 64, :, :]),
                                    dbuf=sstB, r=[sstB])

                    for sq_ in range(nseq):
                        for d in range(2):
                            scan(sq_, d)
                    S.barrier()

                with ExitStack() as s3:
                    y0 = A([128, 2, 1024], F32, "y0", s3); y1 = A([128, 2, 1024], F32, "y1", s3)
                    yc = A([128, 16, 64], F32, "yc", s3); ysq = A([128, 16, 64], F32, "ysq", s3)
                    st1 = A([128, 16], F32, "st1", s3); st2 = A([128, 16], F32, "st2", s3)
                    opt = A([128, KC, 512], BF16, "opt", s3)
                    bt = A([128, 2, 512], F32, "bt", s3); gtt = A([128, 2, 512], F32, "gtt", s3)
                    ot = A([128, 512], F32, "ot", s3)
                    yB_ = [Buf(f"yl{which}0"), Buf(f"yl{which}1")]
                    ycB, ysqB, stB_, optB, otB = Buf("yc"), Buf("ysq"), Buf("st12"), Buf("opt"), Buf("ot")
                    bgB = [Buf(f"bg{which}0"), Buf(f"bg{which}1")]
                    li = 0
                    for tt in range(NT):
                        for b8 in range(8):
                            b64g = tt * 8 + b8
                            k = li % 2
                            li += 1
                            sp.dma(lambda e, k=k, b64g=b64g: e.dma_start(out=y0[:, k, :], in_=SC["y0"][b64g, :, :]), dbuf=yB_[k], r=[SCB["y0"]], w=[yB_[k]])
                            sp.dma(lambda e, k=k, b64g=b64g: e.dma_start(out=y1[:, k, :], in_=SC["y1"][b64g, :, :]), dbuf=yB_[k], r=[SCB["y1"]], w=[yB_[k]])
                            y0v = y0[:, k, :].rearrange("p (c v) -> p c v", c=16)
                            y1v = y1[:, k, :].rearrange("p (c v) -> p c v", c=16)
                            dve.add(lambda e, y0v=y0v, y1v=y1v: e.tensor_tensor(out=yc[:, :, :], in0=y0v, in1=y1v, op=ALU.add), r=[yB_[k]], w=[ycB])
                            dve.add(lambda e: e.tensor_reduce(out=st1[:, :], in_=yc[:, :, :], axis=AX.X, op=ALU.add), r=[ycB], w=[stB_])
                            dve.add(lambda e: e.tensor_scalar(out=st1[:, :], in0=st1[:, :], scalar1=1.0 / 64, scalar2=None, op0=ALU.mult), r=[stB_], w=[stB_])
                            dve.add(lambda e: e.tensor_tensor(out=yc[:, :, :], in0=yc[:, :, :], in1=bcast(st1[:, :].unsqueeze(2), [128, 16, 64]),
                                                              op=ALU.subtract), r=[ycB, stB_], w=[ycB])
                            act.add(lambda e: e.activation(out=ysq[:, :, :], in_=yc[:, :, :], func=AF.Square), r=[ycB], w=[ysqB])
                            dve.add(lambda e: e.tensor_reduce(out=st2[:, :], in_=ysq[:, :, :], axis=AX.X, op=ALU.add), r=[ysqB], w=[stB_])
                            dve.add(lambda e: e.tensor_scalar(out=st2[:, :], in0=st2[:, :], scalar1=1.0 / 64, scalar2=64e-5, op0=ALU.mult, op1=ALU.add),
                                    r=[stB_], w=[stB_])
                            act.add(lambda e: e.activation(out=st2[:, :], in_=st2[:, :], func=AF.Sqrt), r=[stB_], w=[stB_])
                            dve.add(lambda e: e.reciprocal(out=st2[:, :], in_=st2[:, :]), r=[stB_], w=[stB_])
                            dve.add(lambda e: e.tensor_tensor(out=yc[:, :, :], in0=yc[:, :, :], in1=bcast(st2[:, :].unsqueeze(2), [128, 16, 64]),
                                                              op=ALU.mult), r=[ycB, stB_], w=[ycB])
                            for qg in range(2):
                                ps, psb = P.psum()
                                for j in range(4):
                                    q = qg * 4 + j
                                    pe.add(lambda e, ps=ps, j=j, q=q: e.transpose(ps[:, j * 128:(j + 1) * 128], yc[:, 2 * q:2 * q + 2, :], ident[:, :]),
                                           r=[ycB, constB], w=[psb], inc=(j == 3))
                                for j in range(4):
                                    q = qg * 4 + j
                                    act.add(lambda e, ps=ps, j=j, q=q, b8=b8: e.copy(
                                        out=opt[:, 2 * q:2 * q + 2, b8 * 64:(b8 + 1) * 64],
                                        in_=ps[:, j * 128:(j + 1) * 128].rearrange("p (tl hh) -> p hh tl", hh=2)), r=[psb], w=[optB])
                        for pc in range(KC):
                            q, hh = pc // 2, pc % 2
                            k = pc % 2
                            for e_ in range(2):
                                cstd = 2 * q + e_
                                for dstt, key in ((bt, "bon"), (gtt, "g")):
                                    sp.dma(lambda e, dstt=dstt, key=key, cstd=cstd, e_=e_, hh=hh, k=k, tt=tt: e.dma_start(
                                        out=dstt[e_ * 64:(e_ + 1) * 64, k, :], in_=SC[key][cstd, hh * 64:(hh + 1) * 64, tt * 512:(tt + 1) * 512]),
                                        dbuf=bgB[k], r=[SCB[key]], w=[bgB[k]])
                            dve.add(lambda e, pc=pc: e.tensor_scalar(out=ot[:, :], in0=opt[:, pc, :], scalar1=lnp[:, 0, pc:pc + 1],
                                                                    scalar2=lnp[:, 1, pc:pc + 1], op0=ALU.mult, op1=ALU.add), r=[optB, rcB], w=[otB])
                            dve.add(lambda e, k=k: e.tensor_tensor(out=ot[:, :], in0=ot[:, :], in1=bt[:, k, :], op=ALU.add), r=[otB, bgB[k]], w=[otB])
                            dve.add(lambda e, k=k, pc=pc: e.tensor_tensor(out=opt[:, pc, :], in0=ot[:, :], in1=gtt[:, k, :], op=ALU.mult),
                                    r=[otB, bgB[k]], w=[optB])
                        for g in range(4):
                            wv, wb, wi = P.wget(lambda dt, g=g: dt["rw_wo_perm"][:, g * 512:(g + 1) * 512].rearrange("(kc p) n -> p kc n", p=128),
                                                (KC, 512))
                            for nn in range(4):
                                oc = g * 4 + nn
                                ps, psb = P.psum()
                                for pc in range(KC):
                                    pe.add(lambda e, ps=ps, pc=pc, nn=nn, wv=wv: e.matmul(ps[:, :], wv[:, pc, nn * 128:(nn + 1) * 128], opt[:, pc, :],
                                                                                        start=(pc == 0), stop=(pc == KC - 1)),
                                           r=[wb, optB], w=[psb], inc=(pc == KC - 1))
                                dve.add(lambda e, ps=ps, oc=oc, tt=tt: e.scalar_tensor_tensor(
                                    out=xt[:, oc, tt * 512:(tt + 1) * 512], in0=ps[:, :], scalar=normc[:, 2, oc:oc + 1],
                                    in1=xt[:, oc, tt * 512:(tt + 1) * 512], op0=ALU.mult, op1=ALU.add),
                                    r=[psb, normcB, xBt[tt]], w=[xBt[tt]])
                            P.wrel(wi)
                    S.barrier()

        MIXERS[2] = rwkv

        for l in range(depth):
            kind = l % 3
            prep_norm(l)
            with P.sb([128, 4, 512], BF16, "sq") as sq, P.sb([128, 512], F32, "rstd") as rstd, \
                    P.sb([128, 2, 512], F32, "ntmp") as ntmp:
                scrB = [Buf(f"nscr{i}") for i in range(7)]
                if CFG["mixers"][kind]:
                    norm_mod(0, (sq, rstd, ntmp), scrB)
                S.barrier()
            if CFG["mixers"][kind]:
                MIXERS[kind](l)
            with P.sb([128, 4, 512], BF16, "sq") as sq, P.sb([128, 512], F32, "rstd") as rstd, \
                    P.sb([128, 2, 512], F32, "ntmp") as ntmp:
                scrB = [Buf(f"nscr{i}") for i in range(7)]
                norm_mod(1, (sq, rstd, ntmp), scrB)
                S.barrier()
            dump(f"h2_{which}{l}", ht[:, :, 0:T], [128, KC, T], BF16)
            dump(f"normc_{which}{l}", normc[:, :, :], [128, 6, KC])
            mlp(l)
            dump(f"x_{which}{l}", xt[:, :, 0:T], [128, KC, T])

        with P.sb([128, 4, 512], BF16, "sq") as sq, P.sb([128, 512], F32, "rstd") as rstd, \
                P.sb([128, 2, KC, 128], F32, "yfm") as yfm, P.sb([128, 2, D], F32, "ost") as ost:
            scrB = [Buf(f"fscr{i}") for i in range(7)]
            yB = [Buf("yfm0"), Buf("yfm1")]
            oB = [Buf(f"ost{which}0"), Buf(f"ost{which}1")]
            oi = 0
            for tt in range(NT):
                rms_rstd(tt, (sq, rstd), scrB)
                for tb in range(4):
                    sl = oi % 2
                    oi += 1
                    t0 = tt * 512 + tb * 128
                    for c in range(KC):
                        dve.add(lambda e, c=c, t0=t0, tb=tb, sl=sl: e.scalar_tensor_tensor(
                            out=yfm[:, sl, c, :], in0=xt[:, c, t0:t0 + 128], scalar=fngt[:, c:c + 1],
                            in1=rstd[:, tb * 128:(tb + 1) * 128], op0=ALU.mult, op1=ALU.mult),
                            r=[xBt[tt], constB, scrB[4]], w=[yB[sl]])
                    for cg in range(4):
                        ps, psb = P.psum()
                        for j in range(4):
                            c = cg * 4 + j
                            pe.add(lambda e, ps=ps, j=j, c=c, sl=sl: e.transpose(
                                ps[:, j * 128:(j + 1) * 128], yfm[:, sl, c, :], ident[:, :]),
                                r=[yB[sl], constB], w=[psb], inc=(j == 3))
                        if cg % 2 == 0:
                            act.add(lambda e, ps=ps, cg=cg, sl=sl: e.copy(out=ost[:, sl, cg * 512:(cg + 1) * 512], in_=ps[:, :]),
                                    r=[psb], w=[oB[sl]])
                        else:
                            dve.add(lambda e, ps=ps, cg=cg, sl=sl: e.tensor_copy(out=ost[:, sl, cg * 512:(cg + 1) * 512], in_=ps[:, :]),
                                    r=[psb], w=[oB[sl]])
                    sp.dma(lambda e, sl=sl, t0=t0: e.dma_start(out=yout[t0:t0 + 128, :], in_=ost[:, sl, :]),
                           dbuf=oB[sl], r=[oB[sl]])
            S.barrier()

    def locals_ns():
        return None

    MIXERS = {}

    for which in CFG["passes"]:
        with ExitStack() as pes:
            run_pass(which, pes)

    if not CFG["mixers"][2]:
        with P.sb([128, 4096], F32, "zst") as zst:
            zB = Buf("zst")
            dve.add(lambda e: e.memset(zst[:, :], 0.0), w=[zB])
            sp.dma(lambda e: e.dma_start(out=nst.rearrange("a b h v k -> (a b h) (v k)"), in_=zst[:, :]), dbuf=zB, r=[zB])
            S.barrier()

    S.barrier()
    for e in S.engs:
        pass
    es_ops = {e.name: e.ops for e in S.engs}
    return nc, es, P, es_ops


def finish_program(nc, es, ops):
    with nc.Block() as block:
        @block.tensor
        def _(e):
            replay(e, ops["pe"])

        @block.scalar
        def _(e):
            replay(e, ops["act"])

        @block.vector
        def _(e):
            replay(e, ops["dve"])

        @block.gpsimd
        def _(e):
            replay(e, ops["pool"])

        @block.sync
        def _(e):
            replay(e, ops["sp"])
    es.__exit__(None, None, None)
    return nc


_CACHE = {}


def get_program():
    key = (CFG["depth"], CFG["mixers"], CFG["passes"])
    if key not in _CACHE:
        nc0, es0, P0, _ = build_program(True, None)
        es0.__exit__(None, None, None)
        reqs = P0.reqs
        nc, es, P, ops = build_program(False, reqs)
        finish_program(nc, es, ops)
        _CACHE[key] = nc
    return _CACHE[key]


def _consts():
    f32 = np.float32
    t = np.arange(LS)
    row = (t // 64).astype(np.float64)
    col = (t % 64).astype(np.float64)
    inv = 10000.0 ** (-(np.arange(0, 64, 2, dtype=np.float64)) / 64.0)
    p = np.arange(128)
    quarter, i = p // 32, p % 32
    pos = np.where(quarter[:, None] < 2, row[None, :], col[None, :])
    ang = (pos.astype(f32) * inv.astype(f32)[i][:, None]).astype(f32)
    ropeC = np.cos(ang).astype(f32)
    ropeS = np.sin(ang).astype(f32)
    perm = np.zeros((128, 128), f32)
    for m in range(128):
        if (m // 32) % 2 == 0:
            perm[m + 32, m] = -1.0
        else:
            perm[m - 32, m] = 1.0
    kk = np.arange(128)[:, None]
    qq = np.arange(128)[None, :]
    mL = (kk >= qq).astype(f32)
    mU = (kk <= qq).astype(f32)
    masks = np.stack([np.tile(mL, (1, 4)), np.tile(mU, (1, 4))], axis=1)
    out = {"ropeC": ropeC, "ropeS": ropeS, "permS": perm, "masks": np.ascontiguousarray(masks)}
    pp = np.arange(128)
    out["bones"] = (pp[:, None] // 64 == pp[None, :] // 64).astype(f32)
    jj = (pp % 64)[:, None]
    ii = np.arange(64)[None, :]
    mf = np.concatenate([(ii > jj), (ii >= jj)], axis=1).astype(f32)
    mb = np.concatenate([(ii < jj), (ii <= jj)], axis=1).astype(f32)
    out["rwmask"] = np.ascontiguousarray(np.stack([mf, mb], axis=1))
    pr = pp[:, None]
    pc_ = pp[None, :]
    same_head = (pr // 64 == pc_ // 64)
    m16 = same_head & ((pr % 64) // 16 == (pc_ % 64) // 16)
    m32 = same_head & ((pr % 64) // 32 == (pc_ % 64) // 32) & ~m16
    m64 = same_head & ((pr % 64) // 32 != (pc_ % 64) // 32)
    out["hmask"] = np.ascontiguousarray(np.stack([m16, m32, m64], axis=1).astype(f32))
    oh = (pp[:, None] % 64 == np.arange(64)[None, :]).astype(f32)
    hmk = (pp[:, None] // 64 == pp[None, :] // 64).astype(f32)
    zz = np.zeros((128, 254), f32)
    zz[:, 126] = (pp < 64)
    zz[:, 127] = (pp >= 64)
    out["ohm"] = np.ascontiguousarray(np.concatenate([oh, hmk, zz], axis=1))
    deltas = np.linspace(math.log(1e-2) / 1.5, math.log(1e-2) / 0.3, D, dtype=f32)
    out["absd"] = np.ascontiguousarray(np.broadcast_to(np.abs(deltas)[None, :], (128, D)).astype(f32))
    for LL in (LS, LP):
        tt = np.linspace(0.0, 1.0, LL, dtype=f32)
        w = (2 * np.pi * np.arange(LL, dtype=f32) / LL).astype(f32)
        fr = np.linspace(1e-4, 15, 16, dtype=f32)
        zemb = np.concatenate([tt[:, None], np.cos(fr[None, :] * w[:, None]), -np.sin(fr[None, :] * w[:, None])], axis=-1)
        out[f"zemb{LL}"] = np.ascontiguousarray(zemb.T.astype(f32))
        out[f"tneg{LL}"] = np.ascontiguousarray((-tt).reshape(LL // 128, 128).T.astype(f32))
        m0 = np.ones(LL, f32)
        m0[0] = 0.0
        m0 = m0.reshape(LL // 128, 128).T
        out[f"m0_{LL}"] = np.ascontiguousarray(np.stack([m0, -m0], axis=-1).astype(f32))
        ph = np.pi * (2 * np.arange(LL, dtype=np.float64) + 1) / (4 * LL)
        out[f"phi{LL}"] = np.ascontiguousarray(np.stack([np.cos(ph), np.sin(ph)], axis=-1).reshape(LL // 128, 128, 2).transpose(1, 0, 2).astype(f32))
        ff = np.arange(LL, dtype=np.float64)
        th = np.pi * np.outer(2 * ff + 1, 2 * ff + 1) / (4 * LL)
        out[f"dftC{LL}"] = np.cos(th).astype(f32)
        out[f"dftS{LL}"] = np.sin(th).astype(f32)
    return out


def fm(v):
    v = np.asarray(v, dtype=np.float32)
    lead = v.shape[:-1]
    k = v.shape[-1] // 128
    v = v.reshape(lead + (k, 128))
    v = np.moveaxis(v, -1, 0)
    return np.ascontiguousarray(v)


def kernel(**inp):
    nc = get_program()
    f32 = np.float32
    shared = {
        "ada_w": np.ascontiguousarray(inp["ada_w"], dtype=f32),
        "ada_b": fm(inp["ada_b"]),
        "n1g": fm(inp["norm1_g"]),
        "n2g": fm(inp["norm2_g"]),
        "fng": fm(inp["final_norm_g"]),
        "mlp_w1": np.ascontiguousarray(inp["mlp_w1"], dtype=f32),
        "mlp_w2": np.ascontiguousarray(inp["mlp_w2"], dtype=f32),
        "ident": np.eye(128, dtype=f32),
        "attn_wqkv": np.ascontiguousarray(inp["attn_wqkv"], dtype=f32),
        "attn_wo": np.ascontiguousarray(inp["attn_wo"], dtype=f32),
        "sinkb": np.ascontiguousarray(np.broadcast_to(np.asarray(inp["attn_sink"], dtype=f32)[None], (128, 2, 16))),
    }
    shared.update(_consts())
    shared.update({
        "hy_w_in": np.ascontiguousarray(inp["hy_w_in"], dtype=f32),
        "hy_w_out": np.ascontiguousarray(inp["hy_w_out"], dtype=f32),
        "hy_cw": fm(inp["hy_conv_w"][0]),
        "hy_cb": fm(inp["hy_conv_b"][0]),
        "hy_fb": fm(inp["hy_bias"][0]),
        "hy_f_w1": np.ascontiguousarray(inp["hy_f_w1"][0], dtype=f32),
        "hy_f_w2": np.ascontiguousarray(inp["hy_f_w2"][0], dtype=f32),
        "hy_f_w3": np.ascontiguousarray(inp["hy_f_w3"][0], dtype=f32),
        "hy_bf": np.ascontiguousarray(np.stack([inp["hy_f_b1"][0], inp["hy_f_b2"][0], inp["hy_f_b3"][0], inp["hy_f_freq"][0]], axis=1), dtype=f32),
        "hy_f_wout": np.ascontiguousarray(inp["hy_f_wout"][0].reshape(64, 2, D), dtype=f32),
    })
    perm = np.zeros(D, dtype=np.int64)
    for pc in range(KC):
        q, hh = pc // 2, pc % 2
        for e_ in range(2):
            for v_ in range(64):
                perm[pc * 128 + e_ * 64 + v_] = ((2 * q + e_) * 2 + hh) * 64 + v_
    shared.update({
        "rw_wr": np.ascontiguousarray(inp["rw_wr"][0], dtype=f32), "rw_wk": np.ascontiguousarray(inp["rw_wk"][0], dtype=f32),
        "rw_wv": np.ascontiguousarray(inp["rw_wv"][0], dtype=f32),
        "rw_wo_perm": np.ascontiguousarray(inp["rw_wo"][0][perm, :], dtype=f32),
        "rw_w1": np.ascontiguousarray(inp["rw_w1"][0], dtype=f32), "rw_a1": np.ascontiguousarray(inp["rw_a1"][0], dtype=f32),
        "rw_g1": np.ascontiguousarray(inp["rw_g1"][0], dtype=f32),
        "rw_w2": np.ascontiguousarray(inp["rw_w2"][0], dtype=f32), "rw_a2": np.ascontiguousarray(inp["rw_a2"][0], dtype=f32),
        "rw_g2": np.ascontiguousarray(inp["rw_g2"][0], dtype=f32),
        "rw_mu": fm(inp["rw_mu"][0]), "rw_w0": fm(inp["rw_w0"][0]), "rw_a0": fm(inp["rw_a0"][0]),
        "rw_vec": fm(np.stack([inp["rw_k_k"][0], inp["rw_k_a"][0], inp["rw_r_k"][0].reshape(D)])),
        "rw_lnp": fm(np.stack([inp["rw_ln_w"][0][perm], inp["rw_ln_b"][0][perm]])),
        "rw_ln2": fm(np.stack([inp["rw_ln_w"][0], inp["rw_ln_b"][0]])),
        "rw_wo": np.ascontiguousarray(inp["rw_wo"][0], dtype=f32),
    })
    in_maps = []
    for i in range(NCORES):
        m = dict(shared)
        m["x_p"] = np.ascontiguousarray(inp["x_prompt"][2 * i:2 * i + 2].reshape(2 * LP, D), dtype=f32)
        m["x_s"] = np.ascontiguousarray(inp["x_sample"][i], dtype=f32)
        cv = np.stack([inp["c_ctx"], inp["c"][i]], axis=-1).astype(f32)
        m["cvec"] = np.ascontiguousarray(cv.reshape(KC, 128, 2).transpose(1, 0, 2))
        m["st"] = np.ascontiguousarray(inp["state_rwkv"][i, 0], dtype=f32)
        m["ck"] = np.ascontiguousarray(inp["cache_attn_k"][i].reshape(2, PAST, 512), dtype=f32)
        m["cvv"] = np.ascontiguousarray(inp["cache_attn_v"][i].reshape(2, PAST, 512), dtype=f32)
        in_maps.append(m)
    ncr = CFG["ncores"]
    res = run_bass_kernel_spmd(nc, in_maps[:ncr], core_ids=list(range(ncr)))
    R = list(res.results)
    while len(R) < NCORES:
        R.append(R[0])
    global LAST
    LAST = R
    y_prompt = np.stack([R[i]["y_p"].reshape(2, LP, D) for i in range(NCORES)]).reshape(16, LP, D)
    y_sample = np.stack([R[i]["y_s"] for i in range(NCORES)])
    nk = np.concatenate([R[i]["nk"].reshape(2, 2, LP, 4, 128) for i in range(NCORES)], axis=0)
    nv = np.concatenate([R[i]["nv"].reshape(2, 2, LP, 4, 128) for i in range(NCORES)], axis=0)
    nst = np.concatenate([R[i]["nst"].reshape(2, 1, 2, 32, 64, 64) for i in range(NCORES)], axis=0)
    return (y_prompt.astype(f32), y_sample.astype(f32), nk.astype(f32), nv.astype(f32), nst.astype(f32))
```
